# Optimizing a Trainium2 kernel written in Bass

```python
import math, functools
import jax, jax.numpy as jnp
from jax import lax
import numpy as np

D_MODEL = 4096
BATCH = 1
SEQ = 8192
DEPTH = 1
DEC_BATCH = 128
DEC_SEQ = 4
PAST_LEN = 2048
PAGE_SIZE = 128

N_HEADS = 16
N_KV_HEADS = 4
HEAD_DIM = 128
ATTN_WIDTH = N_HEADS * HEAD_DIM
KV_WIDTH = N_KV_HEADS * HEAD_DIM
N_IDX_HEADS = 32
IDX_DIM = 128
IDX_W_SCALE = (N_IDX_HEADS * IDX_DIM) ** -0.5
TOP_K_MAX = 256
Q_BLOCK = 128
N_BUCKETS = 32
MAX_DISTANCE = 128
GDN_HEADS = 16
GDN_DK = 128
GDN_DV = 128
GDN_KW = GDN_HEADS * GDN_DK
GDN_VW = GDN_HEADS * GDN_DV
CONV_WIDTH = 4
CONV_CH = 2 * GDN_KW + GDN_VW
GDN_CHUNK = 64
D_FF = -(-8 * D_MODEL // (3 * 256)) * 256
PLE_DIM = 256
EPS = 1e-6
SPLIT_SIZES = (ATTN_WIDTH, KV_WIDTH, KV_WIDTH, N_IDX_HEADS * IDX_DIM, IDX_DIM, N_IDX_HEADS,
               CONV_CH, GDN_HEADS, GDN_HEADS, GDN_VW, D_MODEL, D_MODEL)
IN_COLS = sum(SPLIT_SIZES)

kernel_name = "hybrid_dsa_gdn_decode_step"


def rmsnorm(x, w):
    xf = x.astype(jnp.float32)
    y = xf * lax.rsqrt(jnp.mean(xf * xf, axis=-1, keepdims=True) + EPS)
    return (y * w.astype(jnp.float32)).astype(x.dtype)


def l2norm(x):
    return x * lax.rsqrt(jnp.sum(x * x, axis=-1, keepdims=True) + 1e-6)


def t5_bucket(dist):
    max_exact = N_BUCKETS // 2
    n = jnp.maximum(dist, 0)
    large = max_exact + (jnp.log(jnp.maximum(n, 1).astype(jnp.float32) / max_exact)
                         / math.log(MAX_DISTANCE / max_exact) * (N_BUCKETS - max_exact)).astype(jnp.int32)
    large = jnp.minimum(large, N_BUCKETS - 1)
    return jnp.where(n < max_exact, n, large)


def index_select(qi, wi, ki, q_pos, n_sel):
    s = jax.nn.relu(jnp.einsum('bthd,bld->bthl', qi.astype(jnp.float32), ki.astype(jnp.float32)))
    score = jnp.einsum('bth,bthl->btl', wi.astype(jnp.float32), s)
    key_pos = jnp.arange(ki.shape[1])
    score = jnp.where(key_pos[None, None, :] <= q_pos[None, :, None], score, -jnp.inf)
    return lax.top_k(score, n_sel)[1]


def sparse_attend(q, k_sel, v_sel, idx, q_pos, rel_bias):
    B, T = q.shape[:2]
    K = idx.shape[-1]
    G = N_HEADS // N_KV_HEADS
    qg = q.reshape(B, T, N_KV_HEADS, G, HEAD_DIM)
    logits = jnp.einsum('btkgd,btskd->btkgs', qg, k_sel).astype(jnp.float32) * HEAD_DIM ** -0.5
    dist = q_pos[None, :, None] - idx
    bias = rel_bias[t5_bucket(dist)].reshape(B, T, K, N_KV_HEADS, G).transpose(0, 1, 3, 4, 2)
    logits = jnp.where((dist >= 0)[:, :, None, None, :], logits + bias.astype(jnp.float32), -jnp.inf)
    probs = jax.nn.softmax(logits, axis=-1)
    out = jnp.einsum('btkgs,btskd->btkgd', probs.astype(v_sel.dtype), v_sel)
    return out.reshape(B, T, ATTN_WIDTH)


def gather_rows(rows, ii):
    return jax.vmap(lambda r, i: r[i])(rows, ii)


def prompt_attention(q, k, v, qi, ki, wi, rel_bias):
    B, S = q.shape[:2]
    n_sel = min(TOP_K_MAX, S // 4)

    def block(i):
        start = i * Q_BLOCK
        q_pos = start + jnp.arange(Q_BLOCK)
        sl = lambda a: lax.dynamic_slice_in_dim(a, start, Q_BLOCK, axis=1)
        idx = index_select(sl(qi), sl(wi), ki, q_pos, n_sel)
        return sparse_attend(sl(q), gather_rows(k, idx), gather_rows(v, idx), idx, q_pos, rel_bias)

    out = lax.map(block, jnp.arange(S // Q_BLOCK))
    return out.transpose(1, 0, 2, 3).reshape(B, S, ATTN_WIDTH)


def sample_attention(q, k, v, qi, ki, wi, cache_k, cache_v, cache_idx_k, page_table, rel_bias):
    DB, T = q.shape[:2]
    L = PAST_LEN + T
    n_sel = min(TOP_K_MAX, L // 4)
    q_pos = PAST_LEN + jnp.arange(T)
    ki_past = cache_idx_k[page_table].reshape(DB, PAST_LEN, IDX_DIM)
    ki_all = jnp.concatenate([ki_past.astype(ki.dtype), ki], axis=1)
    idx = index_select(qi, wi, ki_all, q_pos, n_sel)
    in_past = (idx < PAST_LEN)[..., None, None]
    ip = jnp.minimum(idx, PAST_LEN - 1)
    phys = gather_rows(page_table, ip // PAGE_SIZE) * PAGE_SIZE + ip % PAGE_SIZE
    inew = jnp.clip(idx - PAST_LEN, 0, T - 1)
    k_flat = cache_k.reshape(-1, N_KV_HEADS, HEAD_DIM)
    v_flat = cache_v.reshape(-1, N_KV_HEADS, HEAD_DIM)
    k_sel = jnp.where(in_past, k_flat[phys].astype(k.dtype), gather_rows(k, inew))
    v_sel = jnp.where(in_past, v_flat[phys].astype(v.dtype), gather_rows(v, inew))
    return sparse_attend(q, k_sel, v_sel, idx, q_pos, rel_bias)


def causal_conv(x, buf, w):
    T = x.shape[1]
    xp = jnp.concatenate([buf.astype(x.dtype), x], axis=1)
    y = w[0] * xp[:, 0:T]
    for j in range(1, CONV_WIDTH):
        y = y + w[j] * xp[:, j:j + T]
    return y, xp[:, -(CONV_WIDTH - 1):]


def gdn_step(S, inp):
    u, w, qd, a_in, kt, gl = inp
    v_new = u - jnp.einsum('bhcd,bhde->bhce', w, S)
    o = jnp.einsum('bhcd,bhde->bhce', qd, S) + jnp.einsum('bhcs,bhse->bhce', a_in, v_new)
    S = S * gl[..., None, None] + jnp.einsum('bhcd,bhce->bhde', kt, v_new)
    return S, o


def gated_delta_chunked(q, k, v, g, beta, state):
    B, T, H, _ = q.shape
    f32 = jnp.float32
    C = min(GDN_CHUNK, T)
    n = -(-T // C)
    pad = n * C - T
    q = l2norm(q.astype(f32)) * GDN_DK ** -0.5
    k = l2norm(k.astype(f32))

    def chunks(a):
        a = jnp.pad(a.astype(f32), [(0, 0), (0, pad)] + [(0, 0)] * (a.ndim - 2))
        a = a.reshape((B, n, C) + a.shape[2:])
        return jnp.moveaxis(a, 3, 2).swapaxes(0, 1)

    q, k, v, g, beta = chunks(q), chunks(k), chunks(v), chunks(g), chunks(beta)
    gc = jnp.cumsum(g, axis=-1)
    causal = jnp.tril(jnp.ones((C, C), bool))
    strict = jnp.tril(jnp.ones((C, C), bool), -1)
    decay = jnp.exp(jnp.where(causal, gc[..., :, None] - gc[..., None, :], -jnp.inf))
    kb = k * beta[..., None]
    a_ut = jnp.where(strict, jnp.einsum('nbhcd,nbhsd->nbhcs', kb, k) * decay, 0.0)
    eye = jnp.eye(C, dtype=f32)
    t_inv = lax.linalg.triangular_solve(eye + a_ut, jnp.broadcast_to(eye, a_ut.shape),
                                        left_side=True, lower=True)
    u = t_inv @ (v * beta[..., None])
    w = t_inv @ (kb * jnp.exp(gc)[..., None])
    a_in = jnp.where(causal, jnp.einsum('nbhcd,nbhsd->nbhcs', q, k) * decay, 0.0)
    qd = q * jnp.exp(gc)[..., None]
    kt = k * jnp.exp(gc[..., -1:] - gc)[..., None]
    gl = jnp.exp(gc[..., -1])
    S, o = lax.scan(gdn_step, state.astype(f32), (u, w, qd, a_in, kt, gl))
    o = jnp.moveaxis(o.swapaxes(0, 1), 2, 3).reshape(B, n * C, H, GDN_DV)[:, :T]
    return o, S


def decoder_layer(x, p, attn_fn, conv_buf, ssm_state, norm_mix, w_in, conv_w, a_log, dt_bias, gdn_norm,
                  w_attn_up, w_gdn_up, w_out, norm_ffn, w_gate_up, w_down, norm_ple, w_ple_gate, w_ple):
    B, T, _ = x.shape
    f32 = jnp.float32
    h = rmsnorm(x, norm_mix)
    offsets = [int(o) for o in np.cumsum(SPLIT_SIZES)[:-1]]
    q, k, v, qi, ki, wi, qkv, ga, gb, gz, gate_a, gate_b = jnp.split(h @ w_in, offsets, axis=-1)
    q = q.reshape(B, T, N_HEADS, HEAD_DIM)
    k = k.reshape(B, T, N_KV_HEADS, HEAD_DIM)
    v = v.reshape(B, T, N_KV_HEADS, HEAD_DIM)
    qi = qi.reshape(B, T, N_IDX_HEADS, IDX_DIM)
    attn = attn_fn(q, k, v, qi, ki, wi * IDX_W_SCALE)
    qkv, new_conv = causal_conv(qkv, conv_buf, conv_w)
    gq, gk, gv = jnp.split(jax.nn.silu(qkv), [GDN_KW, 2 * GDN_KW], axis=-1)
    g = -jnp.exp(a_log.astype(f32)) * jax.nn.softplus(ga.astype(f32) + dt_bias.astype(f32))
    beta = jax.nn.sigmoid(gb.astype(f32))
    o, new_ssm = gated_delta_chunked(gq.reshape(B, T, GDN_HEADS, GDN_DK), gk.reshape(B, T, GDN_HEADS, GDN_DK),
                                     gv.reshape(B, T, GDN_HEADS, GDN_DV), g, beta, ssm_state)
    o = rmsnorm(o.astype(x.dtype), gdn_norm) * jax.nn.silu(gz.reshape(B, T, GDN_HEADS, GDN_DV))
    merged = (jax.nn.sigmoid(gate_a) * (attn @ w_attn_up)
              + jax.nn.sigmoid(gate_b) * (o.reshape(B, T, GDN_VW) @ w_gdn_up))
    x = x + merged @ w_out
    gt, up = jnp.split(rmsnorm(x, norm_ffn) @ w_gate_up, 2, axis=-1)
    x = x + (jax.nn.silu(gt) * up) @ w_down
    x = x + jax.nn.sigmoid(rmsnorm(x, norm_ple) @ w_ple_gate) * (p.astype(x.dtype) @ w_ple)
    return x, k, v, ki, new_conv, new_ssm.astype(ssm_state.dtype)


def setup_inputs(seed: int = 0) -> dict:
    key = jax.random.key(seed)
    ks = jax.random.split(key, 32)
    f32 = jnp.float32
    n_pages = PAST_LEN // PAGE_SIZE
    n_used = DEC_BATCH * n_pages
    n_pool = (5 * n_used + 3) // 4
    nrm = lambda k, shape, scale: jax.random.normal(k, shape, f32) * scale
    gain = lambda k, shape: 1.0 + 0.02 * jax.random.normal(k, shape, f32)
    page_table = jax.random.permutation(ks[7], n_pool)[:n_used].reshape(DEC_BATCH, n_pages).astype(jnp.int32)
    dt = jnp.exp(jax.random.uniform(ks[12], (DEPTH, GDN_HEADS), f32, math.log(1e-3), math.log(1e-1)))
    return {
        "x_prompt": nrm(ks[0], (BATCH, SEQ, D_MODEL), 1.0),
        "x_sample": nrm(ks[1], (DEC_BATCH, DEC_SEQ, D_MODEL), 1.0),
        "cache_k": nrm(ks[2], (DEPTH, n_pool, PAGE_SIZE, N_KV_HEADS, HEAD_DIM), 1.0),
        "cache_v": nrm(ks[3], (DEPTH, n_pool, PAGE_SIZE, N_KV_HEADS, HEAD_DIM), 1.0),
        "cache_idx_k": nrm(ks[4], (DEPTH, n_pool, PAGE_SIZE, IDX_DIM), 1.0),
        "state_conv": nrm(ks[5], (DEPTH, DEC_BATCH, CONV_WIDTH - 1, CONV_CH), 1.0),
        "state_ssm": nrm(ks[6], (DEPTH, DEC_BATCH, GDN_HEADS, GDN_DK, GDN_DV), 0.5),
        "page_table": page_table,
        "p_prompt": nrm(ks[8], (DEPTH, BATCH, SEQ, PLE_DIM), 1.0),
        "p_sample": nrm(ks[9], (DEPTH, DEC_BATCH, DEC_SEQ, PLE_DIM), 1.0),
        "rel_bias": nrm(ks[10], (N_BUCKETS, N_HEADS), 0.5),
        "norm_mix": gain(ks[11], (DEPTH, D_MODEL)),
        "w_in": nrm(ks[13], (DEPTH, D_MODEL, IN_COLS), D_MODEL ** -0.5),
        "conv_w": nrm(ks[14], (DEPTH, CONV_WIDTH, CONV_CH), CONV_WIDTH ** -0.5),
        "a_log": jnp.log(jax.random.uniform(ks[15], (DEPTH, GDN_HEADS), f32, 1.0, 16.0)),
        "dt_bias": dt + jnp.log(-jnp.expm1(-dt)),
        "gdn_norm": gain(ks[16], (DEPTH, GDN_DV)),
        "w_attn_up": nrm(ks[17], (DEPTH, ATTN_WIDTH, D_MODEL), ATTN_WIDTH ** -0.5),
        "w_gdn_up": nrm(ks[18], (DEPTH, GDN_VW, D_MODEL), GDN_VW ** -0.5),
        "w_out": nrm(ks[19], (DEPTH, D_MODEL, D_MODEL), D_MODEL ** -0.5),
        "norm_ffn": gain(ks[20], (DEPTH, D_MODEL)),
        "w_gate_up": nrm(ks[21], (DEPTH, D_MODEL, 2 * D_FF), D_MODEL ** -0.5),
        "w_down": nrm(ks[22], (DEPTH, D_FF, D_MODEL), D_FF ** -0.5),
        "norm_ple": gain(ks[23], (DEPTH, D_MODEL)),
        "w_ple_gate": nrm(ks[24], (DEPTH, D_MODEL, D_MODEL), D_MODEL ** -0.5),
        "w_ple": nrm(ks[25], (DEPTH, PLE_DIM, D_MODEL), PLE_DIM ** -0.5),
        "norm_final": gain(ks[26], (D_MODEL,)),
    }


def reference(x_prompt, x_sample, cache_k, cache_v, cache_idx_k, state_conv, state_ssm, page_table,
              p_prompt, p_sample, rel_bias, norm_mix, w_in, conv_w, a_log, dt_bias, gdn_norm,
              w_attn_up, w_gdn_up, w_out, norm_ffn, w_gate_up, w_down, norm_ple, w_ple_gate, w_ple,
              norm_final):
    xp, xs = x_prompt, x_sample
    B = x_prompt.shape[0]
    conv0 = jnp.zeros((B, CONV_WIDTH - 1, CONV_CH), x_prompt.dtype)
    ssm0 = jnp.zeros((B, GDN_HEADS, GDN_DK, GDN_DV), state_ssm.dtype)
    kp_l, vp_l, kip_l, cp_l, sp_l = [], [], [], [], []
    ks_l, vs_l, kis_l, cs_l, ss_l = [], [], [], [], []
    prompt_attn = functools.partial(prompt_attention, rel_bias=rel_bias)
    for i in range(DEPTH):
        lw = (norm_mix[i], w_in[i], conv_w[i], a_log[i], dt_bias[i], gdn_norm[i], w_attn_up[i], w_gdn_up[i],
              w_out[i], norm_ffn[i], w_gate_up[i], w_down[i], norm_ple[i], w_ple_gate[i], w_ple[i])
        sample_attn = functools.partial(sample_attention, cache_k=cache_k[i], cache_v=cache_v[i],
                                        cache_idx_k=cache_idx_k[i], page_table=page_table, rel_bias=rel_bias)
        xp, kp, vp, kip, cp, sp = decoder_layer(xp, p_prompt[i], prompt_attn, conv0, ssm0, *lw)
        xs, ksm, vsm, kism, csm, ssm = decoder_layer(xs, p_sample[i], sample_attn, state_conv[i], state_ssm[i], *lw)
        kp_l.append(kp); vp_l.append(vp); kip_l.append(kip); cp_l.append(cp); sp_l.append(sp)
        ks_l.append(ksm); vs_l.append(vsm); kis_l.append(kism); cs_l.append(csm); ss_l.append(ssm)
    y_prompt = rmsnorm(xp, norm_final)
    y_sample = rmsnorm(xs, norm_final)
    return (y_prompt, y_sample,
            jnp.stack(kp_l), jnp.stack(vp_l), jnp.stack(kip_l), jnp.stack(cp_l), jnp.stack(sp_l),
            jnp.stack(ks_l), jnp.stack(vs_l), jnp.stack(kis_l), jnp.stack(cs_l), jnp.stack(ss_l))
```

```python
from contextlib import ExitStack
import numpy as np
import concourse.bass as bass
import concourse.mybir as mybir
from concourse.bass_utils import run_bass_kernel_spmd

F32 = mybir.dt.float32
BF16 = mybir.dt.bfloat16
I32 = mybir.dt.int32
AF = mybir.ActivationFunctionType
ALU = mybir.AluOpType
AX = mybir.AxisListType

EPOCH = 30000
NCORES = 8
D = 4096
KC = 32
NP = 8192
NS = 512
NALL = NP + NS
NB = 128
STRIDE = 8
NBLK = 8
NCORES_L = 8
PAST = 2048
TOPK_P = 256
TOPK_S = 256
NPOOL = 2560
NOWN = 1088
SUP = 1088
EPS = 1e-6
O_Q, O_K, O_V, O_QI, O_KI, O_WI, O_QKV, O_GA, O_GB, O_GZ, O_GTA, O_GTB = (
    0, 2048, 2560, 3072, 7168, 7296, 7328, 13472, 13488, 13504, 15552, 19648)
DFF = 11008
STOP = None
NOSTACK = False
NFT = 16
SKIP_H = False
DBG = 0


class Trk:
    __slots__ = ("w", "r", "x")

    def __init__(self, x=False):
        self.w = {}
        self.r = {}
        self.x = x


class Eng:
    def __init__(self, fw, name, e):
        self.fw, self.name, self.e = fw, name, e
        self.seq = 0
        self.sems = []
        self.waited = {}

    def sem_for(self, seq):
        ep = (seq - 1) // EPOCH
        while len(self.sems) <= ep:
            self.sems.append(self.fw.nc.alloc_semaphore(f"c_{self.name}_{len(self.sems)}"))
        return self.sems[ep], (seq - 1) % EPOCH + 1


class FW:
    def __init__(self, nc, nslots=6):
        self.nc = nc
        self.engs = {n: Eng(self, n, e) for n, e in (("pe", nc.tensor), ("act", nc.scalar), ("dve", nc.vector),
                                                      ("pool", nc.gpsimd), ("sp", nc.sync))}
        self.slots = {}
        self.slot_epoch = 0
        for q in ("sp", "act", "pool"):
            self.slots[q] = [dict(sem=nc.alloc_semaphore(f"d_{q}_{i}"), k=0, name=f"dma_{q}_{i}", base=0) for i in range(nslots)]
        self.slot_rr = {q: 0 for q in self.slots}
        self.slot_by_name = {s["name"]: s for q in self.slots for s in self.slots[q]}
        self.n_wait = 0
        self.n_ins = 0

    def _wait(self, eng, res, seq):
        if eng.waited.get(res, 0) >= seq:
            return
        eng.waited[res] = seq
        self.n_wait += 1
        if res in self.engs:
            sem, val = self.engs[res].sem_for(seq)
            eng.e.wait_ge(sem, val)
        else:
            s = self.slot_by_name[res]
            eng.e.wait_ge(s["sem"], seq * 16)

    def _deps(self, reads, writes):
        deps = {}
        for t in reads:
            for r, s in t.w.items():
                if deps.get(r, 0) < s:
                    deps[r] = s
        for t in writes:
            for d in (t.w, t.r):
                for r, s in d.items():
                    if deps.get(r, 0) < s:
                        deps[r] = s
        return deps

    def op(self, en, fn, reads=(), writes=(), pe_acc=False):
        eng = self.engs[en]
        xr = [t for t in reads if t.x]
        if xr:
            writes = list(writes) + [t for t in xr if t not in writes]
            reads = [t for t in reads if not t.x]
        deps = self._deps(reads, writes)
        for r, s in deps.items():
            if pe_acc and r == "pe":
                continue
            self._wait(eng, r, s)
        ins = fn(eng.e)
        eng.seq += 1
        sem, _ = eng.sem_for(eng.seq)
        ins.then_inc(sem, 1)
        self.n_ins += 1
        for t in reads:
            t.r[en] = eng.seq
        for t in writes:
            t.w = {en: eng.seq}
            t.r = {}
        return ins

    def dma(self, q, out, in_, reads=(), writes=(), merge=(), **kw):
        eng = self.engs[q]
        deps = self._deps(reads, writes)
        for r, s in deps.items():
            self._wait(eng, r, s)
        i = self.slot_rr[q]
        self.slot_rr[q] = (i + 1) % len(self.slots[q])
        sl = self.slots[q][i]
        if sl["k"] > 0:
            self._wait(eng, sl["name"], sl["k"])
        assert (sl["k"] + 1) * 16 < 65000, "dma slot semaphore overflow"
        ins = eng.e.dma_start(out=out, in_=in_, **kw)
        sl["k"] += 1
        ins.then_inc(sl["sem"], 16)
        self.n_ins += 1
        for t in reads:
            t.r[sl["name"]] = sl["k"]
        for t in writes:
            t.w = {sl["name"]: sl["k"]}
            t.r = {}
        for t in merge:
            t.w[sl["name"]] = sl["k"]
        return ins

    def barrier(self):
        for en, eng in self.engs.items():
            for q in self.slots:
                for sl in self.slots[q]:
                    if sl["k"] > 0:
                        self._wait(eng, sl["name"], sl["k"])
            for en2, e2 in self.engs.items():
                if e2.seq > 0 and en2 != en:
                    self._wait(eng, en2, e2.seq)

    def finish(self):
        eng = self.engs["sp"]
        for q in self.slots:
            for sl in self.slots[q]:
                if sl["k"] > 0:
                    self._wait(eng, sl["name"], sl["k"])
        for en, e in self.engs.items():
            if e.seq > 0 and en != "sp":
                self._wait(eng, en, e.seq)


class B:
    def __init__(self, ap):
        self.ap = ap
        self.t = Trk()


class Ctx:
    def __init__(self):
        self.nc = bass.Bass("TRN2", target_bir_lowering=False)
        self.fw = FW(self.nc)
        self.uid = 0
        self.dq = 0
        self.stack = None

    def begin(self):
        if NOSTACK:
            return
        self.stack = ExitStack()

    def end(self):
        self.fw.barrier()
        if self.stack is not None:
            self.stack.close()
        self.stack = None

    def sb(self, shape, dt, name=None):
        self.uid += 1
        nm = f"{name or 'sb'}_{self.uid}"
        if self.stack is None:
            return B(self.nc.alloc_sbuf_tensor(nm, list(shape), dt).ap())
        return B(self.stack.enter_context(self.nc.sbuf_tensor(nm, list(shape), dt)).ap())

    def ps(self, shape, dt=F32, name=None):
        self.uid += 1
        nm = f"{name or 'ps'}_{self.uid}"
        if self.stack is None:
            b = B(self.nc.alloc_psum_tensor(nm, list(shape), dt).ap())
        else:
            b = B(self.stack.enter_context(self.nc.psum_tensor(nm, list(shape), dt)).ap())
        b.t.x = True
        return b

    def dram(self, name, shape, dt, kind="Internal", nparts=1):
        h = self.nc.dram_tensor(name, list(shape), dt, kind=kind)
        b = B(h.ap())
        b.parts = [Trk() for _ in range(nparts)]
        return b

    def q(self):
        return "sp"


def rmsnorm_T(cx, src, src_trks, dst, dst_trk_fn, nw, ntok, tt=128, out_dt=BF16, pfx="rn"):
    fw = cx.fw
    xs = [cx.sb([128, KC, tt], F32, pfx + "x") for _ in range(2)]
    sq = cx.sb([128, KC, tt], F32, pfx + "sq")
    ob = [cx.sb([128, KC, tt], out_dt, pfx + "o") for _ in range(2)]
    part = cx.sb([128, tt], F32, pfx + "part")
    rstd = cx.sb([128, tt], F32, pfx + "rstd")
    phi = cx.sb([128, tt], BF16, pfx + "phi")
    plo = cx.sb([128, tt], BF16, pfx + "plo")
    tot = cx.ps([128, tt], F32, pfx + "tot")
    nt = (ntok + tt - 1) // tt
    for i in range(nt):
        t0 = i * tt
        n = min(tt, ntok - t0)
        x = xs[i % 2]
        o = ob[i % 2]
        fw.dma(cx.q(), x.ap[:, :, :n], src[:, t0:t0 + n].rearrange("(c p) t -> p c t", p=128), reads=src_trks, writes=[x.t])
        fw.op("act", lambda e: e.activation(sq.ap[:, :, :n], x.ap[:, :, :n], AF.Square), reads=[x.t], writes=[sq.t])
        fw.op("dve", lambda e: e.tensor_reduce(part.ap[:, :n], sq.ap[:, :, :n].rearrange("p c t -> p t c"), AX.X, ALU.add),
              reads=[sq.t], writes=[part.t])
        fw.op("dve", lambda e: e.tensor_copy(phi.ap[:, :n], part.ap[:, :n]), reads=[part.t], writes=[phi.t])
        fw.op("dve", lambda e: e.tensor_tensor(plo.ap[:, :n], part.ap[:, :n], phi.ap[:, :n], ALU.subtract), reads=[part.t, phi.t], writes=[plo.t])
        fw.op("pe", lambda e: e.matmul(tot.ap[:, :n], cx.ones_b.ap, phi.ap[:, :n], start=True, stop=False),
              reads=[phi.t, cx.ones_b.t], writes=[tot.t])
        fw.op("pe", lambda e: e.matmul(tot.ap[:, :n], cx.ones_b.ap, plo.ap[:, :n], start=False, stop=True),
              reads=[plo.t, cx.ones_b.t], writes=[tot.t], pe_acc=True)
        fw.op("dve", lambda e: e.tensor_scalar(rstd.ap[:, :n], tot.ap[:, :n], 1.0 / D, EPS, ALU.mult, ALU.add),
              reads=[tot.t], writes=[rstd.t])
        fw.op("act", lambda e: e.activation(rstd.ap[:, :n], rstd.ap[:, :n], AF.Sqrt), reads=[rstd.t], writes=[rstd.t])
        fw.op("dve", lambda e: e.reciprocal(rstd.ap[:, :n], rstd.ap[:, :n]), reads=[rstd.t], writes=[rstd.t])
        fw.op("dve", lambda e: e.tensor_tensor(x.ap[:, :, :n], x.ap[:, :, :n],
                                               rstd.ap[:, :n].unsqueeze(1).to_broadcast([128, KC, n]), ALU.mult),
              reads=[x.t, rstd.t], writes=[x.t])
        fw.op("pool", lambda e: e.tensor_tensor(o.ap[:, :, :n], x.ap[:, :, :n],
                                                nw.ap.unsqueeze(2).to_broadcast([128, KC, n]), ALU.mult),
              reads=[x.t, nw.t], writes=[o.t])
        fw.dma(cx.q(), dst[:, t0:t0 + n].rearrange("(c p) t -> p c t", p=128), o.ap[:, :, :n], reads=[o.t], writes=dst_trk_fn(i))


class Lin:
    def __init__(self, cx, ntok_max=SUP, kc_max=KC):
        self.cx = cx
        self.wst = [cx.sb([128, KC, 128], F32, "wst") for _ in range(2)]
        self.wbf = [cx.sb([128, KC, 128], BF16, "wbf") for _ in range(3)]
        self.psb = [cx.ps([128, 1536], F32, "lps") for _ in range(2)]
        self.iw = 0
        self.ip = 0

    def load_w(self, W, r0, nk, c0, ncol, wtrks=()):
        cx, fw = self.cx, self.cx.fw
        st = self.wst[self.iw % 2]
        wb = self.wbf[self.iw % 3]
        self.iw += 1
        h = max(1, nk // 2)
        qq = cx.q()
        for (a, b) in ((0, h), (h, nk)):
            if b > a:
                fw.dma(qq, st.ap[:, a:b, :ncol], W[r0 + a * 128:r0 + b * 128, c0:c0 + ncol].rearrange("(c p) f -> p c f", p=128),
                       reads=list(wtrks), writes=[st.t] if a == 0 else [], merge=[] if a == 0 else [st.t])
        fw.op("pool", lambda e: e.tensor_copy(wb.ap[:, :nk, :ncol], st.ap[:, :nk, :ncol]), reads=[st.t], writes=[wb.t])
        return wb

    def run(self, act, act_trks, nkc, ntok, W, c0, ncol, epilogue, wtrks=(), tok0=0):
        cx, fw = self.cx, self.cx.fw
        ps = self.psb[self.ip % 2]
        self.ip += 1
        ngrp = (nkc + KC - 1) // KC
        for g in range(ngrp):
            k0 = g * KC
            nk = min(KC, nkc - k0)
            wb = self.load_w(W, k0 * 128, nk, c0, ncol, wtrks)
            for kc in range(nk):
                for s0 in range(0, ntok, 512):
                    n = min(512, ntok - s0)
                    first = (k0 + kc == 0)
                    last = (k0 + kc == nkc - 1)
                    fw.op("pe", lambda e: e.matmul(ps.ap[:ncol, s0:s0 + n], wb.ap[:, kc, :ncol], act[:, k0 + kc, tok0 + s0:tok0 + s0 + n],
                                                   start=first, stop=last),
                          reads=[wb.t] + list(act_trks), writes=[ps.t], pe_acc=not first)
        epilogue(ps)


def stage_gdn(cx, qkvr, GAB, cst_d, convw_d, sconv_d, galog_d, gnorm_d, sstate_d, ssm_p_out, ssm_s_out, oT_d, n_prompt, n_batch):
    fw = cx.fw
    SEG = 1024
    ident = cx.sb([128, 128], F32, "ident")
    Lm = cx.sb([64, 64], F32, "Lm")
    Um = cx.sb([64, 64], F32, "Um")
    fw.dma("sp", ident.ap, cst_d.ap[:, 0, :], writes=[ident.t])
    fw.dma("sp", Lm.ap, cst_d.ap[0:64, 1, 0:64], writes=[Lm.t])
    fw.dma("sp", Um.ap, cst_d.ap[0:64, 2, 0:64], writes=[Um.t])
    ones_f = cx.sb([128, 128], F32, "ones_f")
    fw.op("dve", lambda e: e.memset(ones_f.ap, 1.0), writes=[ones_f.t])
    convw = cx.sb([128, 6, 4], F32, "convw")
    fw.dma("sp", convw.ap, convw_d.ap, writes=[convw.t])
    galog = cx.sb([128, 2], F32, "galog")
    fw.dma("sp", galog.ap, galog_d.ap, writes=[galog.t])
    nega = cx.sb([128, 1], F32, "nega")
    fw.op("act", lambda e: e.activation(nega.ap, galog.ap[:, 0:1], AF.Exp), reads=[galog.t], writes=[nega.t])
    fw.op("dve", lambda e: e.tensor_scalar(nega.ap, nega.ap, -1.0, None, ALU.mult), reads=[nega.t], writes=[nega.t])
    gnb = cx.sb([64, 128], F32, "gnb")
    fw.dma("sp", gnb.ap, bass.AP(gnorm_d.ap.tensor, 0, [[0, 64], [1, 128]]), writes=[gnb.t])
    S = [cx.sb([128, 128], F32, "S") for _ in range(2)]
    xs = cx.sb([128, 6, SEG + 3], F32, "gxs")
    ys = cx.sb([128, 6, SEG], F32, "gys")
    sq = cx.sb([128, SEG], F32, "gsq")
    rn = cx.sb([128, SEG], F32, "grn")
    R = cx.sb([128, SEG], F32, "gR")
    Tg = [cx.sb([128, SEG], F32, "gTg") for _ in range(2)]
    Tb = cx.sb([128, SEG], F32, "gTb")
    Tgl = cx.sb([128, SEG], F32, "gTgl")
    Te2 = cx.sb([128, SEG], F32, "gTe2")
    GL = [cx.sb([128, 256], F32, "gGL") for _ in range(2)]
    oT_seg = cx.sb([128, 2, SEG], BF16, "goT")
    tcs = cx.sb([128, 6, 384], F32, "gtcs")
    tqs = cx.sb([128, 6, 512], F32, "gtqs")
    cols = cx.sb([64, 3, 128], F32, "gcols")
    bank = [cx.ps([128, 512], F32, f"gb{i}") for i in range(7)]

    def sbs(shape, name):
        return [cx.sb(shape, F32, name) for _ in range(2)]
    ecol, nbc, bec = sbs([64, 1], "ecol"), sbs([64, 1], "nbc"), sbs([64, 1], "bec")
    vb, kbe, kt, u_, vnew, o_ = (sbs([64, 128], n) for n in ("vb", "kbe", "kt", "u", "vnew", "o"))
    xm, xp, DT, Ds, N_, aT, Q_ = (sbs([64, 64], n) for n in ("xm", "xp", "DT", "Ds", "N", "aT", "Q"))
    Xa, Xb, XTa, XTb = (sbs([64, 64], n) for n in ("Xa", "Xb", "XTa", "XTb"))
    wT = sbs([128, 64], "wT")
    ss = sbs([64, 1], "ss")
    junk = sbs([64, 128], "junk")

    def mm(out, lhsT, rhs, reads, pb):
        fw.op("pe", lambda e: e.matmul(out, lhsT, rhs, start=True, stop=True), reads=reads, writes=[pb.t])

    def preprocess(t0, nch, C, sample_b0=None):
        n = nch * C
        if sample_b0 is None:
            if t0 == 0:
                fw.op("dve", lambda e: e.memset(xs.ap[:, :, 0:3], 0.0), writes=[xs.t])
                fw.dma("sp", xs.ap[:, :, 3:3 + n], qkvr.ap[:, 0:n].rearrange("(i p) t -> p i t", p=128), reads=[qkvr.t], merge=[xs.t])
            else:
                fw.dma("sp", xs.ap[:, :, 0:3 + n], qkvr.ap[:, t0 - 3:t0 + n].rearrange("(i p) t -> p i t", p=128), reads=[qkvr.t], writes=[xs.t])
            xv = lambda i, j: xs.ap[:, i, j:j + n]
            yv = lambda i: ys.ap[:, i, :n]
        else:
            xs4 = xs.ap[:, :, 0:nch * 7].rearrange("p i (b r) -> p i b r", r=7)
            fw.dma("sp", tcs.ap[:, :, :nch * 3], sconv_d.ap[:, sample_b0 * 3:(sample_b0 + nch) * 3].rearrange("(i p) t -> p i t", p=128), writes=[tcs.t])
            fw.dma("sp", tqs.ap[:, :, :n], qkvr.ap[:, t0:t0 + n].rearrange("(i p) t -> p i t", p=128), reads=[qkvr.t], writes=[tqs.t])
            fw.op("pool", lambda e: e.tensor_copy(xs4[:, :, :, 0:3], tcs.ap[:, :, :nch * 3].rearrange("p i (b r) -> p i b r", r=3)), reads=[tcs.t], writes=[xs.t])
            fw.op("pool", lambda e: e.tensor_copy(xs4[:, :, :, 3:7], tqs.ap[:, :, :n].rearrange("p i (b r) -> p i b r", r=4)), reads=[tqs.t, xs.t], writes=[xs.t])
            xv = lambda i, j: xs4[:, i, :, j:j + 4]
            yv = lambda i: ys.ap[:, i, :n].rearrange("p (b r) -> p b r", r=4)
        for i in range(6):
            en = "dve"
            fw.op(en, lambda e: e.tensor_scalar(yv(i), xv(i, 0), convw.ap[:, i, 0:1], None, ALU.mult), reads=[xs.t, convw.t], writes=[ys.t])
            for j in range(1, 4):
                fw.op(en, lambda e: e.scalar_tensor_tensor(yv(i), xv(i, j), convw.ap[:, i, j:j + 1], yv(i), ALU.mult, ALU.add),
                      reads=[xs.t, convw.t, ys.t], writes=[ys.t])
        fw.op("act", lambda e: e.activation(ys.ap[:, :, :n], ys.ap[:, :, :n], AF.Silu), reads=[ys.t], writes=[ys.t])
        for i in range(4):
            fw.op("act", lambda e: e.activation(sq.ap[:, :n], ys.ap[:, i, :n], AF.Square), reads=[ys.t], writes=[sq.t])
            for s0 in range(0, n, 512):
                m = min(512, n - s0)
                pb = bank[s0 // 512]
                mm(pb.ap[:, :m], ones_f.ap, sq.ap[:, s0:s0 + m], [ones_f.t, sq.t], pb)
                fw.op("dve", lambda e: e.tensor_scalar(rn.ap[:, s0:s0 + m], pb.ap[:, :m], 1e-6, None, ALU.add), reads=[pb.t], writes=[rn.t])
            fw.op("act", lambda e: e.activation(rn.ap[:, :n], rn.ap[:, :n], AF.Sqrt), reads=[rn.t], writes=[rn.t])
            fw.op("dve", lambda e: e.reciprocal(rn.ap[:, :n], rn.ap[:, :n]), reads=[rn.t], writes=[rn.t])
            sc = 128.0 ** -0.5 if i < 2 else 1.0
            fw.op("dve", lambda e: e.scalar_tensor_tensor(ys.ap[:, i, :n], ys.ap[:, i, :n], sc, rn.ap[:, :n], ALU.mult, ALU.mult),
                  reads=[ys.t, rn.t], writes=[ys.t])
        fw.dma("sp", R.ap[:, :n], GAB.ap[:, t0:t0 + n], reads=[GAB.t], writes=[R.t])
        a, b = Tg
        fw.op("act", lambda e: e.activation(a.ap[:, :n], R.ap[:, :n], AF.Exp, bias=galog.ap[:, 1:2]), reads=[R.t, galog.t], writes=[a.t])
        fw.op("act", lambda e: e.activation(a.ap[:, :n], a.ap[:, :n], AF.Ln, bias=1.0), reads=[a.t], writes=[a.t])
        fw.op("dve", lambda e: e.tensor_scalar(a.ap[:, :n], a.ap[:, :n], nega.ap[:, 0:1], None, ALU.mult), reads=[a.t, nega.t], writes=[a.t])
        fw.op("act", lambda e: e.activation(Tb.ap[:, :n], R.ap[:, :n], AF.Sigmoid), reads=[R.t], writes=[Tb.t])
        sft = 1
        while sft < C:
            av = a.ap[:, :n].rearrange("p (c k) -> p c k", k=C)
            bv = b.ap[:, :n].rearrange("p (c k) -> p c k", k=C)
            fw.op("pool", lambda e: e.tensor_copy(bv[:, :, :sft], av[:, :, :sft]), reads=[a.t], writes=[b.t])
            fw.op("dve", lambda e: e.tensor_tensor(bv[:, :, sft:], av[:, :, sft:], av[:, :, :C - sft], ALU.add), reads=[a.t, b.t], writes=[b.t])
            a, b = b, a
            sft *= 2
        gc = a
        gcv = gc.ap[:, :n].rearrange("p (c k) -> p c k", k=C)
        fw.op("dve", lambda e: e.tensor_copy(Tgl.ap[:, :n].rearrange("p (c k) -> p c k", k=C), gcv[:, :, C - 1:C].to_broadcast([128, nch, C])),
              reads=[gc.t], writes=[Tgl.t])
        fw.op("dve", lambda e: e.tensor_tensor(Te2.ap[:, :n], Tgl.ap[:, :n], gc.ap[:, :n], ALU.subtract), reads=[Tgl.t, gc.t], writes=[Te2.t])
        fw.op("act", lambda e: e.activation(Te2.ap[:, :n], Te2.ap[:, :n], AF.Exp), reads=[Te2.t], writes=[Te2.t])
        for hh in range(2):
            p0 = 32 * hh
            pb = bank[2]
            mm(pb.ap[:, :nch], ones_f.ap[p0:p0 + 1, :], gcv[p0:p0 + 1, :, C - 1], [ones_f.t, gc.t], pb)
            fw.op("act", lambda e: e.activation(GL[hh].ap[:, :nch], pb.ap[:, :nch], AF.Exp), reads=[pb.t], writes=[GL[hh].t])
        return gc

    def chunk(gc, ci, C, niter, t_out0, state_io=None):
        c0 = ci * C
        pb = bank[1]
        for k, src in enumerate((gc, Tb, Te2)):
            mm(pb.ap[:C, k * 128:(k + 1) * 128], src.ap[:, c0:c0 + C], ident.ap, [src.t, ident.t], pb)
        fw.op("act", lambda e: e.copy(cols.ap[:C].rearrange("p k j -> p (k j)"), pb.ap[:C, 0:384]), reads=[pb.t], writes=[cols.t])
        for hh in range(2):
            p0 = 32 * hh
            qT, kT, vT = (ys.ap[:, 2 * k + hh, c0:c0 + C] for k in range(3))
            gcc, bc, e2c = cols.ap[:C, 0, p0:p0 + 1], cols.ap[:C, 1, 64 + p0:65 + p0], cols.ap[:C, 2, p0:p0 + 1]
            Sh = S[hh]
            if state_io is not None:
                fw.dma("sp", Sh.ap, state_io[0](hh), writes=[Sh.t])
            b0 = bank[0]
            mm(b0.ap[:C, 0:128], kT, ident.ap, [ys.t, ident.t], b0)
            mm(b0.ap[:C, 128:256], vT, ident.ap, [ys.t, ident.t], b0)
            fw.op("act", lambda e: e.activation(ecol[hh].ap[:C], gcc, AF.Exp), reads=[cols.t], writes=[ecol[hh].t])
            fw.op("dve", lambda e: e.tensor_scalar(nbc[hh].ap[:C], bc, -1.0, None, ALU.mult), reads=[cols.t], writes=[nbc[hh].t])
            fw.op("dve", lambda e: e.tensor_tensor(bec[hh].ap[:C], bc, ecol[hh].ap[:C], ALU.mult), reads=[cols.t, ecol[hh].t], writes=[bec[hh].t])
            fw.op("act", lambda e: e.activation(vb[hh].ap[:C], b0.ap[:C, 128:256], AF.Copy, scale=bc), reads=[b0.t, cols.t], writes=[vb[hh].t])
            fw.op("dve", lambda e: e.tensor_scalar(kbe[hh].ap[:C], b0.ap[:C, 0:128], bec[hh].ap[:C], None, ALU.mult), reads=[b0.t, bec[hh].t], writes=[kbe[hh].t])
            fw.op("dve", lambda e: e.tensor_scalar(kt[hh].ap[:C], b0.ap[:C, 0:128], e2c, None, ALU.mult), reads=[b0.t, cols.t], writes=[kt[hh].t])
            b2 = bank[2]
            mm(b2.ap[:C, 0:C], ones_f.ap[p0:p0 + 1, :C], gc.ap[p0:p0 + 1, c0:c0 + C], [ones_f.t, gc.t], b2)
            fw.op("dve", lambda e: e.tensor_scalar(xm[hh].ap[:C, :C], b2.ap[:C, 0:C], gcc, 0.0, ALU.subtract, ALU.min), reads=[b2.t, cols.t], writes=[xm[hh].t])
            fw.op("dve", lambda e: e.tensor_scalar(xp[hh].ap[:C, :C], b2.ap[:C, 0:C], gcc, 0.0, ALU.subtract, ALU.max), reads=[b2.t, cols.t], writes=[xp[hh].t])
            fw.op("act", lambda e: e.activation(xm[hh].ap[:C, :C], xm[hh].ap[:C, :C], AF.Exp), reads=[xm[hh].t], writes=[xm[hh].t])
            fw.op("act", lambda e: e.activation(xp[hh].ap[:C, :C], xp[hh].ap[:C, :C], AF.Exp, scale=-1.0), reads=[xp[hh].t], writes=[xp[hh].t])
            fw.op("pool", lambda e: e.tensor_tensor(DT[hh].ap[:C, :C], xm[hh].ap[:C, :C], Um.ap[:C, :C], ALU.mult), reads=[xm[hh].t, Um.t], writes=[DT[hh].t])
            fw.op("pool", lambda e: e.tensor_tensor(Ds[hh].ap[:C, :C], xp[hh].ap[:C, :C], Lm.ap[:C, :C], ALU.mult), reads=[xp[hh].t, Lm.t], writes=[Ds[hh].t])
            mm(b2.ap[:C, 64:64 + C], kT, kT, [ys.t], b2)
            fw.op("dve", lambda e: e.scalar_tensor_tensor(N_[hh].ap[:C, :C], b2.ap[:C, 64:64 + C], nbc[hh].ap[:C], Ds[hh].ap[:C, :C], ALU.mult, ALU.mult),
                  reads=[b2.t, nbc[hh].t, Ds[hh].t], writes=[N_[hh].t])
            mm(b2.ap[:C, 128:128 + C], kT, qT, [ys.t], b2)
            fw.op("dve", lambda e: e.tensor_tensor(aT[hh].ap[:C, :C], b2.ap[:C, 128:128 + C], DT[hh].ap[:C, :C], ALU.mult), reads=[b2.t, DT[hh].t], writes=[aT[hh].t])
            mm(b2.ap[:C, 192:192 + C], N_[hh].ap[:C, :C], ident.ap[:C, :C], [N_[hh].t, ident.t], b2)
            X, XT, Xn, XTn = Xa[hh], N_[hh], Xb[hh], XTb[hh]
            fw.op("act", lambda e: e.copy(X.ap[:C, :C], b2.ap[:C, 192:192 + C]), reads=[b2.t], writes=[X.t])
            fw.op("dve", lambda e: e.tensor_tensor(Q_[hh].ap[:C, :C], b2.ap[:C, 192:192 + C], ident.ap[:C, :C], ALU.add), reads=[b2.t, ident.t], writes=[Q_[hh].t])
            b3 = bank[3]
            for it in range(niter):
                mm(b3.ap[:C, 0:C], XT.ap[:C, :C], X.ap[:C, :C], [X.t, XT.t], b3)
                mm(b3.ap[:C, 64:64 + C], X.ap[:C, :C], XT.ap[:C, :C], [X.t, XT.t], b3)
                fw.op("act", lambda e: e.copy(Xn.ap[:C, :C], b3.ap[:C, 0:C]), reads=[b3.t], writes=[Xn.t])
                fw.op("dve", lambda e: e.tensor_copy(XTn.ap[:C, :C], b3.ap[:C, 64:64 + C]), reads=[b3.t], writes=[XTn.t])
                mm(b3.ap[:C, 128:128 + C], XTn.ap[:C, :C], Q_[hh].ap[:C, :C], [XTn.t, Q_[hh].t], b3)
                fw.op("dve", lambda e: e.tensor_tensor(Q_[hh].ap[:C, :C], Q_[hh].ap[:C, :C], b3.ap[:C, 128:128 + C], ALU.add), reads=[b3.t, Q_[hh].t], writes=[Q_[hh].t])
                if it == 0:
                    X, XT, Xn, XTn = Xn, XTn, Xa[hh], XTa[hh]
                else:
                    X, XT, Xn, XTn = Xn, XTn, X, XT
            b4 = bank[4]
            mm(b4.ap[:C, 0:128], Q_[hh].ap[:C, :C], vb[hh].ap[:C], [Q_[hh].t, vb[hh].t], b4)
            mm(b4.ap[:, 128:128 + C], kbe[hh].ap[:C], Q_[hh].ap[:C, :C], [Q_[hh].t, kbe[hh].t], b4)
            fw.op("act", lambda e: e.copy(u_[hh].ap[:C], b4.ap[:C, 0:128]), reads=[b4.t], writes=[u_[hh].t])
            fw.op("dve", lambda e: e.tensor_copy(wT[hh].ap[:, :C], b4.ap[:, 128:128 + C]), reads=[b4.t], writes=[wT[hh].t])
            b5, b6 = bank[5], bank[6]
            mm(b5.ap[:C, 0:128], wT[hh].ap[:, :C], Sh.ap, [wT[hh].t, Sh.t], b5)
            mm(b5.ap[:C, 128:256], qT, Sh.ap, [ys.t, Sh.t], b5)
            fw.op("dve", lambda e: e.tensor_tensor(vnew[hh].ap[:C], u_[hh].ap[:C], b5.ap[:C, 0:128], ALU.subtract), reads=[b5.t, u_[hh].t], writes=[vnew[hh].t])
            fw.op("act", lambda e: e.activation(o_[hh].ap[:C], b5.ap[:C, 128:256], AF.Copy, scale=ecol[hh].ap[:C]), reads=[b5.t, ecol[hh].t], writes=[o_[hh].t])
            mm(b5.ap[:C, 256:384], aT[hh].ap[:C, :C], vnew[hh].ap[:C], [aT[hh].t, vnew[hh].t], b5)
            fw.op("dve", lambda e: e.tensor_tensor(o_[hh].ap[:C], o_[hh].ap[:C], b5.ap[:C, 256:384], ALU.add), reads=[b5.t, o_[hh].t], writes=[o_[hh].t])
            mm(b6.ap[:, 0:128], kt[hh].ap[:C], vnew[hh].ap[:C], [kt[hh].t, vnew[hh].t], b6)
            fw.op("dve", lambda e: e.scalar_tensor_tensor(Sh.ap, Sh.ap, GL[hh].ap[:, ci:ci + 1], b6.ap[:, 0:128], ALU.mult, ALU.add),
                  reads=[b6.t, Sh.t, GL[hh].t], writes=[Sh.t])
            if state_io is not None:
                fw.dma("sp", state_io[1](hh), Sh.ap, reads=[Sh.t])
            fw.op("act", lambda e: e.activation(junk[hh].ap[:C], o_[hh].ap[:C], AF.Square, accum_out=ss[hh].ap[:C]), reads=[o_[hh].t], writes=[junk[hh].t, ss[hh].t])
            fw.op("dve", lambda e: e.tensor_scalar(ss[hh].ap[:C], ss[hh].ap[:C], 1.0 / 128, EPS, ALU.mult, ALU.add), reads=[ss[hh].t], writes=[ss[hh].t])
            fw.op("act", lambda e: e.activation(ss[hh].ap[:C], ss[hh].ap[:C], AF.Sqrt), reads=[ss[hh].t], writes=[ss[hh].t])
            fw.op("dve", lambda e: e.reciprocal(ss[hh].ap[:C], ss[hh].ap[:C]), reads=[ss[hh].t], writes=[ss[hh].t])
            fw.op("dve", lambda e: e.scalar_tensor_tensor(o_[hh].ap[:C], o_[hh].ap[:C], ss[hh].ap[:C], gnb.ap[:C], ALU.mult, ALU.mult),
                  reads=[o_[hh].t, ss[hh].t, gnb.t], writes=[o_[hh].t])
            mm(b6.ap[:, 128:128 + C], o_[hh].ap[:C], ident.ap[:C, :C], [o_[hh].t, ident.t], b6)
            fw.op("act", lambda e: e.copy(oT_seg.ap[:, hh, c0:c0 + C], b6.ap[:, 128:128 + C]), reads=[b6.t], writes=[oT_seg.t])

    for hh in range(2):
        fw.op("dve", lambda e: e.memset(S[hh].ap, 0.0), writes=[S[hh].t])
    C = 64
    for t0 in range(0, n_prompt, SEG):
        n = min(SEG, n_prompt - t0)
        nch = n // C
        gc = preprocess(t0, nch, C)
        for ci in range(nch):
            chunk(gc, ci, C, 5, t0)
        for k in range(n // 128):
            g = t0 // 128 + k
            fw.dma("sp", oT_d.ap[g % NCORES_L, :, (g // NCORES_L) * 128:(g // NCORES_L + 1) * 128].rearrange("(h p) t -> p h t", p=128),
                   oT_seg.ap[:, :, k * 128:(k + 1) * 128], reads=[oT_seg.t], writes=[oT_d.t])
    for hh in range(2):
        fw.dma("sp", ssm_p_out.ap[hh], S[hh].ap, reads=[S[hh].t])
    C = 4
    BSEG = 128
    for b0 in range(0, n_batch, BSEG):
        nb = min(BSEG, n_batch - b0)
        t0 = n_prompt + b0 * 4
        gc = preprocess(t0, nb, C, sample_b0=b0)
        for ci in range(nb):
            b = b0 + ci
            chunk(gc, ci, C, 1, t0, state_io=(lambda hh, b=b: sstate_d.ap[b, hh], lambda hh, b=b: ssm_s_out.ap[b, hh]))
        NBC = n_batch // NCORES_L
        for grp in range(nb // NBC):
            dest = (b0 + grp * NBC) // NBC
            fw.dma("sp", oT_d.ap[dest, :, NBLK * 128:NBLK * 128 + NBC * 4].rearrange("(h p) t -> p h t", p=128),
                   oT_seg.ap[:, :, grp * NBC * 4:(grp + 1) * NBC * 4], reads=[oT_seg.t], writes=[oT_d.t])


NEG = -1.0e30


def build_bias_factors(cx, rb_sb, ohd_d, valid_d, fv_d, J, dst, tiles, nq):
    fw = cx.fw
    ohd = cx.sb([32, J], F32, "ohd")
    val = cx.sb([16, J], F32, "val")
    fv = cx.sb([16, J], F32, "fv")
    stg = [cx.sb([128, nq], F32, "bfst") for _ in range(2)]
    ps = cx.ps([128, 512], F32, "bfps")
    fw.dma("sp", ohd.ap, ohd_d, writes=[ohd.t])
    fw.dma("sp", val.ap, valid_d, writes=[val.t])
    for j0 in range(0, J, 512):
        n = min(512, J - j0)
        fw.op("pe", lambda e: e.matmul(ps.ap[:16, :n], rb_sb.ap, ohd.ap[:, j0:j0 + n], start=True, stop=True), reads=[rb_sb.t, ohd.t], writes=[ps.t])
        fw.op("act", lambda e: e.activation(fv.ap[:, j0:j0 + n], ps.ap[:16, :n], AF.Exp), reads=[ps.t], writes=[fv.t])
    fw.op("dve", lambda e: e.tensor_tensor(fv.ap, fv.ap, val.ap, ALU.mult), reads=[fv.t, val.t], writes=[fv.t])
    fw.dma("sp", fv_d.ap, fv.ap, reads=[fv.t], writes=[fv_d.t])
    k = 0
    for ti, base in enumerate(tiles):
        for h in range(16):
            st = stg[k % 2]
            k += 1
            src = bass.AP(fv_d.ap.tensor, h * J + base, [[1, 128], [-1, nq]])
            fw.dma("sp", st.ap, src, reads=[fv_d.t], writes=[st.t], allow_slow_non_contiguous=True)
            fw.op("pool", lambda e: e.tensor_copy(dst.ap[:, ti, h, :], st.ap), reads=[st.t], writes=[dst.t])


class Attn:
    def __init__(self, cx, Lmax, nqmax, ident_bf, ones_bf):
        self.cx = cx
        self.ident_bf, self.ones_bf = ident_bf, ones_bf
        self.sc = cx.sb([128, Lmax], F32, "a_sc")
        self.wk = cx.sb([128, Lmax], F32, "a_wk")
        self.r = [cx.sb([128, 512], F32, "a_r") for _ in range(2)]
        self.mx = cx.sb([128, 8], F32, "a_mx")
        self.thr = cx.sb([128, 1], F32, "a_thr")
        self.P = [cx.sb([128, 512], BF16, "a_P") for _ in range(2)]
        self.ot = cx.sb([128, 16, 128], BF16, "a_ot")
        self.rec = cx.sb([128, 4], F32, "a_rec")
        self.aT = cx.sb([128, 16, 128], BF16, "a_aT")
        self.ps_i = [cx.ps([128, 512], F32, "a_psi") for _ in range(2)]
        self.ps_l = [cx.ps([128, 512], F32, "a_psl") for _ in range(2)]
        self.ps_o = cx.ps([128, 512], F32, "a_pso")
        self.ps_d = cx.ps([128, 512], F32, "a_psd")
        self.ps_t = cx.ps([128, 512], F32, "a_pst")
        self.ii = 0
        self.il = 0

    def run(self, nq, NKT, qiT, wi, qT, kiT, kv_tile, tail_w, cm, pen, topk, bf, bf_tiles, out_dst, out_trk, out_sb=None):
        cx, fw = self.cx, self.cx.fw
        L = NKT * 128
        sc, wk = self.sc, self.wk
        for s0 in range(0, L, 512):
            n = min(512, L - s0)
            for hi in range(32):
                ps = self.ps_i[self.ii % 2]
                r = self.r[self.ii % 2]
                self.ii += 1
                fw.op("pe", lambda e: e.matmul(ps.ap[:nq, :n], qiT.ap[:, hi, :nq], kiT.ap[:, s0:s0 + n], start=True, stop=True),
                      reads=[qiT.t, kiT.t], writes=[ps.t])
                fw.op("act", lambda e: e.activation(r.ap[:nq, :n], ps.ap[:nq, :n], AF.Relu), reads=[ps.t], writes=[r.t])
                if hi == 0:
                    fw.op("dve", lambda e: e.tensor_scalar(sc.ap[:nq, s0:s0 + n], r.ap[:nq, :n], wi.ap[:nq, 0:1], None, ALU.mult),
                          reads=[r.t, wi.t], writes=[sc.t])
                else:
                    fw.op("dve", lambda e: e.scalar_tensor_tensor(sc.ap[:nq, s0:s0 + n], r.ap[:nq, :n], wi.ap[:nq, hi:hi + 1], sc.ap[:nq, s0:s0 + n], ALU.mult, ALU.add),
                          reads=[r.t, wi.t, sc.t], writes=[sc.t])
        tl = sc.ap[:nq, L - tail_w:L]
        fw.op("dve", lambda e: e.tensor_tensor(tl, tl, cm.ap[:nq, :tail_w], ALU.mult), reads=[sc.t, cm.t], writes=[sc.t])
        fw.op("dve", lambda e: e.tensor_tensor(tl, tl, pen.ap[:nq, :tail_w], ALU.add), reads=[sc.t, pen.t], writes=[sc.t])
        fw.op("pool", lambda e: e.tensor_copy(wk.ap[:nq, :L], sc.ap[:nq, :L]), reads=[sc.t], writes=[wk.t])
        for it in range(topk // 8):
            fw.op("dve", lambda e: e.max(out=self.mx.ap[:nq], in_=wk.ap[:nq, :L]), reads=[wk.t], writes=[self.mx.t])
            if it < topk // 8 - 1:
                fw.op("dve", lambda e: e.match_replace(out=wk.ap[:nq, :L], in_to_replace=self.mx.ap[:nq], in_values=wk.ap[:nq, :L], imm_value=NEG),
                      reads=[self.mx.t, wk.t], writes=[wk.t])
        fw.op("dve", lambda e: e.tensor_scalar(self.thr.ap[:nq], self.mx.ap[:nq, 7:8], -1.0e29, None, ALU.max), reads=[self.mx.t], writes=[self.thr.t])
        wkb = wk.ap.bitcast(BF16)
        Lm = self.sc.ap.shape[1]
        m01 = wkb[:, 0:L]
        mT = wkb[:, Lm:Lm + NKT * nq].rearrange("p (k t) -> p k t", t=nq)
        fw.op("dve", lambda e: e.tensor_scalar(m01[:nq], sc.ap[:nq, :L], self.thr.ap[:nq, 0:1], None, ALU.is_ge), reads=[sc.t, self.thr.t], writes=[wk.t])
        per = max(1, 512 // nq)
        for k0 in range(0, NKT, per):
            kn = min(per, NKT - k0)
            pt = self.ps_t
            for k in range(kn):
                fw.op("pe", lambda e: e.matmul(pt.ap[:, k * nq:(k + 1) * nq], m01[:nq, (k0 + k) * 128:(k0 + k + 1) * 128], self.ident_bf.ap[:nq, :nq], start=True, stop=True),
                      reads=[wk.t, self.ident_bf.t], writes=[pt.t])
            fw.op("act", lambda e: e.copy(mT[:, k0:k0 + kn, :].rearrange("p k t -> p (k t)"), pt.ap[:, :kn * nq]), reads=[pt.t], writes=[wk.t])
        G = 4
        scale = 128.0 ** -0.5
        for kvh in range(4):
            po, pd = self.ps_o, self.ps_d
            for kt in range(NKT):
                KTt, Vt, kv_trks = kv_tile(kvh, kt)
                pl = self.ps_l[self.il % 2]
                P = self.P[self.il % 2]
                self.il += 1
                Pv = P.ap[:, :G * nq].rearrange("p (g t) -> p g t", t=nq)
                fw.op("pe", lambda e: e.matmul(pl.ap[:, :G * nq].rearrange("p (g t) -> p g t", t=nq), KTt, qT.ap[:, kvh * G:(kvh + 1) * G, :nq], start=True, stop=True),
                      reads=[qT.t] + kv_trks, writes=[pl.t])
                fw.op("act", lambda e: e.activation(P.ap[:, :G * nq], pl.ap[:, :G * nq], AF.Exp, scale=scale), reads=[pl.t], writes=[P.t])
                fw.op("dve", lambda e: e.tensor_tensor(Pv, Pv, mT[:, kt:kt + 1, :].to_broadcast([128, G, nq]), ALU.mult), reads=[P.t, wk.t], writes=[P.t])
                if kt in bf_tiles:
                    bi = bf_tiles[kt]
                    fw.op("dve", lambda e: e.tensor_tensor(Pv, Pv, bf.ap[:, bi, kvh * G:(kvh + 1) * G, :nq], ALU.mult), reads=[P.t, bf.t], writes=[P.t])
                for g in range(G):
                    fw.op("pe", lambda e: e.matmul(po.ap[:nq, g * 128:(g + 1) * 128], Pv[:, g, :], Vt, start=(kt == 0 and g == 0), stop=(kt == NKT - 1 and g == G - 1)),
                          reads=[P.t] + kv_trks, writes=[po.t], pe_acc=(kt > 0 or g > 0))
                for g in range(G):
                    fw.op("pe", lambda e: e.matmul(pd.ap[:nq, g:g + 1], Pv[:, g, :], self.ones_bf.ap[:, 0:1], start=(kt == 0 and g == 0), stop=(kt == NKT - 1 and g == G - 1)),
                          reads=[P.t, self.ones_bf.t], writes=[pd.t], pe_acc=(kt > 0 or g > 0))
            fw.op("dve", lambda e: e.reciprocal(self.rec.ap[:nq], pd.ap[:nq, 0:4]), reads=[pd.t], writes=[self.rec.t])
            for g in range(G):
                fw.op("act", lambda e: e.activation(self.ot.ap[:nq, kvh * G + g, :], po.ap[:nq, g * 128:(g + 1) * 128], AF.Copy, scale=self.rec.ap[:nq, g:g + 1]),
                      reads=[po.t, self.rec.t], writes=[self.ot.t])
        for h0 in range(0, 16, 4):
            pt = self.ps_t
            per_h = nq
            for k in range(4):
                fw.op("pe", lambda e: e.matmul(pt.ap[:, k * per_h:(k + 1) * per_h], self.ot.ap[:nq, h0 + k, :], self.ident_bf.ap[:nq, :nq], start=True, stop=True),
                      reads=[self.ot.t, self.ident_bf.t], writes=[pt.t])
            fw.op("act", lambda e: e.copy(self.aT.ap[:, h0:h0 + 4, :nq], pt.ap[:, :4 * per_h].rearrange("p (k t) -> p k t", t=per_h)), reads=[pt.t], writes=[self.aT.t])
        if out_sb is not None:
            ob, c0 = out_sb
            fw.op("pool", lambda e: e.tensor_copy(ob.ap[:, :, c0:c0 + nq], self.aT.ap[:, :, :nq]), reads=[self.aT.t], writes=[ob.t])
        else:
            fw.dma("sp", out_dst.rearrange("(h d) t -> d h t", d=128), self.aT.ap[:, :, :nq], reads=[self.aT.t], writes=out_trk)


def stage_vtok(cx, VT, Vtok, ident_bf, ntok):
    fw = cx.fw
    vin = [cx.sb([128, 4, 128], BF16, "vt_in") for _ in range(2)]
    vout = [cx.sb([128, 512], BF16, "vt_out") for _ in range(2)]
    ps = [cx.ps([128, 512], F32, "vt_ps") for _ in range(2)]
    for i in range(ntok // 128):
        a, o, p = vin[i % 2], vout[i % 2], ps[i % 2]
        fw.dma("sp", a.ap, VT.ap[:, i * 128:(i + 1) * 128].rearrange("(h d) t -> d h t", d=128), reads=[VT.t], writes=[a.t])
        for h in range(4):
            fw.op("pe", lambda e: e.matmul(p.ap[:, h * 128:(h + 1) * 128], a.ap[:, h, :], ident_bf.ap, start=True, stop=True), reads=[a.t, ident_bf.t], writes=[p.t])
        fw.op("act", lambda e: e.copy(o.ap, p.ap), reads=[p.t], writes=[o.t])
        fw.dma("sp", Vtok.ap[i * 128:(i + 1) * 128, :], o.ap, reads=[o.t], writes=[Vtok.t])


def stage_attn_prompt(cx, D_, ident_bf, ones_bf, ident_f, rb_sb):
    fw = cx.fw
    Lmax = STRIDE * NBLK * 128
    TW = STRIDE * 128
    A = Attn(cx, Lmax, 128, ident_bf, ones_bf)
    cmpen = cx.sb([128, 2, TW], F32, "cmpen")
    fw.dma("sp", cmpen.ap, D_["cmp"].ap, writes=[cmpen.t])
    cmB, penB = B(cmpen.ap[:, 0, :]), B(cmpen.ap[:, 1, :])
    cmB.t = penB.t = cmpen.t
    NW = STRIDE + 1
    bf = cx.sb([128, NW, 16, 128], BF16, "bf_p")
    Jp = 255 + STRIDE * 128
    build_bias_factors(cx, rb_sb, D_["ohd_p"].ap, D_["val_p"].ap, D_["fv_p"], Jp, bf, [255 + kr * 128 for kr in range(-1, STRIDE)], 128)
    qiT = cx.sb([128, 32, 128], BF16, "p_qiT")
    qT = cx.sb([128, 16, 128], BF16, "p_qT")
    wiT = cx.sb([32, 128], F32, "p_wiT")
    wi = cx.sb([128, 32], F32, "p_wi")
    kiT = cx.sb([128, Lmax], BF16, "p_kiT")
    CH = 16
    KTc = [cx.sb([128, CH * 128], BF16, "p_KT") for _ in range(2)]
    Vc = [cx.sb([128, CH, 128], BF16, "p_V") for _ in range(2)]
    state = {"i": 0, "cur": None}
    for j in range(NBLK):
        NKT = STRIDE * (j + 1)
        L = NKT * 128
        t0 = j * 128
        fw.dma("sp", qiT.ap, D_["qiT"].ap[:, t0:t0 + 128].rearrange("(h d) t -> d h t", d=128), reads=[D_["qiT"].t], writes=[qiT.t])
        fw.dma("sp", qT.ap, D_["qT"].ap[:, t0:t0 + 128].rearrange("(h d) t -> d h t", d=128), reads=[D_["qT"].t], writes=[qT.t])
        fw.dma("sp", wiT.ap, D_["wiT"].ap[:, t0:t0 + 128], reads=[D_["wiT"].t], writes=[wiT.t])
        fw.op("pe", lambda e: e.matmul(A.ps_t.ap[:, :32], wiT.ap, ident_f.ap[:32, :32], start=True, stop=True), reads=[wiT.t, ident_f.t], writes=[A.ps_t.t])
        fw.op("act", lambda e: e.copy(wi.ap, A.ps_t.ap[:, :32]), reads=[A.ps_t.t], writes=[wi.t])
        for l0 in range(0, L, 2048):
            n = min(2048, L - l0)
            fw.dma("sp", kiT.ap[:, l0:l0 + n], D_["KIT"].ap[:, l0:l0 + n], reads=[D_["KIT"].t], writes=[kiT.t] if l0 == 0 else [], merge=[] if l0 == 0 else [kiT.t])

        def kv_tile(kvh, kt, NKT=NKT):
            c0 = (kt // CH) * CH
            key = (j, kvh, c0)
            if state["cur"] != key:
                state["cur"] = key
                state["i"] += 1
                kb, vb_ = KTc[state["i"] % 2], Vc[state["i"] % 2]
                n = min(CH, NKT - c0)
                fw.dma("sp", kb.ap[:, :n * 128], D_["KT"].ap[kvh * 128:(kvh + 1) * 128, c0 * 128:(c0 + n) * 128], reads=[D_["KT"].t], writes=[kb.t])
                fw.dma("sp", vb_.ap[:, :n, :], D_["Vtok"].ap[c0 * 128:(c0 + n) * 128, kvh * 128:(kvh + 1) * 128].rearrange("(k p) d -> p k d", p=128),
                       reads=[D_["Vtok"].t], writes=[vb_.t])
            kb, vb_ = KTc[state["i"] % 2], Vc[state["i"] % 2]
            k = kt - c0
            return kb.ap[:, k * 128:(k + 1) * 128], vb_.ap[:, k, :], [kb.t, vb_.t]
        bf_tiles = {}
        for bi, kr in enumerate(range(-1, STRIDE)):
            kt = NKT - STRIDE + kr
            if kt >= 0:
                bf_tiles[kt] = bi
        A.run(128, NKT, qiT, wi, qT, kiT, kv_tile, TW, cmB, penB, TOPK_P, bf, bf_tiles, D_["attnT"].ap[:, t0:t0 + 128], [D_["attnT"].t])


def stage_attn_sample(cx, D_, ident_bf, ones_bf, ident_f, rb_sb):
    fw = cx.fw
    nc = cx.nc
    U32 = mybir.dt.uint32
    NPG = PAST // 128
    NKT = NPG + 1
    L = NKT * 128
    NBC = NB // NCORES_L
    nst = NBC * 4
    A = Attn(cx, L, 4, ident_bf, ones_bf)
    cmpen = cx.sb([128, 2, 128], F32, "cmpen_s")
    fw.dma("sp", cmpen.ap, D_["cms"].ap, writes=[cmpen.t])
    cmB, penB = B(cmpen.ap[:, 0, :]), B(cmpen.ap[:, 1, :])
    cmB.t = penB.t = cmpen.t
    bf = cx.sb([128, 2, 16, 4], BF16, "bf_s")
    Js = 259
    build_bias_factors(cx, rb_sb, D_["ohd_s"].ap, D_["val_s"].ap, D_["fv_s"], Js, bf, [3, 3 + 128], 4)
    npt = NBC * NPG
    pti = cx.sb([128, npt], I32, "pti")
    ptf = cx.sb([128, npt], F32, "ptf")
    idx = cx.sb([128, npt], I32, "idx")
    pcol = cx.sb([128, 1], F32, "pcol")
    fw.dma("sp", pti.ap, bass.AP(D_["pt"].ap.tensor, 0, [[0, 128], [1, npt]]), writes=[pti.t])
    fw.dma("sp", pcol.ap, D_["pcol"].ap, writes=[pcol.t])
    fw.op("dve", lambda e: e.tensor_copy(ptf.ap, pti.ap), reads=[pti.t], writes=[ptf.t])
    fw.op("dve", lambda e: e.tensor_scalar(ptf.ap, ptf.ap, 128.0, pcol.ap[:, 0:1], ALU.mult, ALU.add), reads=[ptf.t, pcol.t], writes=[ptf.t])
    fw.op("dve", lambda e: e.tensor_copy(idx.ap, ptf.ap), reads=[ptf.t], writes=[idx.t])
    s0 = NBLK * 128
    qiTs = cx.sb([128, 32, nst], BF16, "s_qiT")
    qTs = cx.sb([128, 16, nst], BF16, "s_qT")
    wiTs = cx.sb([32, nst], F32, "s_wiT")
    fw.dma("sp", qiTs.ap, D_["qiT"].ap[:, s0:s0 + nst].rearrange("(h d) t -> d h t", d=128), reads=[D_["qiT"].t], writes=[qiTs.t])
    fw.dma("sp", qTs.ap, D_["qT"].ap[:, s0:s0 + nst].rearrange("(h d) t -> d h t", d=128), reads=[D_["qT"].t], writes=[qTs.t])
    fw.dma("sp", wiTs.ap, D_["wiT"].ap[:, s0:s0 + nst], reads=[D_["wiT"].t], writes=[wiTs.t])
    qiTb = cx.sb([128, 32, 4], BF16, "s_qiTb")
    qTb = cx.sb([128, 16, 4], BF16, "s_qTb")
    wi = cx.sb([128, 32], F32, "s_wi")
    Kg = cx.sb([128, NPG, 512], BF16, "s_Kg")
    Vg = cx.sb([128, NPG, 512], BF16, "s_Vg")
    kig = cx.sb([128, NPG, 128], BF16, "s_kig")
    KTb = cx.sb([128, 4, L], BF16, "s_KTb")
    kiTb = cx.sb([128, L], BF16, "s_kiTb")
    Vn = cx.sb([128, 512], BF16, "s_Vn")
    aTs = cx.sb([128, 16, nst], BF16, "s_aTs")
    fw.op("pool", lambda e: e.memset(KTb.ap, 0.0), writes=[KTb.t])
    fw.op("pool", lambda e: e.memset(kiTb.ap, 0.0), writes=[kiTb.t])
    fw.op("pool", lambda e: e.memset(Vn.ap, 0.0), writes=[Vn.t])
    gsem = nc.alloc_semaphore("gsem")
    gcount = [0]
    gtrk_name = "gather"
    peng = fw.engs["pool"]

    def gather(dst, src_rows, col):
        for r, s_ in fw._deps([idx.t], [dst.t]).items():
            fw._wait(peng, r, s_)
        for pg in range(NPG):
            ins = nc.gpsimd.indirect_dma_start(out=dst.ap[:, pg, :], out_offset=None, in_=src_rows,
                                               in_offset=bass.IndirectOffsetOnAxis(ap=idx.ap[:, col + pg:col + pg + 1].bitcast(U32), axis=0))
            ins.then_inc(gsem, 16)
            gcount[0] += 1
        for en in ("pe", "act", "dve", "pool"):
            fw.engs[en].e.wait_ge(gsem, gcount[0] * 16)
        dst.t.w = {}
        dst.t.r = {}

    pt_ = A.ps_t
    for b in range(NBC):
        col = b * NPG
        tb = NP + (b) * 4
        gather(Kg, D_["ck"].ap, col)
        gather(Vg, D_["cv"].ap, col)
        gather(kig, D_["cki"].ap, col)
        fw.dma("sp", KTb.ap[:, :, PAST:PAST + 4], D_["KTsmp"].ap[:, b * 4:(b + 1) * 4].rearrange("(h d) t -> d h t", d=128), reads=[D_["KTsmp"].t], writes=[KTb.t])
        fw.dma("sp", kiTb.ap[:, PAST:PAST + 4], D_["KITsmp"].ap[:, b * 4:(b + 1) * 4], reads=[D_["KITsmp"].t], writes=[kiTb.t])
        fw.dma("sp", Vn.ap[0:4, :], D_["Vsmp"].ap[b * 4:(b + 1) * 4, :], reads=[D_["Vsmp"].t], writes=[Vn.t])
        for kvh in range(5):
            for g0 in range(0, NPG, 4):
                gn = min(4, NPG - g0)
                for k in range(gn):
                    src = Kg.ap[:, g0 + k, kvh * 128:(kvh + 1) * 128] if kvh < 4 else kig.ap[:, g0 + k, :]
                    fw.op("pe", lambda e: e.matmul(pt_.ap[:, k * 128:(k + 1) * 128], src, ident_bf.ap, start=True, stop=True),
                          reads=[Kg.t if kvh < 4 else kig.t, ident_bf.t], writes=[pt_.t])
                dstv = KTb.ap[:, kvh, g0 * 128:(g0 + gn) * 128] if kvh < 4 else kiTb.ap[:, g0 * 128:(g0 + gn) * 128]
                fw.op("act" if (g0 // 4) % 2 == 0 else "dve", (lambda e: e.copy(dstv, pt_.ap[:, :gn * 128])) if (g0 // 4) % 2 == 0 else (lambda e: e.tensor_copy(dstv, pt_.ap[:, :gn * 128])),
                      reads=[pt_.t], writes=[KTb.t if kvh < 4 else kiTb.t])
        fw.op("pool", lambda e: e.tensor_copy(qiTb.ap, qiTs.ap[:, :, b * 4:(b + 1) * 4]), reads=[qiTs.t], writes=[qiTb.t])
        fw.op("pool", lambda e: e.tensor_copy(qTb.ap, qTs.ap[:, :, b * 4:(b + 1) * 4]), reads=[qTs.t], writes=[qTb.t])
        fw.op("pe", lambda e: e.matmul(pt_.ap[:4, :32], wiTs.ap[:, b * 4:(b + 1) * 4], ident_f.ap[:32, :32], start=True, stop=True), reads=[wiTs.t, ident_f.t], writes=[pt_.t])
        fw.op("act", lambda e: e.copy(wi.ap[:4], pt_.ap[:4, :32]), reads=[pt_.t], writes=[wi.t])

        def kv_tile(kvh, kt):
            if kt < NPG:
                return KTb.ap[:, kvh, kt * 128:(kt + 1) * 128], Vg.ap[:, kt, kvh * 128:(kvh + 1) * 128], [KTb.t, Vg.t]
            return KTb.ap[:, kvh, PAST:PAST + 128], Vn.ap[:, kvh * 128:(kvh + 1) * 128], [KTb.t, Vn.t]
        A.run(4, NKT, qiTb, wi, qTb, kiTb, kv_tile, 128, cmB, penB, TOPK_S, bf, {NPG - 1: 0, NPG: 1}, None, None, out_sb=(aTs, b * 4))
    fw.dma("sp", D_["attnT"].ap[:, s0:s0 + nst].rearrange("(h d) t -> d h t", d=128), aTs.ap, reads=[aTs.t], writes=[D_["attnT"].t])


def build():
    cx = Ctx()
    nc, fw = cx.nc, cx.fw
    xT = cx.dram("xT", [D, NALL], F32, kind="ExternalInput")
    w_seq = cx.dram("w_seq", [D, 2048], F32, kind="ExternalInput")
    norm_mix = cx.dram("norm_mix", [128, KC], F32, kind="ExternalInput")
    kvkT = cx.dram("kvkT", [1152, NALL], F32, kind="ExternalOutput")
    qkvr = cx.dram("qkvr", [768, NALL], F32, kind="ExternalOutput")
    hT_all = cx.dram("hT_all", [D, NALL], BF16)
    KT = cx.dram("KT_s", [512, NALL], BF16)
    VT = cx.dram("VT_s", [512, NALL], BF16)
    KIT = cx.dram("KIT_s", [128, NALL], BF16)
    GAB = cx.dram("GAB_s", [128, NALL], F32)

    cst_d = cx.dram("cst", [128, 3, 128], F32, kind="ExternalInput")
    convw_d = cx.dram("convw", [128, 6, 4], F32, kind="ExternalInput")
    sconv_d = cx.dram("sconv", [768, NB * 3], F32, kind="ExternalInput")
    galog_d = cx.dram("galog", [128, 2], F32, kind="ExternalInput")
    gnorm_d = cx.dram("gnorm", [1, 128], F32, kind="ExternalInput")
    sstate_d = cx.dram("sstate", [NB, 2, 128, 128], F32, kind="ExternalInput")
    ssm_p_out = cx.dram("ssm_p", [2, 128, 128], F32, kind="ExternalOutput")
    ssm_s_out = cx.dram("ssm_s", [NB, 2, 128, 128], F32, kind="ExternalOutput")
    NOWN_ = NBLK * 128 + (NB // NCORES_L) * 4
    oT_d = cx.dram("oT_sh", [NCORES_L, 256, NOWN_], BF16, kind="ExternalOutput")
    xT_own = cx.dram("xT_own", [D, NOWN_], F32, kind="ExternalInput")
    w_in_d = cx.dram("w_in", [D, 7328], F32, kind="ExternalInput")
    relb_d = cx.dram("rel_bias", [32, 16], F32, kind="ExternalInput")
    TWp = STRIDE * 128
    Jp = 255 + STRIDE * 128
    Dd = {}
    Dd["cmp"] = cx.dram("cmp", [128, 2, TWp], F32, kind="ExternalInput")
    Dd["cms"] = cx.dram("cms", [128, 2, 128], F32, kind="ExternalInput")
    Dd["ohd_p"] = cx.dram("ohd_p", [32, Jp], F32, kind="ExternalInput")
    Dd["val_p"] = cx.dram("val_p", [16, Jp], F32, kind="ExternalInput")
    Dd["ohd_s"] = cx.dram("ohd_s", [32, 259], F32, kind="ExternalInput")
    Dd["val_s"] = cx.dram("val_s", [16, 259], F32, kind="ExternalInput")
    Dd["pcol"] = cx.dram("pcol", [128, 1], F32, kind="ExternalInput")
    NPOOLR = NPOOL * 128
    Dd["ck"] = cx.dram("cache_k", [NPOOLR, 512], F32, kind="ExternalInput")
    Dd["cv"] = cx.dram("cache_v", [NPOOLR, 512], F32, kind="ExternalInput")
    Dd["cki"] = cx.dram("cache_ki", [NPOOLR, 128], F32, kind="ExternalInput")
    Dd["pt"] = cx.dram("ptab", [1, (NB // NCORES_L) * (PAST // 128)], I32, kind="ExternalInput")
    Dd["fv_p"] = cx.dram("fv_p", [16, Jp], F32)
    Dd["fv_s"] = cx.dram("fv_s", [16, 259], F32)
    Dd["attnT"] = cx.dram("attnT", [2048, NOWN_], BF16, kind="ExternalOutput")
    hT_own = cx.dram("hT_own", [D, NOWN_], BF16)
    Dd["qT"] = cx.dram("qT_s", [2048, NOWN_], BF16)
    Dd["qiT"] = cx.dram("qiT_s", [4096, NOWN_], BF16)
    Dd["wiT"] = cx.dram("wiT_s", [32, NOWN_], F32)
    KTo = cx.dram("KTo_s", [512, NOWN_], BF16)
    VTo = cx.dram("VTo_s", [512, NOWN_], BF16)
    KITo = cx.dram("KITo_s", [128, NOWN_], BF16)
    Dd["Vsmp"] = cx.dram("Vsmp_s", [128, 512], BF16)
    Dd["Vtok"] = cx.dram("Vtok_s", [NALL, 512], BF16)
    cx.ones_b = cx.sb([128, 128], BF16, "ones_b")
    fw.op("dve", lambda e: e.memset(cx.ones_b.ap, 1.0), writes=[cx.ones_b.t])
    nw_mix = cx.sb([128, KC], F32, "nw_mix")
    fw.dma("sp", nw_mix.ap, norm_mix.ap, writes=[nw_mix.t])

    nsup = NALL // SUP
    hparts = [Trk() for _ in range((NALL + 127) // 128)]
    cx.begin()
    if not SKIP_H:
        rmsnorm_T(cx, xT.ap, [], hT_all.ap, lambda i: [hparts[i]], nw_mix, NALL, pfx="rnA")
    cx.end()
    if STOP == "H":
        fw.finish()
        return cx
    cx.begin()

    lin = Lin(cx)
    act = cx.sb([128, KC, SUP], BF16, "actA")
    ef = [cx.sb([128, SUP], F32, "ef") for _ in range(2)]
    eb = [cx.sb([128, SUP], BF16, "eb") for _ in range(2)]
    ei = [0]
    for s in range(nsup):
        t0 = s * SUP
        ptr = hparts[(t0 // 128):(t0 + SUP + 127) // 128]
        qq = cx.q()
        for c4 in range(0, KC, 8):
            fw.dma(qq, act.ap[:, c4:c4 + 8, :], hT_all.ap[c4 * 128:(c4 + 8) * 128, t0:t0 + SUP].rearrange("(c p) t -> p c t", p=128),
                   reads=ptr, writes=[act.t] if c4 == 0 else [], merge=[] if c4 == 0 else [act.t])
        for ft in range(NFT):
            def epi(ps, ft=ft, t0=t0):
                f = ef[ei[0] % 2]
                b = eb[ei[0] % 2]
                ei[0] += 1
                for s0 in range(0, SUP, 512):
                    n = min(512, SUP - s0)
                    fw.op("act", lambda e: e.copy(f.ap[:, s0:s0 + n], ps.ap[:, s0:s0 + n]), reads=[ps.t], writes=[f.t, ps.t] if DBG == 7 else [f.t])
                    if ft < 9 and DBG == 4:
                        fw.op("dve", lambda e: e.tensor_copy(b.ap[:, s0:s0 + n], f.ap[:, s0:s0 + n]), reads=[f.t], writes=[b.t])
                    elif ft < 9 and DBG == 6:
                        fw.op("act", lambda e: e.copy(b.ap[:, s0:s0 + n], ps.ap[:, s0:s0 + n]), reads=[ps.t], writes=[b.t])
                    elif ft < 9 and DBG == 7:
                        fw.op("dve", lambda e: e.tensor_copy(b.ap[:, s0:s0 + n], ps.ap[:, s0:s0 + n]), reads=[ps.t], writes=[b.t, ps.t])
                    elif ft < 9 and DBG != 1:
                        fw.op("dve", lambda e: e.tensor_copy(b.ap[:, s0:s0 + n], ps.ap[:, s0:s0 + n]), reads=[ps.t], writes=[b.t])
                if ft < 9:
                    fw.dma(cx.q(), kvkT.ap[ft * 128:(ft + 1) * 128, t0:t0 + SUP], f.ap, reads=[f.t])
                    dst = KT.ap[ft * 128:(ft + 1) * 128] if ft < 4 else (VT.ap[(ft - 4) * 128:(ft - 3) * 128] if ft < 8 else KIT.ap)
                    if DBG == 3:
                        fw.dma("sp", dst[:, t0:t0 + SUP], b.ap, reads=[b.t])
                    elif DBG not in (1, 2):
                        fw.dma(cx.q(), dst[:, t0:t0 + SUP], b.ap, reads=[b.t])
                elif ft < 15:
                    fw.dma(cx.q(), qkvr.ap[(ft - 9) * 128:(ft - 8) * 128, t0:t0 + SUP], f.ap, reads=[f.t])
                else:
                    fw.dma(cx.q(), GAB.ap[:, t0:t0 + SUP], f.ap, reads=[f.t])
            lin.run(act.ap, [act.t], KC, SUP, w_seq.ap, ft * 128, 128, epi)
    cx.end()
    if STOP == "A":
        fw.finish()
        return cx
    cx.begin()
    stage_gdn(cx, qkvr, GAB, cst_d, convw_d, sconv_d, galog_d, gnorm_d, sstate_d, ssm_p_out, ssm_s_out, oT_d, NP, NB)
    cx.end()
    if STOP == "G":
        fw.finish()
        return cx
    ident_f = cx.sb([128, 128], F32, "ident_fp")
    fw.dma("sp", ident_f.ap, cst_d.ap[:, 0, :], writes=[ident_f.t])
    ident_bf = cx.sb([128, 128], BF16, "ident_bf")
    fw.op("dve", lambda e: e.tensor_copy(ident_bf.ap, ident_f.ap), reads=[ident_f.t], writes=[ident_bf.t])
    rb_sb = cx.sb([32, 16], F32, "rb_sb")
    fw.dma("sp", rb_sb.ap, relb_d.ap, writes=[rb_sb.t])
    cx.begin()
    stage_vtok(cx, VT, Dd["Vtok"], ident_bf, NP)
    cx.end()
    cx.begin()
    rmsnorm_T(cx, xT_own.ap, [], hT_own.ap, lambda i: [hT_own.t], nw_mix, NOWN_, pfx="rnB")
    cx.end()
    cx.begin()
    lin = Lin(cx)
    act = cx.sb([128, KC, NOWN_], BF16, "actB")
    for c4 in range(0, KC, 8):
        fw.dma("sp", act.ap[:, c4:c4 + 8, :], hT_own.ap[c4 * 128:(c4 + 8) * 128, :].rearrange("(c p) t -> p c t", p=128),
               reads=[hT_own.t], writes=[act.t] if c4 == 0 else [], merge=[] if c4 == 0 else [act.t])
    eb = [cx.sb([128, NOWN_], BF16, "ebB") for _ in range(2)]
    ef = cx.sb([32, NOWN_], F32, "efB")
    ei = [0]
    jobs = [(O_Q, 2048, Dd["qT"]), (O_QI, 4096, Dd["qiT"]), (O_K, 512, KTo), (O_V, 512, VTo), (O_KI, 128, KITo)]
    for (o0, ncols, dstd) in jobs:
        for ft in range(ncols // 128):
            def epi(ps, ft=ft, dstd=dstd):
                b = eb[ei[0] % 2]
                ei[0] += 1
                for s0 in range(0, NOWN_, 512):
                    n = min(512, NOWN_ - s0)
                    fw.op("act", lambda e: e.copy(b.ap[:, s0:s0 + n], ps.ap[:, s0:s0 + n]), reads=[ps.t], writes=[b.t])
                fw.dma("sp", dstd.ap[ft * 128:(ft + 1) * 128, :], b.ap, reads=[b.t], writes=[dstd.t])
            lin.run(act.ap, [act.t], KC, NOWN_, w_in_d.ap, o0 + ft * 128, 128, epi)

    def epi_wi(ps):
        for s0 in range(0, NOWN_, 512):
            n = min(512, NOWN_ - s0)
            fw.op("act", lambda e: e.copy(ef.ap[:, s0:s0 + n], ps.ap[:32, s0:s0 + n]), reads=[ps.t], writes=[ef.t])
        fw.dma("sp", Dd["wiT"].ap, ef.ap, reads=[ef.t], writes=[Dd["wiT"].t])
    lin.run(act.ap, [act.t], KC, NOWN_, w_in_d.ap, O_WI, 32, epi_wi)
    cx.end()
    cx.begin()
    nst = (NB // NCORES_L) * 4
    vin = cx.sb([128, 4, nst], BF16, "vs_in")
    vout = cx.sb([128, 512], BF16, "vs_out")
    vps = cx.ps([128, 512], F32, "vs_ps")
    fw.dma("sp", vin.ap, VTo.ap[:, NBLK * 128:NBLK * 128 + nst].rearrange("(h d) t -> d h t", d=128), reads=[VTo.t], writes=[vin.t])
    for h in range(4):
        fw.op("pe", lambda e: e.matmul(vps.ap[:nst, h * 128:(h + 1) * 128], vin.ap[:, h, :], ident_bf.ap, start=True, stop=True), reads=[vin.t, ident_bf.t], writes=[vps.t])
    fw.op("act", lambda e: e.copy(vout.ap[:nst], vps.ap[:nst]), reads=[vps.t], writes=[vout.t])
    fw.dma("sp", Dd["Vsmp"].ap[:nst], vout.ap[:nst], reads=[vout.t], writes=[Dd["Vsmp"].t])
    cx.end()
    Dd["KT"], Dd["KIT"] = KT, KIT
    Dd["KTsmp"] = B(KTo.ap[:, NBLK * 128:NBLK * 128 + nst]); Dd["KTsmp"].t = KTo.t
    Dd["KITsmp"] = B(KITo.ap[:, NBLK * 128:NBLK * 128 + nst]); Dd["KITsmp"].t = KITo.t
    if STOP != "noP":
        cx.begin()
        stage_attn_prompt(cx, Dd, ident_bf, cx.ones_b, ident_f, rb_sb)
        cx.end()
    if STOP != "noS":
        cx.begin()
        stage_attn_sample(cx, Dd, ident_bf, cx.ones_b, ident_f, rb_sb)
        cx.end()
    fw.finish()
    return cx


def _t5_bucket(d):
    import math
    d = np.asarray(d)
    n = np.maximum(d, 0)
    large = 16 + (np.log(np.maximum(n, 1).astype(np.float32) / np.float32(16)) / np.float32(math.log(128 / 16)) * np.float32(16)).astype(np.int32)
    large = np.minimum(large, 31)
    return np.where(n < 16, n, large)


def host_consts(c):
    f32 = np.float32
    out = {}
    TW = STRIDE * 128
    t = np.arange(128)[:, None]
    u = np.arange(TW)[None, :]
    cm = (u <= c * 128 + t).astype(f32)
    out["cmp"] = np.ascontiguousarray(np.stack([cm, (cm - 1) * f32(1e30)], 1).astype(f32))
    s_ = np.arange(128)[None, :]
    cms = ((s_ <= t) & (s_ < 4)).astype(f32)
    out["cms"] = np.ascontiguousarray(np.stack([cms, (cms - 1) * f32(1e30)], 1).astype(f32))
    Jp = 255 + STRIDE * 128
    d = c * 128 + 255 - np.arange(Jp)
    oh = np.zeros((32, Jp), f32)
    bk = _t5_bucket(d)
    oh[bk, np.arange(Jp)] = 1.0
    oh[31, :] -= 1.0
    oh[:, d < 0] = 0.0
    out["ohd_p"] = oh
    out["val_p"] = np.ascontiguousarray(np.broadcast_to((d >= 0).astype(f32)[None, :], (16, Jp)))
    Js = 259
    d = 131 - np.arange(Js)
    oh = np.zeros((32, Js), f32)
    bk = _t5_bucket(d)
    oh[bk, np.arange(Js)] = 1.0
    oh[31, :] -= 1.0
    oh[:, d < 0] = 0.0
    out["ohd_s"] = oh
    out["val_s"] = np.ascontiguousarray(np.broadcast_to((d >= 0).astype(f32)[None, :], (16, Js)))
    out["pcol"] = np.arange(128, dtype=f32)[:, None]
    cst = np.zeros((128, 3, 128), f32)
    cst[:, 0, :] = np.eye(128, dtype=f32)
    pp = np.arange(128)[:, None]
    jj = np.arange(128)[None, :]
    cst[:, 1, :] = (jj < pp)
    cst[:, 2, :] = (jj >= pp)
    out["cst"] = cst
    return out


def build2():
    cx = Ctx()
    nc, fw = cx.nc, cx.fw
    NO = NBLK * 128 + (NB // NCORES_L) * 4
    xT_own = cx.dram("xT_own", [D, NO], F32, kind="ExternalInput")
    attnT = cx.dram("attnT_in", [2048, NO], BF16, kind="ExternalInput")
    oT = cx.dram("oT_all", [2048, NO], BF16, kind="ExternalInput")
    pT = cx.dram("pT", [256, NO], F32, kind="ExternalInput")
    w_in = cx.dram("w_in", [D, 10240], F32, kind="ExternalInput")
    w_au = cx.dram("w_attn_up", [2048, D], F32, kind="ExternalInput")
    w_gu = cx.dram("w_gdn_up", [2048, D], F32, kind="ExternalInput")
    w_out = cx.dram("w_out", [D, D], F32, kind="ExternalInput")
    w_gup = cx.dram("w_gate_up", [D, 2 * DFF], F32, kind="ExternalInput")
    w_dn = cx.dram("w_down", [DFF, D], F32, kind="ExternalInput")
    w_pg = cx.dram("w_ple_gate", [D, D], F32, kind="ExternalInput")
    w_pl = cx.dram("w_ple", [256, D], F32, kind="ExternalInput")
    norms = cx.dram("norms", [128, 4, KC], F32, kind="ExternalInput")
    yT = cx.dram("yT", [D, NO], F32, kind="ExternalOutput")
    hT = cx.dram("t_hT", [D, NO], BF16)
    OG = cx.dram("t_OG", [2048, NO], BF16)
    GA = cx.dram("t_GA", [D, NO], BF16)
    GB = cx.dram("t_GB", [D, NO], BF16)
    M1 = cx.dram("t_M1", [D, NO], F32)
    MG = cx.dram("t_MG", [D, NO], BF16)
    X1 = cx.dram("t_X1", [D, NO], F32)
    H1 = cx.dram("t_H1", [D, NO], BF16)
    MID = cx.dram("t_MID", [DFF, NO], BF16)
    X2 = cx.dram("t_X2", [D, NO], F32)
    H2 = cx.dram("t_H2", [D, NO], BF16)
    PG = cx.dram("t_PG", [D, NO], BF16)
    X3 = cx.dram("t_X3", [D, NO], F32)
    cx.ones_b = cx.sb([128, 128], BF16, "ones_b")
    fw.op("dve", lambda e: e.memset(cx.ones_b.ap, 1.0), writes=[cx.ones_b.t])
    nws = cx.sb([128, 4, KC], F32, "nws")
    fw.dma("sp", nws.ap, norms.ap, writes=[nws.t])

    def nw(i):
        b = B(nws.ap[:, i, :])
        b.t = nws.t
        return b

    def norm_stage(src, dst, i, out_dt=BF16):
        cx.begin()
        rmsnorm_T(cx, src.ap, [src.t], dst.ap, lambda k: [dst.t], nw(i), NO, out_dt=out_dt, pfx="rn2")
        cx.end()

    def load_act(act, src, nkc, tok0=0, ntok=None):
        ntok = ntok or NO
        for c4 in range(0, nkc, 8):
            ce = min(nkc, c4 + 8)
            fw.dma("sp", act.ap[:, c4:ce, :ntok], src.ap[c4 * 128:ce * 128, tok0:tok0 + ntok].rearrange("(c p) t -> p c t", p=128),
                   reads=[src.t], writes=[act.t] if c4 == 0 else [], merge=[] if c4 == 0 else [act.t])

    def segs(n):
        return [(s0, min(512, n - s0)) for s0 in range(0, n, 512)]

    norm_stage(xT_own, hT, 0)
    cx.begin()
    lin = Lin(cx)
    act = cx.sb([128, KC, NO], BF16, "act")
    load_act(act, hT, KC)
    tf = [cx.sb([128, NO], F32, "tf") for _ in range(2)]
    tb = [cx.sb([128, NO], BF16, "tb") for _ in range(2)]
    tl = [cx.sb([128, NO], BF16, "tl") for _ in range(2)]
    k_ = [0]
    for ft in range(16):
        def epi(ps, ft=ft):
            i = k_[0] % 2
            k_[0] += 1
            fw.dma("sp", tl[i].ap, oT.ap[ft * 128:(ft + 1) * 128, :], writes=[tl[i].t])
            for s0, n in segs(NO):
                fw.op("act", lambda e: e.activation(tf[i].ap[:, s0:s0 + n], ps.ap[:, s0:s0 + n], AF.Silu), reads=[ps.t], writes=[tf[i].t])
            fw.op("dve", lambda e: e.tensor_tensor(tb[i].ap, tf[i].ap, tl[i].ap, ALU.mult), reads=[tf[i].t, tl[i].t], writes=[tb[i].t])
            fw.dma("sp", OG.ap[ft * 128:(ft + 1) * 128, :], tb[i].ap, reads=[tb[i].t], writes=[OG.t])
        lin.run(act.ap, [act.t], KC, NO, w_in.ap, O_GZ - O_GZ + ft * 128, 128, epi)
    for (o0, dstd) in ((O_GTA, GA), (O_GTB, GB)):
        for ft in range(32):
            def epi(ps, ft=ft, dstd=dstd):
                i = k_[0] % 2
                k_[0] += 1
                for s0, n in segs(NO):
                    fw.op("act", lambda e: e.activation(tb[i].ap[:, s0:s0 + n], ps.ap[:, s0:s0 + n], AF.Sigmoid), reads=[ps.t], writes=[tb[i].t])
                fw.dma("sp", dstd.ap[ft * 128:(ft + 1) * 128, :], tb[i].ap, reads=[tb[i].t], writes=[dstd.t])
            lin.run(act.ap, [act.t], KC, NO, w_in.ap, o0 - O_GZ + ft * 128, 128, epi)
    cx.end()
    cx.begin()
    lin = Lin(cx)
    act = cx.sb([128, 16, NO], BF16, "act")
    load_act(act, attnT, 16)
    tf = [cx.sb([128, NO], F32, "tf") for _ in range(2)]
    tl = [cx.sb([128, NO], BF16, "tl") for _ in range(2)]
    for ft in range(32):
        def epi(ps, ft=ft):
            i = ft % 2
            fw.dma("sp", tl[i].ap, GA.ap[ft * 128:(ft + 1) * 128, :], reads=[GA.t], writes=[tl[i].t])
            for s0, n in segs(NO):
                fw.op("dve", lambda e: e.tensor_tensor(tf[i].ap[:, s0:s0 + n], ps.ap[:, s0:s0 + n], tl[i].ap[:, s0:s0 + n], ALU.mult), reads=[ps.t, tl[i].t], writes=[tf[i].t])
            fw.dma("sp", M1.ap[ft * 128:(ft + 1) * 128, :], tf[i].ap, reads=[tf[i].t], writes=[M1.t])
        lin.run(act.ap, [act.t], 16, NO, w_au.ap, ft * 128, 128, epi)
    cx.end()
    cx.begin()
    lin = Lin(cx)
    act = cx.sb([128, 16, NO], BF16, "act")
    load_act(act, OG, 16)
    tf = [cx.sb([128, NO], F32, "tf") for _ in range(2)]
    tm = [cx.sb([128, NO], F32, "tm") for _ in range(2)]
    tl = [cx.sb([128, NO], BF16, "tl") for _ in range(2)]
    tb = [cx.sb([128, NO], BF16, "tb") for _ in range(2)]
    for ft in range(32):
        def epi(ps, ft=ft):
            i = ft % 2
            fw.dma("sp", tl[i].ap, GB.ap[ft * 128:(ft + 1) * 128, :], reads=[GB.t], writes=[tl[i].t])
            fw.dma("sp", tm[i].ap, M1.ap[ft * 128:(ft + 1) * 128, :], reads=[M1.t], writes=[tm[i].t])
            for s0, n in segs(NO):
                fw.op("dve", lambda e: e.tensor_tensor(tf[i].ap[:, s0:s0 + n], ps.ap[:, s0:s0 + n], tl[i].ap[:, s0:s0 + n], ALU.mult), reads=[ps.t, tl[i].t], writes=[tf[i].t])
            fw.op("pool", lambda e: e.tensor_tensor(tb[i].ap, tf[i].ap, tm[i].ap, ALU.add), reads=[tf[i].t, tm[i].t], writes=[tb[i].t])
            fw.dma("sp", MG.ap[ft * 128:(ft + 1) * 128, :], tb[i].ap, reads=[tb[i].t], writes=[MG.t])
        lin.run(act.ap, [act.t], 16, NO, w_gu.ap, ft * 128, 128, epi)
    cx.end()

    def resid_lin(src_act, nkc, W, res_src, dst, ntok_split=1, gate_src=None):
        cx.begin()
        lin = Lin(cx)
        nt = NO // ntok_split
        act = cx.sb([128, nkc, nt], BF16, "act")
        tf = [cx.sb([128, nt], F32, "tf") for _ in range(2)]
        tm = [cx.sb([128, nt], F32, "tm") for _ in range(2)]
        tl = [cx.sb([128, nt], BF16, "tl") for _ in range(2)]
        for half in range(ntok_split):
            tok0 = half * nt
            load_act(act, src_act, nkc, tok0, nt)
            for ft in range(32):
                def epi(ps, ft=ft, tok0=tok0):
                    i = ft % 2
                    fw.dma("sp", tm[i].ap, res_src.ap[ft * 128:(ft + 1) * 128, tok0:tok0 + nt], reads=[res_src.t], writes=[tm[i].t])
                    if gate_src is not None:
                        fw.dma("sp", tl[i].ap, gate_src.ap[ft * 128:(ft + 1) * 128, tok0:tok0 + nt], reads=[gate_src.t], writes=[tl[i].t])
                    for s0, n in segs(nt):
                        if gate_src is not None:
                            fw.op("dve", lambda e: e.tensor_tensor(tf[i].ap[:, s0:s0 + n], ps.ap[:, s0:s0 + n], tl[i].ap[:, s0:s0 + n], ALU.mult), reads=[ps.t, tl[i].t], writes=[tf[i].t])
                            fw.op("pool", lambda e: e.tensor_tensor(tf[i].ap[:, s0:s0 + n], tf[i].ap[:, s0:s0 + n], tm[i].ap[:, s0:s0 + n], ALU.add), reads=[tf[i].t, tm[i].t], writes=[tf[i].t])
                        else:
                            fw.op("dve", lambda e: e.tensor_tensor(tf[i].ap[:, s0:s0 + n], ps.ap[:, s0:s0 + n], tm[i].ap[:, s0:s0 + n], ALU.add), reads=[ps.t, tm[i].t], writes=[tf[i].t])
                    fw.dma("sp", dst.ap[ft * 128:(ft + 1) * 128, tok0:tok0 + nt], tf[i].ap, reads=[tf[i].t], writes=[dst.t])
                lin.run(act.ap, [act.t], nkc, nt, W.ap, ft * 128, 128, epi)
        cx.end()

    resid_lin(MG, KC, w_out, xT_own, X1)
    norm_stage(X1, H1, 1)
    cx.begin()
    lin = Lin(cx)
    act = cx.sb([128, KC, NO], BF16, "act")
    load_act(act, H1, KC)
    tf = [cx.sb([128, NO], F32, "tf") for _ in range(2)]
    tb = [cx.sb([128, NO], BF16, "tb") for _ in range(2)]
    for ft in range(DFF // 128):
        i = ft % 2

        def epi_g(ps, i=i):
            for s0, n in segs(NO):
                fw.op("act", lambda e: e.activation(tf[i].ap[:, s0:s0 + n], ps.ap[:, s0:s0 + n], AF.Silu), reads=[ps.t], writes=[tf[i].t])

        def epi_u(ps, i=i, ft=ft):
            for s0, n in segs(NO):
                fw.op("dve", lambda e: e.tensor_tensor(tb[i].ap[:, s0:s0 + n], ps.ap[:, s0:s0 + n], tf[i].ap[:, s0:s0 + n], ALU.mult), reads=[ps.t, tf[i].t], writes=[tb[i].t])
            fw.dma("sp", MID.ap[ft * 128:(ft + 1) * 128, :], tb[i].ap, reads=[tb[i].t], writes=[MID.t])
        lin.run(act.ap, [act.t], KC, NO, w_gup.ap, ft * 128, 128, epi_g)
        lin.run(act.ap, [act.t], KC, NO, w_gup.ap, DFF + ft * 128, 128, epi_u)
    cx.end()
    resid_lin(MID, DFF // 128, w_dn, X1, X2, ntok_split=2)
    norm_stage(X2, H2, 2)
    cx.begin()
    lin = Lin(cx)
    act = cx.sb([128, KC, NO], BF16, "act")
    load_act(act, H2, KC)
    tb = [cx.sb([128, NO], BF16, "tb") for _ in range(2)]
    for ft in range(32):
        def epi(ps, ft=ft):
            i = ft % 2
            for s0, n in segs(NO):
                fw.op("act", lambda e: e.activation(tb[i].ap[:, s0:s0 + n], ps.ap[:, s0:s0 + n], AF.Sigmoid), reads=[ps.t], writes=[tb[i].t])
            fw.dma("sp", PG.ap[ft * 128:(ft + 1) * 128, :], tb[i].ap, reads=[tb[i].t], writes=[PG.t])
        lin.run(act.ap, [act.t], KC, NO, w_pg.ap, ft * 128, 128, epi)
    cx.end()
    PB = cx.dram("t_PB", [256, NO], BF16)
    cx.begin()
    pf = cx.sb([128, 2, NO], F32, "pf")
    pb = cx.sb([128, 2, NO], BF16, "pb")
    fw.dma("sp", pf.ap, pT.ap.rearrange("(c p) t -> p c t", p=128), writes=[pf.t])
    fw.op("dve", lambda e: e.tensor_copy(pb.ap, pf.ap), reads=[pf.t], writes=[pb.t])
    fw.dma("sp", PB.ap.rearrange("(c p) t -> p c t", p=128), pb.ap, reads=[pb.t], writes=[PB.t])
    cx.end()
    resid_lin(PB, 2, w_pl, X2, X3, gate_src=PG)
    norm_stage(X3, yT, 3, out_dt=F32)
    fw.finish()
    return cx


_BUILT = None


def _own_idx(c):
    idx = []
    for j in range(NBLK):
        g = c + NCORES_L * j
        idx.extend(range(g * 128, (g + 1) * 128))
    nst = (NB // NCORES_L) * 4
    idx.extend(range(NP + nst * c, NP + nst * (c + 1)))
    return np.array(idx)


def kernel(**inp):
    global _BUILT
    f32 = np.float32
    xp = np.asarray(inp["x_prompt"], f32)[0]
    xs = np.asarray(inp["x_sample"], f32).reshape(NS, D)
    xT = np.ascontiguousarray(np.concatenate([xp, xs], 0).T)
    W = np.asarray(inp["w_in"], f32)[0]
    nm = np.ascontiguousarray(np.asarray(inp["norm_mix"], f32)[0].reshape(KC, 128).T)
    conv_w = np.asarray(inp["conv_w"], f32)[0]
    state_conv = np.asarray(inp["state_conv"], f32)[0]
    state_ssm = np.asarray(inp["state_ssm"], f32)[0]
    a_log = np.asarray(inp["a_log"], f32)[0]
    dt_bias = np.asarray(inp["dt_bias"], f32)[0]
    gnorm = np.ascontiguousarray(np.asarray(inp["gdn_norm"], f32).reshape(1, 128))
    cst = np.zeros((128, 3, 128), f32)
    cst[:, 0, :] = np.eye(128, dtype=f32)
    pp = np.arange(128)[:, None]
    jj = np.arange(128)[None, :]
    cst[:, 1, :] = (jj < pp)
    cst[:, 2, :] = (jj >= pp)
    rel_bias = np.asarray(inp["rel_bias"], f32)
    ck = np.asarray(inp["cache_k"], f32)[0].reshape(-1, 512)
    cv = np.asarray(inp["cache_v"], f32)[0].reshape(-1, 512)
    cki = np.asarray(inp["cache_idx_k"], f32)[0].reshape(-1, 128)
    page_table = np.asarray(inp["page_table"], np.int32)
    xT_owns = []
    W1 = np.ascontiguousarray(W[:, :7328])
    W2 = np.ascontiguousarray(W[:, O_GZ:])
    in_maps = []
    for c in range(NCORES):
        cols = [W[:, O_K:O_K + 512], W[:, O_V:O_V + 512], W[:, O_KI:O_KI + 128]]
        for part in range(3):
            for h in (2 * c, 2 * c + 1):
                o = O_QKV + part * 2048 + h * 128
                cols.append(W[:, o:o + 128])
        gab = np.zeros((D, 128), f32)
        gab[:, 0] = W[:, O_GA + 2 * c]
        gab[:, 32] = W[:, O_GA + 2 * c + 1]
        gab[:, 64] = W[:, O_GB + 2 * c]
        gab[:, 96] = W[:, O_GB + 2 * c + 1]
        cols.append(gab)
        w_seq = np.ascontiguousarray(np.concatenate(cols, 1))
        chs = np.concatenate([np.arange(part * 2048 + h * 128, part * 2048 + (h + 1) * 128) for part in range(3) for h in (2 * c, 2 * c + 1)])
        convw = np.ascontiguousarray(conv_w[:, chs].T.reshape(6, 128, 4).transpose(1, 0, 2))
        sconv = np.ascontiguousarray(state_conv[:, :, chs].transpose(2, 0, 1).reshape(768, 128 * 3))
        galog = np.zeros((128, 2), f32)
        galog[0] = [a_log[2 * c], dt_bias[2 * c]]
        galog[32] = [a_log[2 * c + 1], dt_bias[2 * c + 1]]
        sstate = np.ascontiguousarray(state_ssm[:, 2 * c:2 * c + 2])
        m = {"xT": xT, "w_seq": w_seq, "norm_mix": nm, "convw": convw, "sconv": sconv, "galog": galog,
             "gnorm": gnorm, "sstate": sstate}
        m.update(host_consts(c))
        own = _own_idx(c)
        xT_own = np.ascontiguousarray(xT[:, own])
        xT_owns.append(xT_own)
        m.update({"xT_own": xT_own, "w_in": W1, "rel_bias": rel_bias, "cache_k": ck, "cache_v": cv, "cache_ki": cki,
                  "ptab": np.ascontiguousarray(page_table[16 * c:16 * (c + 1)].reshape(1, -1))})
        in_maps.append(m)
    if _BUILT is None:
        _BUILT = (build(), build2())
    cx, cx2 = _BUILT
    res = run_bass_kernel_spmd(cx.nc, in_maps, core_ids=list(range(NCORES)))
    R = res.results
    del in_maps
    pall = np.concatenate([np.asarray(inp["p_prompt"], f32)[0, 0], np.asarray(inp["p_sample"], f32)[0].reshape(NS, 256)], 0)
    norms = np.stack([np.asarray(inp[k], f32).reshape(D) for k in ("norm_mix", "norm_ffn", "norm_ple", "norm_final")], 0)
    norms = np.ascontiguousarray(norms.reshape(4, KC, 128).transpose(2, 0, 1))
    in2 = []
    for c in range(NCORES):
        o_all = np.ascontiguousarray(np.concatenate([R[i]["oT_sh"][c] for i in range(NCORES)], 0))
        in2.append({"xT_own": xT_owns[c], "attnT_in": R[c]["attnT"], "oT_all": o_all,
                    "pT": np.ascontiguousarray(pall[_own_idx(c)].T), "w_in": W2,
                    "w_attn_up": np.asarray(inp["w_attn_up"], f32)[0], "w_gdn_up": np.asarray(inp["w_gdn_up"], f32)[0],
                    "w_out": np.asarray(inp["w_out"], f32)[0], "w_gate_up": np.asarray(inp["w_gate_up"], f32)[0],
                    "w_down": np.asarray(inp["w_down"], f32)[0], "w_ple_gate": np.asarray(inp["w_ple_gate"], f32)[0],
                    "w_ple": np.asarray(inp["w_ple"], f32)[0], "norms": norms})
    res2 = run_bass_kernel_spmd(cx2.nc, in2, core_ids=list(range(NCORES)))
    yall = np.zeros((NALL, D), f32)
    for c in range(NCORES):
        yall[_own_idx(c)] = res2.results[c]["yT"].T
    kvk = R[0]["kvkT"]
    kT, vT, kiT = kvk[:512], kvk[512:1024], kvk[1024:1152]
    k_prompt = np.ascontiguousarray(kT[:, :NP].T).reshape(1, 1, NP, 4, 128)
    v_prompt = np.ascontiguousarray(vT[:, :NP].T).reshape(1, 1, NP, 4, 128)
    ki_prompt = np.ascontiguousarray(kiT[:, :NP].T).reshape(1, 1, NP, 128)
    k_sample = np.ascontiguousarray(kT[:, NP:].T).reshape(1, 128, 4, 4, 128)
    v_sample = np.ascontiguousarray(vT[:, NP:].T).reshape(1, 128, 4, 4, 128)
    ki_sample = np.ascontiguousarray(kiT[:, NP:].T).reshape(1, 128, 4, 128)
    conv_p = np.zeros((1, 1, 3, 6144), f32)
    conv_s = np.zeros((1, 128, 3, 6144), f32)
    for c in range(NCORES):
        q = R[c]["qkvr"]
        for part in range(3):
            for hh in range(2):
                h = 2 * c + hh
                rows = q[(part * 2 + hh) * 128:(part * 2 + hh + 1) * 128]
                ch0 = part * 2048 + h * 128
                conv_p[0, 0, :, ch0:ch0 + 128] = rows[:, NP - 3:NP].T
                conv_s[0, :, :, ch0:ch0 + 128] = rows[:, NP:].T.reshape(128, 4, 128)[:, 1:4, :]
    y_prompt = np.ascontiguousarray(yall[:NP]).reshape(1, NP, D)
    y_sample = np.ascontiguousarray(yall[NP:]).reshape(128, 4, D)
    ssm_p = np.zeros((1, 1, 16, 128, 128), f32)
    ssm_s = np.zeros((1, 128, 16, 128, 128), f32)
    for c in range(NCORES):
        ssm_p[0, 0, 2 * c:2 * c + 2] = R[c]["ssm_p"]
        ssm_s[0, :, 2 * c:2 * c + 2] = R[c]["ssm_s"]
    return (y_prompt, y_sample, k_prompt, v_prompt, ki_prompt, conv_p, ssm_p, k_sample, v_sample, ki_sample, conv_s, ssm_s)
```

```python
from contextlib import ExitStack
import numpy as np
import concourse.bass as bass
import concourse.mybir as mybir
from concourse.bass_utils import run_bass_kernel_spmd

F32 = mybir.dt.float32
BF16 = mybir.dt.bfloat16
I32 = mybir.dt.int32
AF = mybir.ActivationFunctionType
ALU = mybir.AluOpType
AX = mybir.AxisListType

EPOCH = 30000
NCORES = 8
D = 4096
KC = 32
NP = 8192
NS = 512
NALL = NP + NS
NB = 128
STRIDE = 8
NBLK = 8
NCORES_L = 8
PAST = 2048
TOPK_P = 256
TOPK_S = 256
NPOOL = 2560
FUSED = True
NOWN = 1088
SUP = 1088
EPS = 1e-6
O_Q, O_K, O_V, O_QI, O_KI, O_WI, O_QKV, O_GA, O_GB, O_GZ, O_GTA, O_GTB = (
    0, 2048, 2560, 3072, 7168, 7296, 7328, 13472, 13488, 13504, 15552, 19648)
DFF = 11008
STOP = None
NOSTACK = False
NFT = 16
SKIP_H = False
DBG = 0


class Trk:
    __slots__ = ("w", "r", "x")

    def __init__(self, x=False):
        self.w = {}
        self.r = {}
        self.x = x


class Eng:
    def __init__(self, fw, name, e):
        self.fw, self.name, self.e = fw, name, e
        self.seq = 0
        self.sems = []
        self.waited = {}

    def sem_for(self, seq):
        ep = (seq - 1) // EPOCH
        while len(self.sems) <= ep:
            self.sems.append(self.fw.nc.alloc_semaphore(f"c_{self.name}_{len(self.sems)}"))
        return self.sems[ep], (seq - 1) % EPOCH + 1


class FW:
    def __init__(self, nc, nslots=6):
        self.nc = nc
        self.engs = {n: Eng(self, n, e) for n, e in (("pe", nc.tensor), ("act", nc.scalar), ("dve", nc.vector),
                                                      ("pool", nc.gpsimd), ("sp", nc.sync))}
        self.slots = {}
        self.slot_epoch = 0
        for q in ("sp", "act", "pool"):
            self.slots[q] = [dict(sem=nc.alloc_semaphore(f"d_{q}_{i}"), k=0, name=f"dma_{q}_{i}", base=0) for i in range(nslots)]
        self.slot_rr = {q: 0 for q in self.slots}
        self.slot_by_name = {s["name"]: s for q in self.slots for s in self.slots[q]}
        self.n_wait = 0
        self.n_ins = 0

    def _wait(self, eng, res, seq):
        if eng.waited.get(res, 0) >= seq:
            return
        eng.waited[res] = seq
        self.n_wait += 1
        if res in self.engs:
            sem, val = self.engs[res].sem_for(seq)
            eng.e.wait_ge(sem, val)
        else:
            s = self.slot_by_name[res]
            eng.e.wait_ge(s["sem"], seq * 16)

    def _deps(self, reads, writes):
        deps = {}
        for t in reads:
            for r, s in t.w.items():
                if deps.get(r, 0) < s:
                    deps[r] = s
        for t in writes:
            for d in (t.w, t.r):
                for r, s in d.items():
                    if deps.get(r, 0) < s:
                        deps[r] = s
        return deps

    def op(self, en, fn, reads=(), writes=(), pe_acc=False):
        eng = self.engs[en]
        xr = [t for t in reads if t.x]
        if xr:
            writes = list(writes) + [t for t in xr if t not in writes]
            reads = [t for t in reads if not t.x]
        deps = self._deps(reads, writes)
        for r, s in deps.items():
            if pe_acc and r == "pe":
                continue
            self._wait(eng, r, s)
        ins = fn(eng.e)
        eng.seq += 1
        sem, _ = eng.sem_for(eng.seq)
        ins.then_inc(sem, 1)
        self.n_ins += 1
        for t in reads:
            t.r[en] = eng.seq
        for t in writes:
            t.w = {en: eng.seq}
            t.r = {}
        return ins

    def dma(self, q, out, in_, reads=(), writes=(), merge=(), **kw):
        eng = self.engs[q]
        deps = self._deps(reads, writes)
        for r, s in deps.items():
            self._wait(eng, r, s)
        i = self.slot_rr[q]
        self.slot_rr[q] = (i + 1) % len(self.slots[q])
        sl = self.slots[q][i]
        if sl["k"] > 0:
            self._wait(eng, sl["name"], sl["k"])
        assert (sl["k"] + 1) * 16 < 65000, "dma slot semaphore overflow"
        ins = eng.e.dma_start(out=out, in_=in_, **kw)
        sl["k"] += 1
        ins.then_inc(sl["sem"], 16)
        self.n_ins += 1
        for t in reads:
            t.r[sl["name"]] = sl["k"]
        for t in writes:
            t.w = {sl["name"]: sl["k"]}
            t.r = {}
        for t in merge:
            t.w[sl["name"]] = sl["k"]
        return ins

    def barrier(self):
        for en, eng in self.engs.items():
            for q in self.slots:
                for sl in self.slots[q]:
                    if sl["k"] > 0:
                        self._wait(eng, sl["name"], sl["k"])
            for en2, e2 in self.engs.items():
                if e2.seq > 0 and en2 != en:
                    self._wait(eng, en2, e2.seq)

    def finish(self):
        eng = self.engs["sp"]
        for q in self.slots:
            for sl in self.slots[q]:
                if sl["k"] > 0:
                    self._wait(eng, sl["name"], sl["k"])
        for en, e in self.engs.items():
            if e.seq > 0 and en != "sp":
                self._wait(eng, en, e.seq)


class B:
    def __init__(self, ap):
        self.ap = ap
        self.t = Trk()


class Ctx:
    def __init__(self, num_devices=None):
        self.nc = bass.Bass("TRN2", target_bir_lowering=False, num_devices=num_devices)
        self.fw = FW(self.nc)
        self.uid = 0
        self.dq = 0
        self.stack = None

    def begin(self):
        if NOSTACK:
            return
        self.stack = ExitStack()

    def end(self):
        self.fw.barrier()
        if self.stack is not None:
            self.stack.close()
        self.stack = None

    def sb(self, shape, dt, name=None):
        self.uid += 1
        nm = f"{name or 'sb'}_{self.uid}"
        if self.stack is None:
            return B(self.nc.alloc_sbuf_tensor(nm, list(shape), dt).ap())
        return B(self.stack.enter_context(self.nc.sbuf_tensor(nm, list(shape), dt)).ap())

    def ps(self, shape, dt=F32, name=None):
        self.uid += 1
        nm = f"{name or 'ps'}_{self.uid}"
        if self.stack is None:
            b = B(self.nc.alloc_psum_tensor(nm, list(shape), dt).ap())
        else:
            b = B(self.stack.enter_context(self.nc.psum_tensor(nm, list(shape), dt)).ap())
        b.t.x = True
        return b

    def dram(self, name, shape, dt, kind="Internal", nparts=1):
        h = self.nc.dram_tensor(name, list(shape), dt, kind=kind)
        b = B(h.ap())
        b.parts = [Trk() for _ in range(nparts)]
        return b

    def q(self):
        return "sp"


def rmsnorm_T(cx, src, src_trks, dst, dst_trk_fn, nw, ntok, tt=128, out_dt=BF16, pfx="rn"):
    fw = cx.fw
    xs = [cx.sb([128, KC, tt], F32, pfx + "x") for _ in range(2)]
    sq = cx.sb([128, KC, tt], F32, pfx + "sq")
    ob = [cx.sb([128, KC, tt], out_dt, pfx + "o") for _ in range(2)]
    part = cx.sb([128, tt], F32, pfx + "part")
    rstd = cx.sb([128, tt], F32, pfx + "rstd")
    phi = cx.sb([128, tt], BF16, pfx + "phi")
    plo = cx.sb([128, tt], BF16, pfx + "plo")
    tot = cx.ps([128, tt], F32, pfx + "tot")
    nt = (ntok + tt - 1) // tt
    for i in range(nt):
        t0 = i * tt
        n = min(tt, ntok - t0)
        x = xs[i % 2]
        o = ob[i % 2]
        fw.dma(cx.q(), x.ap[:, :, :n], src[:, t0:t0 + n].rearrange("(c p) t -> p c t", p=128), reads=src_trks, writes=[x.t])
        fw.op("act", lambda e: e.activation(sq.ap[:, :, :n], x.ap[:, :, :n], AF.Square), reads=[x.t], writes=[sq.t])
        fw.op("dve", lambda e: e.tensor_reduce(part.ap[:, :n], sq.ap[:, :, :n].rearrange("p c t -> p t c"), AX.X, ALU.add),
              reads=[sq.t], writes=[part.t])
        fw.op("dve", lambda e: e.tensor_copy(phi.ap[:, :n], part.ap[:, :n]), reads=[part.t], writes=[phi.t])
        fw.op("dve", lambda e: e.tensor_tensor(plo.ap[:, :n], part.ap[:, :n], phi.ap[:, :n], ALU.subtract), reads=[part.t, phi.t], writes=[plo.t])
        fw.op("pe", lambda e: e.matmul(tot.ap[:, :n], cx.ones_b.ap, phi.ap[:, :n], start=True, stop=False),
              reads=[phi.t, cx.ones_b.t], writes=[tot.t])
        fw.op("pe", lambda e: e.matmul(tot.ap[:, :n], cx.ones_b.ap, plo.ap[:, :n], start=False, stop=True),
              reads=[plo.t, cx.ones_b.t], writes=[tot.t], pe_acc=True)
        fw.op("dve", lambda e: e.tensor_scalar(rstd.ap[:, :n], tot.ap[:, :n], 1.0 / D, EPS, ALU.mult, ALU.add),
              reads=[tot.t], writes=[rstd.t])
        fw.op("act", lambda e: e.activation(rstd.ap[:, :n], rstd.ap[:, :n], AF.Sqrt), reads=[rstd.t], writes=[rstd.t])
        fw.op("dve", lambda e: e.reciprocal(rstd.ap[:, :n], rstd.ap[:, :n]), reads=[rstd.t], writes=[rstd.t])
        fw.op("dve", lambda e: e.tensor_tensor(x.ap[:, :, :n], x.ap[:, :, :n],
                                               rstd.ap[:, :n].unsqueeze(1).to_broadcast([128, KC, n]), ALU.mult),
              reads=[x.t, rstd.t], writes=[x.t])
        fw.op("pool", lambda e: e.tensor_tensor(o.ap[:, :, :n], x.ap[:, :, :n],
                                                nw.ap.unsqueeze(2).to_broadcast([128, KC, n]), ALU.mult),
              reads=[x.t, nw.t], writes=[o.t])
        fw.dma(cx.q(), dst[:, t0:t0 + n].rearrange("(c p) t -> p c t", p=128), o.ap[:, :, :n], reads=[o.t], writes=dst_trk_fn(i))


class Lin:
    def __init__(self, cx, ntok_max=SUP, kc_max=KC):
        self.cx = cx
        self.wst = [cx.sb([128, KC, 128], F32, "wst") for _ in range(2)]
        self.wbf = [cx.sb([128, KC, 128], BF16, "wbf") for _ in range(3)]
        self.psb = [cx.ps([128, 1536], F32, "lps") for _ in range(2)]
        self.iw = 0
        self.ip = 0

    def load_w(self, W, r0, nk, c0, ncol, wtrks=()):
        cx, fw = self.cx, self.cx.fw
        st = self.wst[self.iw % 2]
        wb = self.wbf[self.iw % 3]
        self.iw += 1
        h = max(1, nk // 2)
        qq = cx.q()
        for (a, b) in ((0, h), (h, nk)):
            if b > a:
                fw.dma(qq, st.ap[:, a:b, :ncol], W[r0 + a * 128:r0 + b * 128, c0:c0 + ncol].rearrange("(c p) f -> p c f", p=128),
                       reads=list(wtrks), writes=[st.t] if a == 0 else [], merge=[] if a == 0 else [st.t])
        fw.op("pool", lambda e: e.tensor_copy(wb.ap[:, :nk, :ncol], st.ap[:, :nk, :ncol]), reads=[st.t], writes=[wb.t])
        return wb

    def run(self, act, act_trks, nkc, ntok, W, c0, ncol, epilogue, wtrks=(), tok0=0):
        cx, fw = self.cx, self.cx.fw
        ps = self.psb[self.ip % 2]
        self.ip += 1
        ngrp = (nkc + KC - 1) // KC
        for g in range(ngrp):
            k0 = g * KC
            nk = min(KC, nkc - k0)
            wb = self.load_w(W, k0 * 128, nk, c0, ncol, wtrks)
            for kc in range(nk):
                for s0 in range(0, ntok, 512):
                    n = min(512, ntok - s0)
                    first = (k0 + kc == 0)
                    last = (k0 + kc == nkc - 1)
                    fw.op("pe", lambda e: e.matmul(ps.ap[:ncol, s0:s0 + n], wb.ap[:, kc, :ncol], act[:, k0 + kc, tok0 + s0:tok0 + s0 + n],
                                                   start=first, stop=last),
                          reads=[wb.t] + list(act_trks), writes=[ps.t], pe_acc=not first)
        epilogue(ps)


def stage_gdn(cx, qkvr, GAB, cst_d, convw_d, sconv_d, galog_d, gnorm_d, sstate_d, ssm_p_out, ssm_s_out, oT_d, n_prompt, n_batch):
    fw = cx.fw
    SEG = 1024
    ident = cx.sb([128, 128], F32, "ident")
    Lm = cx.sb([64, 64], F32, "Lm")
    Um = cx.sb([64, 64], F32, "Um")
    fw.dma("sp", ident.ap, cst_d.ap[:, 0, :], writes=[ident.t])
    fw.dma("sp", Lm.ap, cst_d.ap[0:64, 1, 0:64], writes=[Lm.t])
    fw.dma("sp", Um.ap, cst_d.ap[0:64, 2, 0:64], writes=[Um.t])
    ones_f = cx.sb([128, 128], F32, "ones_f")
    fw.op("dve", lambda e: e.memset(ones_f.ap, 1.0), writes=[ones_f.t])
    convw = cx.sb([128, 6, 4], F32, "convw")
    fw.dma("sp", convw.ap, convw_d.ap, writes=[convw.t])
    galog = cx.sb([128, 2], F32, "galog")
    fw.dma("sp", galog.ap, galog_d.ap, writes=[galog.t])
    nega = cx.sb([128, 1], F32, "nega")
    fw.op("act", lambda e: e.activation(nega.ap, galog.ap[:, 0:1], AF.Exp), reads=[galog.t], writes=[nega.t])
    fw.op("dve", lambda e: e.tensor_scalar(nega.ap, nega.ap, -1.0, None, ALU.mult), reads=[nega.t], writes=[nega.t])
    gnb = cx.sb([64, 128], F32, "gnb")
    fw.dma("sp", gnb.ap, bass.AP(gnorm_d.ap.tensor, 0, [[0, 64], [1, 128]]), writes=[gnb.t])
    S = [cx.sb([128, 128], F32, "S") for _ in range(2)]
    xs = cx.sb([128, 6, SEG + 3], F32, "gxs")
    ys = cx.sb([128, 6, SEG], F32, "gys")
    sq = cx.sb([128, SEG], F32, "gsq")
    rn = cx.sb([128, SEG], F32, "grn")
    R = cx.sb([128, SEG], F32, "gR")
    Tg = [cx.sb([128, SEG], F32, "gTg") for _ in range(2)]
    Tb = cx.sb([128, SEG], F32, "gTb")
    Tgl = cx.sb([128, SEG], F32, "gTgl")
    Te2 = cx.sb([128, SEG], F32, "gTe2")
    GL = [cx.sb([128, 256], F32, "gGL") for _ in range(2)]
    oT_seg = cx.sb([128, 2, SEG], BF16, "goT")
    tcs = cx.sb([128, 6, 384], F32, "gtcs")
    tqs = cx.sb([128, 6, 512], F32, "gtqs")
    cols = cx.sb([64, 3, 128], F32, "gcols")
    bank = [cx.ps([128, 512], F32, f"gb{i}") for i in range(7)]

    def sbs(shape, name):
        return [cx.sb(shape, F32, name) for _ in range(2)]
    ecol, nbc, bec = sbs([64, 1], "ecol"), sbs([64, 1], "nbc"), sbs([64, 1], "bec")
    vb, kbe, kt, u_, vnew, o_ = (sbs([64, 128], n) for n in ("vb", "kbe", "kt", "u", "vnew", "o"))
    xm, xp, DT, Ds, N_, aT, Q_ = (sbs([64, 64], n) for n in ("xm", "xp", "DT", "Ds", "N", "aT", "Q"))
    Xa, Xb, XTa, XTb = (sbs([64, 64], n) for n in ("Xa", "Xb", "XTa", "XTb"))
    wT = sbs([128, 64], "wT")
    ss = sbs([64, 1], "ss")
    junk = sbs([64, 128], "junk")

    def mm(out, lhsT, rhs, reads, pb):
        fw.op("pe", lambda e: e.matmul(out, lhsT, rhs, start=True, stop=True), reads=reads, writes=[pb.t])

    def preprocess(t0, nch, C, sample_b0=None):
        n = nch * C
        if sample_b0 is None:
            if t0 == 0:
                fw.op("dve", lambda e: e.memset(xs.ap[:, :, 0:3], 0.0), writes=[xs.t])
                fw.dma("sp", xs.ap[:, :, 3:3 + n], qkvr.ap[:, 0:n].rearrange("(i p) t -> p i t", p=128), reads=[qkvr.t], merge=[xs.t])
            else:
                fw.dma("sp", xs.ap[:, :, 0:3 + n], qkvr.ap[:, t0 - 3:t0 + n].rearrange("(i p) t -> p i t", p=128), reads=[qkvr.t], writes=[xs.t])
            xv = lambda i, j: xs.ap[:, i, j:j + n]
            yv = lambda i: ys.ap[:, i, :n]
        else:
            xs4 = xs.ap[:, :, 0:nch * 7].rearrange("p i (b r) -> p i b r", r=7)
            fw.dma("sp", tcs.ap[:, :, :nch * 3], sconv_d.ap[:, sample_b0 * 3:(sample_b0 + nch) * 3].rearrange("(i p) t -> p i t", p=128), writes=[tcs.t])
            fw.dma("sp", tqs.ap[:, :, :n], qkvr.ap[:, t0:t0 + n].rearrange("(i p) t -> p i t", p=128), reads=[qkvr.t], writes=[tqs.t])
            fw.op("pool", lambda e: e.tensor_copy(xs4[:, :, :, 0:3], tcs.ap[:, :, :nch * 3].rearrange("p i (b r) -> p i b r", r=3)), reads=[tcs.t], writes=[xs.t])
            fw.op("pool", lambda e: e.tensor_copy(xs4[:, :, :, 3:7], tqs.ap[:, :, :n].rearrange("p i (b r) -> p i b r", r=4)), reads=[tqs.t, xs.t], writes=[xs.t])
            xv = lambda i, j: xs4[:, i, :, j:j + 4]
            yv = lambda i: ys.ap[:, i, :n].rearrange("p (b r) -> p b r", r=4)
        for i in range(6):
            en = "dve"
            fw.op(en, lambda e: e.tensor_scalar(yv(i), xv(i, 0), convw.ap[:, i, 0:1], None, ALU.mult), reads=[xs.t, convw.t], writes=[ys.t])
            for j in range(1, 4):
                fw.op(en, lambda e: e.scalar_tensor_tensor(yv(i), xv(i, j), convw.ap[:, i, j:j + 1], yv(i), ALU.mult, ALU.add),
                      reads=[xs.t, convw.t, ys.t], writes=[ys.t])
        fw.op("act", lambda e: e.activation(ys.ap[:, :, :n], ys.ap[:, :, :n], AF.Silu), reads=[ys.t], writes=[ys.t])
        for i in range(4):
            fw.op("act", lambda e: e.activation(sq.ap[:, :n], ys.ap[:, i, :n], AF.Square), reads=[ys.t], writes=[sq.t])
            for s0 in range(0, n, 512):
                m = min(512, n - s0)
                pb = bank[s0 // 512]
                mm(pb.ap[:, :m], ones_f.ap, sq.ap[:, s0:s0 + m], [ones_f.t, sq.t], pb)
                fw.op("dve", lambda e: e.tensor_scalar(rn.ap[:, s0:s0 + m], pb.ap[:, :m], 1e-6, None, ALU.add), reads=[pb.t], writes=[rn.t])
            fw.op("act", lambda e: e.activation(rn.ap[:, :n], rn.ap[:, :n], AF.Sqrt), reads=[rn.t], writes=[rn.t])
            fw.op("dve", lambda e: e.reciprocal(rn.ap[:, :n], rn.ap[:, :n]), reads=[rn.t], writes=[rn.t])
            sc = 128.0 ** -0.5 if i < 2 else 1.0
            fw.op("dve", lambda e: e.scalar_tensor_tensor(ys.ap[:, i, :n], ys.ap[:, i, :n], sc, rn.ap[:, :n], ALU.mult, ALU.mult),
                  reads=[ys.t, rn.t], writes=[ys.t])
        fw.dma("sp", R.ap[:, :n], GAB.ap[:, t0:t0 + n], reads=[GAB.t], writes=[R.t])
        a, b = Tg
        fw.op("act", lambda e: e.activation(a.ap[:, :n], R.ap[:, :n], AF.Exp, bias=galog.ap[:, 1:2]), reads=[R.t, galog.t], writes=[a.t])
        fw.op("act", lambda e: e.activation(a.ap[:, :n], a.ap[:, :n], AF.Ln, bias=1.0), reads=[a.t], writes=[a.t])
        fw.op("dve", lambda e: e.tensor_scalar(a.ap[:, :n], a.ap[:, :n], nega.ap[:, 0:1], None, ALU.mult), reads=[a.t, nega.t], writes=[a.t])
        fw.op("act", lambda e: e.activation(Tb.ap[:, :n], R.ap[:, :n], AF.Sigmoid), reads=[R.t], writes=[Tb.t])
        sft = 1
        while sft < C:
            av = a.ap[:, :n].rearrange("p (c k) -> p c k", k=C)
            bv = b.ap[:, :n].rearrange("p (c k) -> p c k", k=C)
            fw.op("pool", lambda e: e.tensor_copy(bv[:, :, :sft], av[:, :, :sft]), reads=[a.t], writes=[b.t])
            fw.op("dve", lambda e: e.tensor_tensor(bv[:, :, sft:], av[:, :, sft:], av[:, :, :C - sft], ALU.add), reads=[a.t, b.t], writes=[b.t])
            a, b = b, a
            sft *= 2
        gc = a
        gcv = gc.ap[:, :n].rearrange("p (c k) -> p c k", k=C)
        fw.op("dve", lambda e: e.tensor_copy(Tgl.ap[:, :n].rearrange("p (c k) -> p c k", k=C), gcv[:, :, C - 1:C].to_broadcast([128, nch, C])),
              reads=[gc.t], writes=[Tgl.t])
        fw.op("dve", lambda e: e.tensor_tensor(Te2.ap[:, :n], Tgl.ap[:, :n], gc.ap[:, :n], ALU.subtract), reads=[Tgl.t, gc.t], writes=[Te2.t])
        fw.op("act", lambda e: e.activation(Te2.ap[:, :n], Te2.ap[:, :n], AF.Exp), reads=[Te2.t], writes=[Te2.t])
        for hh in range(2):
            p0 = 32 * hh
            pb = bank[2]
            mm(pb.ap[:, :nch], ones_f.ap[p0:p0 + 1, :], gcv[p0:p0 + 1, :, C - 1], [ones_f.t, gc.t], pb)
            fw.op("act", lambda e: e.activation(GL[hh].ap[:, :nch], pb.ap[:, :nch], AF.Exp), reads=[pb.t], writes=[GL[hh].t])
        return gc

    def chunk(gc, ci, C, niter, t_out0, state_io=None):
        c0 = ci * C
        pb = bank[6]
        for k, src in enumerate((gc, Tb, Te2)):
            mm(pb.ap[:C, k * 128:(k + 1) * 128], src.ap[:, c0:c0 + C], ident.ap, [src.t, ident.t], pb)
        fw.op("act", lambda e: e.copy(cols.ap[:C].rearrange("p k j -> p (k j)"), pb.ap[:C, 0:384]), reads=[pb.t], writes=[cols.t])
        def head(hh):
            p0 = 32 * hh
            bA, bB, bC = bank[3 * hh], bank[3 * hh + 1], bank[3 * hh + 2]
            qT, kT, vT = (ys.ap[:, 2 * k + hh, c0:c0 + C] for k in range(3))
            gcc, bc, e2c = cols.ap[:C, 0, p0:p0 + 1], cols.ap[:C, 1, 64 + p0:65 + p0], cols.ap[:C, 2, p0:p0 + 1]
            Sh = S[hh]
            if state_io is not None:
                fw.dma("sp", Sh.ap, state_io[0](hh), writes=[Sh.t])
            b0 = bA
            mm(b0.ap[:C, 0:128], kT, ident.ap, [ys.t, ident.t], b0)
            mm(b0.ap[:C, 128:256], vT, ident.ap, [ys.t, ident.t], b0)
            fw.op("act", lambda e: e.activation(ecol[hh].ap[:C], gcc, AF.Exp), reads=[cols.t], writes=[ecol[hh].t])
            fw.op("dve", lambda e: e.tensor_scalar(nbc[hh].ap[:C], bc, -1.0, None, ALU.mult), reads=[cols.t], writes=[nbc[hh].t])
            fw.op("dve", lambda e: e.tensor_tensor(bec[hh].ap[:C], bc, ecol[hh].ap[:C], ALU.mult), reads=[cols.t, ecol[hh].t], writes=[bec[hh].t])
            fw.op("act", lambda e: e.activation(vb[hh].ap[:C], b0.ap[:C, 128:256], AF.Copy, scale=bc), reads=[b0.t, cols.t], writes=[vb[hh].t])
            fw.op("dve", lambda e: e.tensor_scalar(kbe[hh].ap[:C], b0.ap[:C, 0:128], bec[hh].ap[:C], None, ALU.mult), reads=[b0.t, bec[hh].t], writes=[kbe[hh].t])
            fw.op("dve", lambda e: e.tensor_scalar(kt[hh].ap[:C], b0.ap[:C, 0:128], e2c, None, ALU.mult), reads=[b0.t, cols.t], writes=[kt[hh].t])
            yield
            b2 = bA
            mm(b2.ap[:C, 256:256 + C], ones_f.ap[p0:p0 + 1, :C], gc.ap[p0:p0 + 1, c0:c0 + C], [ones_f.t, gc.t], b2)
            fw.op("dve", lambda e: e.tensor_scalar(xm[hh].ap[:C, :C], b2.ap[:C, 256:256 + C], gcc, 0.0, ALU.subtract, ALU.min), reads=[b2.t, cols.t], writes=[xm[hh].t])
            fw.op("dve", lambda e: e.tensor_scalar(xp[hh].ap[:C, :C], b2.ap[:C, 256:256 + C], gcc, 0.0, ALU.subtract, ALU.max), reads=[b2.t, cols.t], writes=[xp[hh].t])
            fw.op("act", lambda e: e.activation(xm[hh].ap[:C, :C], xm[hh].ap[:C, :C], AF.Exp), reads=[xm[hh].t], writes=[xm[hh].t])
            fw.op("act", lambda e: e.activation(xp[hh].ap[:C, :C], xp[hh].ap[:C, :C], AF.Exp, scale=-1.0), reads=[xp[hh].t], writes=[xp[hh].t])
            fw.op("pool", lambda e: e.tensor_tensor(DT[hh].ap[:C, :C], xm[hh].ap[:C, :C], Um.ap[:C, :C], ALU.mult), reads=[xm[hh].t, Um.t], writes=[DT[hh].t])
            fw.op("pool", lambda e: e.tensor_tensor(Ds[hh].ap[:C, :C], xp[hh].ap[:C, :C], Lm.ap[:C, :C], ALU.mult), reads=[xp[hh].t, Lm.t], writes=[Ds[hh].t])
            yield
            mm(b2.ap[:C, 320:320 + C], kT, kT, [ys.t], b2)
            fw.op("dve", lambda e: e.scalar_tensor_tensor(N_[hh].ap[:C, :C], b2.ap[:C, 320:320 + C], nbc[hh].ap[:C], Ds[hh].ap[:C, :C], ALU.mult, ALU.mult),
                  reads=[b2.t, nbc[hh].t, Ds[hh].t], writes=[N_[hh].t])
            mm(b2.ap[:C, 384:384 + C], kT, qT, [ys.t], b2)
            fw.op("dve", lambda e: e.tensor_tensor(aT[hh].ap[:C, :C], b2.ap[:C, 384:384 + C], DT[hh].ap[:C, :C], ALU.mult), reads=[b2.t, DT[hh].t], writes=[aT[hh].t])
            mm(b2.ap[:C, 448:448 + C], N_[hh].ap[:C, :C], ident.ap[:C, :C], [N_[hh].t, ident.t], b2)
            X, XT, Xn, XTn = Xa[hh], N_[hh], Xb[hh], XTb[hh]
            fw.op("act", lambda e: e.copy(X.ap[:C, :C], b2.ap[:C, 448:448 + C]), reads=[b2.t], writes=[X.t])
            fw.op("dve", lambda e: e.tensor_tensor(Q_[hh].ap[:C, :C], b2.ap[:C, 448:448 + C], ident.ap[:C, :C], ALU.add), reads=[b2.t, ident.t], writes=[Q_[hh].t])
            yield
            b3 = bB
            for it in range(niter):
                mm(b3.ap[:C, 0:C], XT.ap[:C, :C], X.ap[:C, :C], [X.t, XT.t], b3)
                mm(b3.ap[:C, 64:64 + C], X.ap[:C, :C], XT.ap[:C, :C], [X.t, XT.t], b3)
                fw.op("act", lambda e: e.copy(Xn.ap[:C, :C], b3.ap[:C, 0:C]), reads=[b3.t], writes=[Xn.t])
                fw.op("dve", lambda e: e.tensor_copy(XTn.ap[:C, :C], b3.ap[:C, 64:64 + C]), reads=[b3.t], writes=[XTn.t])
                mm(b3.ap[:C, 128:128 + C], XTn.ap[:C, :C], Q_[hh].ap[:C, :C], [XTn.t, Q_[hh].t], b3)
                fw.op("dve", lambda e: e.tensor_tensor(Q_[hh].ap[:C, :C], Q_[hh].ap[:C, :C], b3.ap[:C, 128:128 + C], ALU.add), reads=[b3.t, Q_[hh].t], writes=[Q_[hh].t])
                if it == 0:
                    X, XT, Xn, XTn = Xn, XTn, Xa[hh], XTa[hh]
                else:
                    X, XT, Xn, XTn = Xn, XTn, X, XT
                yield
            b4 = bB
            mm(b4.ap[:C, 192:320], Q_[hh].ap[:C, :C], vb[hh].ap[:C], [Q_[hh].t, vb[hh].t], b4)
            mm(b4.ap[:, 320:320 + C], kbe[hh].ap[:C], Q_[hh].ap[:C, :C], [Q_[hh].t, kbe[hh].t], b4)
            fw.op("act", lambda e: e.copy(u_[hh].ap[:C], b4.ap[:C, 192:320]), reads=[b4.t], writes=[u_[hh].t])
            fw.op("dve", lambda e: e.tensor_copy(wT[hh].ap[:, :C], b4.ap[:, 320:320 + C]), reads=[b4.t], writes=[wT[hh].t])
            yield
            b5, b6 = bC, bC
            b7 = bB
            mm(b5.ap[:C, 0:128], wT[hh].ap[:, :C], Sh.ap, [wT[hh].t, Sh.t], b5)
            mm(b5.ap[:C, 128:256], qT, Sh.ap, [ys.t, Sh.t], b5)
            fw.op("dve", lambda e: e.tensor_tensor(vnew[hh].ap[:C], u_[hh].ap[:C], b5.ap[:C, 0:128], ALU.subtract), reads=[b5.t, u_[hh].t], writes=[vnew[hh].t])
            fw.op("act", lambda e: e.activation(o_[hh].ap[:C], b5.ap[:C, 128:256], AF.Copy, scale=ecol[hh].ap[:C]), reads=[b5.t, ecol[hh].t], writes=[o_[hh].t])
            mm(b5.ap[:C, 256:384], aT[hh].ap[:C, :C], vnew[hh].ap[:C], [aT[hh].t, vnew[hh].t], b5)
            fw.op("dve", lambda e: e.tensor_tensor(o_[hh].ap[:C], o_[hh].ap[:C], b5.ap[:C, 256:384], ALU.add), reads=[b5.t, o_[hh].t], writes=[o_[hh].t])
            mm(b6.ap[:, 384:512], kt[hh].ap[:C], vnew[hh].ap[:C], [kt[hh].t, vnew[hh].t], b6)
            fw.op("dve", lambda e: e.scalar_tensor_tensor(Sh.ap, Sh.ap, GL[hh].ap[:, ci:ci + 1], b6.ap[:, 384:512], ALU.mult, ALU.add),
                  reads=[b6.t, Sh.t, GL[hh].t], writes=[Sh.t])
            if state_io is not None:
                fw.dma("sp", state_io[1](hh), Sh.ap, reads=[Sh.t])
            yield
            fw.op("act", lambda e: e.activation(junk[hh].ap[:C], o_[hh].ap[:C], AF.Square, accum_out=ss[hh].ap[:C]), reads=[o_[hh].t], writes=[junk[hh].t, ss[hh].t])
            fw.op("dve", lambda e: e.tensor_scalar(ss[hh].ap[:C], ss[hh].ap[:C], 1.0 / 128, EPS, ALU.mult, ALU.add), reads=[ss[hh].t], writes=[ss[hh].t])
            fw.op("act", lambda e: e.activation(ss[hh].ap[:C], ss[hh].ap[:C], AF.Sqrt), reads=[ss[hh].t], writes=[ss[hh].t])
            fw.op("dve", lambda e: e.reciprocal(ss[hh].ap[:C], ss[hh].ap[:C]), reads=[ss[hh].t], writes=[ss[hh].t])
            fw.op("dve", lambda e: e.scalar_tensor_tensor(o_[hh].ap[:C], o_[hh].ap[:C], ss[hh].ap[:C], gnb.ap[:C], ALU.mult, ALU.mult),
                  reads=[o_[hh].t, ss[hh].t, gnb.t], writes=[o_[hh].t])
            mm(b7.ap[:, 384:384 + C], o_[hh].ap[:C], ident.ap[:C, :C], [o_[hh].t, ident.t], b7)
            fw.op("act", lambda e: e.copy(oT_seg.ap[:, hh, c0:c0 + C], b7.ap[:, 384:384 + C]), reads=[b7.t], writes=[oT_seg.t])

        gens = [head(0), head(1)]
        while gens:
            for g_ in list(gens):
                try:
                    next(g_)
                except StopIteration:
                    gens.remove(g_)

    for hh in range(2):
        fw.op("dve", lambda e: e.memset(S[hh].ap, 0.0), writes=[S[hh].t])
    C = 64
    for t0 in range(0, n_prompt, SEG):
        n = min(SEG, n_prompt - t0)
        nch = n // C
        gc = preprocess(t0, nch, C)
        for ci in range(nch):
            chunk(gc, ci, C, 5, t0)
        for k in range(n // 128):
            g = t0 // 128 + k
            fw.dma("sp", oT_d.ap[g % NCORES_L, :, (g // NCORES_L) * 128:(g // NCORES_L + 1) * 128].rearrange("(h p) t -> p h t", p=128),
                   oT_seg.ap[:, :, k * 128:(k + 1) * 128], reads=[oT_seg.t], writes=[oT_d.t])
    for hh in range(2):
        fw.dma("sp", ssm_p_out.ap[hh], S[hh].ap, reads=[S[hh].t])
    C = 4
    BSEG = 128
    for b0 in range(0, n_batch, BSEG):
        nb = min(BSEG, n_batch - b0)
        t0 = n_prompt + b0 * 4
        gc = preprocess(t0, nb, C, sample_b0=b0)
        for ci in range(nb):
            b = b0 + ci
            chunk(gc, ci, C, 1, t0, state_io=(lambda hh, b=b: sstate_d.ap[b, hh], lambda hh, b=b: ssm_s_out.ap[b, hh]))
        NBC = n_batch // NCORES_L
        for grp in range(nb // NBC):
            dest = (b0 + grp * NBC) // NBC
            fw.dma("sp", oT_d.ap[dest, :, NBLK * 128:NBLK * 128 + NBC * 4].rearrange("(h p) t -> p h t", p=128),
                   oT_seg.ap[:, :, grp * NBC * 4:(grp + 1) * NBC * 4], reads=[oT_seg.t], writes=[oT_d.t])


NEG = -1.0e30


def build_bias_factors(cx, rb_sb, ohd_d, valid_d, fv_d, J, dst, tiles, nq):
    fw = cx.fw
    ohd = cx.sb([32, J], F32, "ohd")
    val = cx.sb([16, J], F32, "val")
    fv = cx.sb([16, J], F32, "fv")
    stg = [cx.sb([128, nq], F32, "bfst") for _ in range(2)]
    ps = cx.ps([128, 512], F32, "bfps")
    fw.dma("sp", ohd.ap, ohd_d, writes=[ohd.t])
    fw.dma("sp", val.ap, valid_d, writes=[val.t])
    for j0 in range(0, J, 512):
        n = min(512, J - j0)
        fw.op("pe", lambda e: e.matmul(ps.ap[:16, :n], rb_sb.ap, ohd.ap[:, j0:j0 + n], start=True, stop=True), reads=[rb_sb.t, ohd.t], writes=[ps.t])
        fw.op("act", lambda e: e.activation(fv.ap[:, j0:j0 + n], ps.ap[:16, :n], AF.Exp), reads=[ps.t], writes=[fv.t])
    fw.op("dve", lambda e: e.tensor_tensor(fv.ap, fv.ap, val.ap, ALU.mult), reads=[fv.t, val.t], writes=[fv.t])
    fw.dma("sp", fv_d.ap, fv.ap, reads=[fv.t], writes=[fv_d.t])
    k = 0
    for ti, base in enumerate(tiles):
        for h in range(16):
            st = stg[k % 2]
            k += 1
            src = bass.AP(fv_d.ap.tensor, h * J + base, [[1, 128], [-1, nq]])
            fw.dma("sp", st.ap, src, reads=[fv_d.t], writes=[st.t], allow_slow_non_contiguous=True)
            fw.op("pool", lambda e: e.tensor_copy(dst.ap[:, ti, h, :], st.ap), reads=[st.t], writes=[dst.t])


class Attn:
    def __init__(self, cx, Lmax, nqmax, ident_bf, ones_bf):
        self.cx = cx
        self.ident_bf, self.ones_bf = ident_bf, ones_bf
        self.sc = cx.sb([128, Lmax], F32, "a_sc")
        self.wk = cx.sb([128, Lmax], F32, "a_wk")
        self.r = [cx.sb([128, 512], F32, "a_r") for _ in range(2)]
        self.mx = cx.sb([128, 8], F32, "a_mx")
        self.thr = cx.sb([128, 1], F32, "a_thr")
        self.P = [cx.sb([128, 512], BF16, "a_P") for _ in range(2)]
        self.ot = cx.sb([128, 16, 128], BF16, "a_ot")
        self.rec = cx.sb([128, 4], F32, "a_rec")
        self.aT = cx.sb([128, 16, 128], BF16, "a_aT")
        self.ps_i = [cx.ps([128, 512], F32, "a_psi") for _ in range(2)]
        self.ps_l = [cx.ps([128, 512], F32, "a_psl") for _ in range(2)]
        self.ps_o = cx.ps([128, 512], F32, "a_pso")
        self.ps_d = cx.ps([128, 512], F32, "a_psd")
        self.ps_t = cx.ps([128, 512], F32, "a_pst")
        self.ii = 0
        self.il = 0

    def run(self, nq, NKT, qiT, wi, qT, kiT, kv_tile, tail_w, cm, pen, topk, bf, bf_tiles, out_dst, out_trk, out_sb=None):
        cx, fw = self.cx, self.cx.fw
        L = NKT * 128
        sc, wk = self.sc, self.wk
        for s0 in range(0, L, 512):
            n = min(512, L - s0)
            for hi in range(32):
                ps = self.ps_i[self.ii % 2]
                r = self.r[self.ii % 2]
                self.ii += 1
                fw.op("pe", lambda e: e.matmul(ps.ap[:nq, :n], qiT.ap[:, hi, :nq], kiT.ap[:, s0:s0 + n], start=True, stop=True),
                      reads=[qiT.t, kiT.t], writes=[ps.t])
                fw.op("act", lambda e: e.activation(r.ap[:nq, :n], ps.ap[:nq, :n], AF.Relu), reads=[ps.t], writes=[r.t])
                if hi == 0:
                    fw.op("dve", lambda e: e.tensor_scalar(sc.ap[:nq, s0:s0 + n], r.ap[:nq, :n], wi.ap[:nq, 0:1], None, ALU.mult),
                          reads=[r.t, wi.t], writes=[sc.t])
                else:
                    fw.op("dve", lambda e: e.scalar_tensor_tensor(sc.ap[:nq, s0:s0 + n], r.ap[:nq, :n], wi.ap[:nq, hi:hi + 1], sc.ap[:nq, s0:s0 + n], ALU.mult, ALU.add),
                          reads=[r.t, wi.t, sc.t], writes=[sc.t])
        tl = sc.ap[:nq, L - tail_w:L]
        fw.op("dve", lambda e: e.tensor_tensor(tl, tl, cm.ap[:nq, :tail_w], ALU.mult), reads=[sc.t, cm.t], writes=[sc.t])
        fw.op("dve", lambda e: e.tensor_tensor(tl, tl, pen.ap[:nq, :tail_w], ALU.add), reads=[sc.t, pen.t], writes=[sc.t])
        fw.op("pool", lambda e: e.tensor_copy(wk.ap[:nq, :L], sc.ap[:nq, :L]), reads=[sc.t], writes=[wk.t])
        for it in range(topk // 8):
            fw.op("dve", lambda e: e.max(out=self.mx.ap[:nq], in_=wk.ap[:nq, :L]), reads=[wk.t], writes=[self.mx.t])
            if it < topk // 8 - 1:
                fw.op("dve", lambda e: e.match_replace(out=wk.ap[:nq, :L], in_to_replace=self.mx.ap[:nq], in_values=wk.ap[:nq, :L], imm_value=NEG),
                      reads=[self.mx.t, wk.t], writes=[wk.t])
        fw.op("dve", lambda e: e.tensor_scalar(self.thr.ap[:nq], self.mx.ap[:nq, 7:8], -1.0e29, None, ALU.max), reads=[self.mx.t], writes=[self.thr.t])
        wkb = wk.ap.bitcast(BF16)
        Lm = self.sc.ap.shape[1]
        m01 = wkb[:, 0:L]
        mT = wkb[:, Lm:Lm + NKT * nq].rearrange("p (k t) -> p k t", t=nq)
        fw.op("dve", lambda e: e.tensor_scalar(m01[:nq], sc.ap[:nq, :L], self.thr.ap[:nq, 0:1], None, ALU.is_ge), reads=[sc.t, self.thr.t], writes=[wk.t])
        per = max(1, 512 // nq)
        for k0 in range(0, NKT, per):
            kn = min(per, NKT - k0)
            pt = self.ps_t
            for k in range(kn):
                fw.op("pe", lambda e: e.matmul(pt.ap[:, k * nq:(k + 1) * nq], m01[:nq, (k0 + k) * 128:(k0 + k + 1) * 128], self.ident_bf.ap[:nq, :nq], start=True, stop=True),
                      reads=[wk.t, self.ident_bf.t], writes=[pt.t])
            fw.op("act", lambda e: e.copy(mT[:, k0:k0 + kn, :].rearrange("p k t -> p (k t)"), pt.ap[:, :kn * nq]), reads=[pt.t], writes=[wk.t])
        G = 4
        scale = 128.0 ** -0.5
        for kvh in range(4):
            po, pd = self.ps_o, self.ps_d
            for kt in range(NKT):
                KTt, Vt, kv_trks = kv_tile(kvh, kt)
                pl = self.ps_l[self.il % 2]
                P = self.P[self.il % 2]
                self.il += 1
                Pv = P.ap[:, :G * nq].rearrange("p (g t) -> p g t", t=nq)
                fw.op("pe", lambda e: e.matmul(pl.ap[:, :G * nq].rearrange("p (g t) -> p g t", t=nq), KTt, qT.ap[:, kvh * G:(kvh + 1) * G, :nq], start=True, stop=True),
                      reads=[qT.t] + kv_trks, writes=[pl.t])
                fw.op("act", lambda e: e.activation(P.ap[:, :G * nq], pl.ap[:, :G * nq], AF.Exp, scale=scale), reads=[pl.t], writes=[P.t])
                fw.op("dve", lambda e: e.tensor_tensor(Pv, Pv, mT[:, kt:kt + 1, :].to_broadcast([128, G, nq]), ALU.mult), reads=[P.t, wk.t], writes=[P.t])
                if kt in bf_tiles:
                    bi = bf_tiles[kt]
                    fw.op("dve", lambda e: e.tensor_tensor(Pv, Pv, bf.ap[:, bi, kvh * G:(kvh + 1) * G, :nq], ALU.mult), reads=[P.t, bf.t], writes=[P.t])
                for g in range(G):
                    fw.op("pe", lambda e: e.matmul(po.ap[:nq, g * 128:(g + 1) * 128], Pv[:, g, :], Vt, start=(kt == 0 and g == 0), stop=(kt == NKT - 1 and g == G - 1)),
                          reads=[P.t] + kv_trks, writes=[po.t], pe_acc=(kt > 0 or g > 0))
                for g in range(G):
                    fw.op("pe", lambda e: e.matmul(pd.ap[:nq, g:g + 1], Pv[:, g, :], self.ones_bf.ap[:, 0:1], start=(kt == 0 and g == 0), stop=(kt == NKT - 1 and g == G - 1)),
                          reads=[P.t, self.ones_bf.t], writes=[pd.t], pe_acc=(kt > 0 or g > 0))
            fw.op("dve", lambda e: e.reciprocal(self.rec.ap[:nq], pd.ap[:nq, 0:4]), reads=[pd.t], writes=[self.rec.t])
            for g in range(G):
                fw.op("act", lambda e: e.activation(self.ot.ap[:nq, kvh * G + g, :], po.ap[:nq, g * 128:(g + 1) * 128], AF.Copy, scale=self.rec.ap[:nq, g:g + 1]),
                      reads=[po.t, self.rec.t], writes=[self.ot.t])
        for h0 in range(0, 16, 4):
            pt = self.ps_t
            per_h = nq
            for k in range(4):
                fw.op("pe", lambda e: e.matmul(pt.ap[:, k * per_h:(k + 1) * per_h], self.ot.ap[:nq, h0 + k, :], self.ident_bf.ap[:nq, :nq], start=True, stop=True),
                      reads=[self.ot.t, self.ident_bf.t], writes=[pt.t])
            fw.op("act", lambda e: e.copy(self.aT.ap[:, h0:h0 + 4, :nq], pt.ap[:, :4 * per_h].rearrange("p (k t) -> p k t", t=per_h)), reads=[pt.t], writes=[self.aT.t])
        if out_sb is not None:
            ob, c0 = out_sb
            fw.op("pool", lambda e: e.tensor_copy(ob.ap[:, :, c0:c0 + nq], self.aT.ap[:, :, :nq]), reads=[self.aT.t], writes=[ob.t])
        else:
            fw.dma("sp", out_dst.rearrange("(h d) t -> d h t", d=128), self.aT.ap[:, :, :nq], reads=[self.aT.t], writes=out_trk)


def stage_vtok(cx, VT, Vtok, ident_bf, ntok):
    fw = cx.fw
    vin = [cx.sb([128, 4, 128], BF16, "vt_in") for _ in range(2)]
    vout = [cx.sb([128, 512], BF16, "vt_out") for _ in range(2)]
    ps = [cx.ps([128, 512], F32, "vt_ps") for _ in range(2)]
    for i in range(ntok // 128):
        a, o, p = vin[i % 2], vout[i % 2], ps[i % 2]
        fw.dma("sp", a.ap, VT.ap[:, i * 128:(i + 1) * 128].rearrange("(h d) t -> d h t", d=128), reads=[VT.t], writes=[a.t])
        for h in range(4):
            fw.op("pe", lambda e: e.matmul(p.ap[:, h * 128:(h + 1) * 128], a.ap[:, h, :], ident_bf.ap, start=True, stop=True), reads=[a.t, ident_bf.t], writes=[p.t])
        fw.op("act", lambda e: e.copy(o.ap, p.ap), reads=[p.t], writes=[o.t])
        fw.dma("sp", Vtok.ap[i * 128:(i + 1) * 128, :], o.ap, reads=[o.t], writes=[Vtok.t])


def stage_attn_prompt(cx, D_, ident_bf, ones_bf, ident_f, rb_sb):
    fw = cx.fw
    Lmax = STRIDE * NBLK * 128
    TW = STRIDE * 128
    A = Attn(cx, Lmax, 128, ident_bf, ones_bf)
    cmpen = cx.sb([128, 2, TW], F32, "cmpen")
    fw.dma("sp", cmpen.ap, D_["cmp"].ap, writes=[cmpen.t])
    cmB, penB = B(cmpen.ap[:, 0, :]), B(cmpen.ap[:, 1, :])
    cmB.t = penB.t = cmpen.t
    NW = STRIDE + 1
    bf = cx.sb([128, NW, 16, 128], BF16, "bf_p")
    Jp = 255 + STRIDE * 128
    build_bias_factors(cx, rb_sb, D_["ohd_p"].ap, D_["val_p"].ap, D_["fv_p"], Jp, bf, [255 + kr * 128 for kr in range(-1, STRIDE)], 128)
    qiT = cx.sb([128, 32, 128], BF16, "p_qiT")
    qT = cx.sb([128, 16, 128], BF16, "p_qT")
    wiT = cx.sb([32, 128], F32, "p_wiT")
    wi = cx.sb([128, 32], F32, "p_wi")
    kiT = cx.sb([128, Lmax], BF16, "p_kiT")
    CH = 16
    KTc = [cx.sb([128, CH * 128], BF16, "p_KT") for _ in range(2)]
    Vc = [cx.sb([128, CH, 128], BF16, "p_V") for _ in range(2)]
    state = {"i": 0, "cur": None}
    for j in range(NBLK):
        NKT = STRIDE * (j + 1)
        L = NKT * 128
        t0 = j * 128
        fw.dma("sp", qiT.ap, D_["qiT"].ap[:, t0:t0 + 128].rearrange("(h d) t -> d h t", d=128), reads=[D_["qiT"].t], writes=[qiT.t])
        fw.dma("sp", qT.ap, D_["qT"].ap[:, t0:t0 + 128].rearrange("(h d) t -> d h t", d=128), reads=[D_["qT"].t], writes=[qT.t])
        fw.dma("sp", wiT.ap, D_["wiT"].ap[:, t0:t0 + 128], reads=[D_["wiT"].t], writes=[wiT.t])
        fw.op("pe", lambda e: e.matmul(A.ps_t.ap[:, :32], wiT.ap, ident_f.ap[:32, :32], start=True, stop=True), reads=[wiT.t, ident_f.t], writes=[A.ps_t.t])
        fw.op("act", lambda e: e.copy(wi.ap, A.ps_t.ap[:, :32]), reads=[A.ps_t.t], writes=[wi.t])
        for l0 in range(0, L, 2048):
            n = min(2048, L - l0)
            fw.dma("sp", kiT.ap[:, l0:l0 + n], D_["KIT"].ap[:, l0:l0 + n], reads=[D_["KIT"].t], writes=[kiT.t] if l0 == 0 else [], merge=[] if l0 == 0 else [kiT.t])

        def kv_tile(kvh, kt, NKT=NKT):
            c0 = (kt // CH) * CH
            key = (j, kvh, c0)
            if state["cur"] != key:
                state["cur"] = key
                state["i"] += 1
                kb, vb_ = KTc[state["i"] % 2], Vc[state["i"] % 2]
                n = min(CH, NKT - c0)
                fw.dma("sp", kb.ap[:, :n * 128], D_["KT"].ap[kvh * 128:(kvh + 1) * 128, c0 * 128:(c0 + n) * 128], reads=[D_["KT"].t], writes=[kb.t])
                fw.dma("sp", vb_.ap[:, :n, :], D_["Vtok"].ap[c0 * 128:(c0 + n) * 128, kvh * 128:(kvh + 1) * 128].rearrange("(k p) d -> p k d", p=128),
                       reads=[D_["Vtok"].t], writes=[vb_.t])
            kb, vb_ = KTc[state["i"] % 2], Vc[state["i"] % 2]
            k = kt - c0
            return kb.ap[:, k * 128:(k + 1) * 128], vb_.ap[:, k, :], [kb.t, vb_.t]
        bf_tiles = {}
        for bi, kr in enumerate(range(-1, STRIDE)):
            kt = NKT - STRIDE + kr
            if kt >= 0:
                bf_tiles[kt] = bi
        A.run(128, NKT, qiT, wi, qT, kiT, kv_tile, TW, cmB, penB, TOPK_P, bf, bf_tiles, D_["attnT"].ap[:, t0:t0 + 128], [D_["attnT"].t])


def stage_attn_sample(cx, D_, ident_bf, ones_bf, ident_f, rb_sb):
    fw = cx.fw
    nc = cx.nc
    U32 = mybir.dt.uint32
    NPG = PAST // 128
    NKT = NPG + 1
    L = NKT * 128
    NBC = NB // NCORES_L
    nst = NBC * 4
    A = Attn(cx, L, 4, ident_bf, ones_bf)
    cmpen = cx.sb([128, 2, 128], F32, "cmpen_s")
    fw.dma("sp", cmpen.ap, D_["cms"].ap, writes=[cmpen.t])
    cmB, penB = B(cmpen.ap[:, 0, :]), B(cmpen.ap[:, 1, :])
    cmB.t = penB.t = cmpen.t
    bf = cx.sb([128, 2, 16, 4], BF16, "bf_s")
    Js = 259
    build_bias_factors(cx, rb_sb, D_["ohd_s"].ap, D_["val_s"].ap, D_["fv_s"], Js, bf, [3, 3 + 128], 4)
    npt = NBC * NPG
    pti = cx.sb([128, npt], I32, "pti")
    ptf = cx.sb([128, npt], F32, "ptf")
    idx = cx.sb([128, npt], I32, "idx")
    pcol = cx.sb([128, 1], F32, "pcol")
    fw.dma("sp", pti.ap, bass.AP(D_["pt"].ap.tensor, 0, [[0, 128], [1, npt]]), writes=[pti.t])
    fw.dma("sp", pcol.ap, D_["pcol"].ap, writes=[pcol.t])
    fw.op("dve", lambda e: e.tensor_copy(ptf.ap, pti.ap), reads=[pti.t], writes=[ptf.t])
    fw.op("dve", lambda e: e.tensor_scalar(ptf.ap, ptf.ap, 128.0, pcol.ap[:, 0:1], ALU.mult, ALU.add), reads=[ptf.t, pcol.t], writes=[ptf.t])
    fw.op("dve", lambda e: e.tensor_copy(idx.ap, ptf.ap), reads=[ptf.t], writes=[idx.t])
    s0 = NBLK * 128
    qiTs = cx.sb([128, 32, nst], BF16, "s_qiT")
    qTs = cx.sb([128, 16, nst], BF16, "s_qT")
    wiTs = cx.sb([32, nst], F32, "s_wiT")
    fw.dma("sp", qiTs.ap, D_["qiT"].ap[:, s0:s0 + nst].rearrange("(h d) t -> d h t", d=128), reads=[D_["qiT"].t], writes=[qiTs.t])
    fw.dma("sp", qTs.ap, D_["qT"].ap[:, s0:s0 + nst].rearrange("(h d) t -> d h t", d=128), reads=[D_["qT"].t], writes=[qTs.t])
    fw.dma("sp", wiTs.ap, D_["wiT"].ap[:, s0:s0 + nst], reads=[D_["wiT"].t], writes=[wiTs.t])
    qiTb = cx.sb([128, 32, 4], BF16, "s_qiTb")
    qTb = cx.sb([128, 16, 4], BF16, "s_qTb")
    wi = cx.sb([128, 32], F32, "s_wi")
    Kg = cx.sb([128, NPG, 512], BF16, "s_Kg")
    Vg = cx.sb([128, NPG, 512], BF16, "s_Vg")
    kig = cx.sb([128, NPG, 128], BF16, "s_kig")
    KTb = cx.sb([128, 4, L], BF16, "s_KTb")
    kiTb = cx.sb([128, L], BF16, "s_kiTb")
    Vn = cx.sb([128, 512], BF16, "s_Vn")
    aTs = cx.sb([128, 16, nst], BF16, "s_aTs")
    fw.op("pool", lambda e: e.memset(KTb.ap, 0.0), writes=[KTb.t])
    fw.op("pool", lambda e: e.memset(kiTb.ap, 0.0), writes=[kiTb.t])
    fw.op("pool", lambda e: e.memset(Vn.ap, 0.0), writes=[Vn.t])
    gsem = nc.alloc_semaphore("gsem")
    gcount = [0]
    gtrk_name = "gather"
    peng = fw.engs["pool"]

    def gather(dst, src_rows, col):
        for r, s_ in fw._deps([idx.t], [dst.t]).items():
            fw._wait(peng, r, s_)
        for pg in range(NPG):
            ins = nc.gpsimd.indirect_dma_start(out=dst.ap[:, pg, :], out_offset=None, in_=src_rows,
                                               in_offset=bass.IndirectOffsetOnAxis(ap=idx.ap[:, col + pg:col + pg + 1].bitcast(U32), axis=0))
            ins.then_inc(gsem, 16)
            gcount[0] += 1
        for en in ("pe", "act", "dve", "pool"):
            fw.engs[en].e.wait_ge(gsem, gcount[0] * 16)
        dst.t.w = {}
        dst.t.r = {}

    pt_ = A.ps_t
    for b in range(NBC):
        col = b * NPG
        tb = NP + (b) * 4
        gather(Kg, D_["ck"].ap, col)
        gather(Vg, D_["cv"].ap, col)
        gather(kig, D_["cki"].ap, col)
        fw.dma("sp", KTb.ap[:, :, PAST:PAST + 4], D_["KTsmp"].ap[:, b * 4:(b + 1) * 4].rearrange("(h d) t -> d h t", d=128), reads=[D_["KTsmp"].t], writes=[KTb.t])
        fw.dma("sp", kiTb.ap[:, PAST:PAST + 4], D_["KITsmp"].ap[:, b * 4:(b + 1) * 4], reads=[D_["KITsmp"].t], writes=[kiTb.t])
        fw.dma("sp", Vn.ap[0:4, :], D_["Vsmp"].ap[b * 4:(b + 1) * 4, :], reads=[D_["Vsmp"].t], writes=[Vn.t])
        for kvh in range(5):
            for g0 in range(0, NPG, 4):
                gn = min(4, NPG - g0)
                for k in range(gn):
                    src = Kg.ap[:, g0 + k, kvh * 128:(kvh + 1) * 128] if kvh < 4 else kig.ap[:, g0 + k, :]
                    fw.op("pe", lambda e: e.matmul(pt_.ap[:, k * 128:(k + 1) * 128], src, ident_bf.ap, start=True, stop=True),
                          reads=[Kg.t if kvh < 4 else kig.t, ident_bf.t], writes=[pt_.t])
                dstv = KTb.ap[:, kvh, g0 * 128:(g0 + gn) * 128] if kvh < 4 else kiTb.ap[:, g0 * 128:(g0 + gn) * 128]
                fw.op("act" if (g0 // 4) % 2 == 0 else "dve", (lambda e: e.copy(dstv, pt_.ap[:, :gn * 128])) if (g0 // 4) % 2 == 0 else (lambda e: e.tensor_copy(dstv, pt_.ap[:, :gn * 128])),
                      reads=[pt_.t], writes=[KTb.t if kvh < 4 else kiTb.t])
        fw.op("pool", lambda e: e.tensor_copy(qiTb.ap, qiTs.ap[:, :, b * 4:(b + 1) * 4]), reads=[qiTs.t], writes=[qiTb.t])
        fw.op("pool", lambda e: e.tensor_copy(qTb.ap, qTs.ap[:, :, b * 4:(b + 1) * 4]), reads=[qTs.t], writes=[qTb.t])
        fw.op("pe", lambda e: e.matmul(pt_.ap[:4, :32], wiTs.ap[:, b * 4:(b + 1) * 4], ident_f.ap[:32, :32], start=True, stop=True), reads=[wiTs.t, ident_f.t], writes=[pt_.t])
        fw.op("act", lambda e: e.copy(wi.ap[:4], pt_.ap[:4, :32]), reads=[pt_.t], writes=[wi.t])

        def kv_tile(kvh, kt):
            if kt < NPG:
                return KTb.ap[:, kvh, kt * 128:(kt + 1) * 128], Vg.ap[:, kt, kvh * 128:(kvh + 1) * 128], [KTb.t, Vg.t]
            return KTb.ap[:, kvh, PAST:PAST + 128], Vn.ap[:, kvh * 128:(kvh + 1) * 128], [KTb.t, Vn.t]
        A.run(4, NKT, qiTb, wi, qTb, kiTb, kv_tile, 128, cmB, penB, TOPK_S, bf, {NPG - 1: 0, NPG: 1}, None, None, out_sb=(aTs, b * 4))
    fw.dma("sp", D_["attnT"].ap[:, s0:s0 + nst].rearrange("(h d) t -> d h t", d=128), aTs.ap, reads=[aTs.t], writes=[D_["attnT"].t])


def build():
    cx = Ctx(num_devices=NCORES_L if FUSED else None)
    nc, fw = cx.nc, cx.fw
    xT = cx.dram("xT", [D, NALL], F32, kind="ExternalInput")
    w_seq = cx.dram("w_seq", [D, 2048], F32, kind="ExternalInput")
    norm_mix = cx.dram("norm_mix", [128, KC], F32, kind="ExternalInput")
    kvkT = cx.dram("kvkT", [1152, NALL], F32, kind="ExternalOutput")
    qkvr = cx.dram("qkvr", [768, NALL], F32, kind="ExternalOutput")
    hT_all = cx.dram("hT_all", [D, NALL], BF16)
    KT = cx.dram("KT_s", [512, NALL], BF16)
    VT = cx.dram("VT_s", [512, NALL], BF16)
    KIT = cx.dram("KIT_s", [128, NALL], BF16)
    GAB = cx.dram("GAB_s", [128, NALL], F32)

    cst_d = cx.dram("cst", [128, 3, 128], F32, kind="ExternalInput")
    convw_d = cx.dram("convw", [128, 6, 4], F32, kind="ExternalInput")
    sconv_d = cx.dram("sconv", [768, NB * 3], F32, kind="ExternalInput")
    galog_d = cx.dram("galog", [128, 2], F32, kind="ExternalInput")
    gnorm_d = cx.dram("gnorm", [1, 128], F32, kind="ExternalInput")
    sstate_d = cx.dram("sstate", [NB, 2, 128, 128], F32, kind="ExternalInput")
    ssm_p_out = cx.dram("ssm_p", [2, 128, 128], F32, kind="ExternalOutput")
    ssm_s_out = cx.dram("ssm_s", [NB, 2, 128, 128], F32, kind="ExternalOutput")
    NOWN_ = NBLK * 128 + (NB // NCORES_L) * 4
    oT_d = cx.dram("oT_sh", [NCORES_L, 256, NOWN_], BF16, kind="Internal" if FUSED else "ExternalOutput")
    xT_own = cx.dram("xT_own", [D, NOWN_], F32, kind="ExternalInput")
    w_in_d = cx.dram("w_in", [D, 7328], F32, kind="ExternalInput")
    relb_d = cx.dram("rel_bias", [32, 16], F32, kind="ExternalInput")
    TWp = STRIDE * 128
    Jp = 255 + STRIDE * 128
    Dd = {}
    Dd["cmp"] = cx.dram("cmp", [128, 2, TWp], F32, kind="ExternalInput")
    Dd["cms"] = cx.dram("cms", [128, 2, 128], F32, kind="ExternalInput")
    Dd["ohd_p"] = cx.dram("ohd_p", [32, Jp], F32, kind="ExternalInput")
    Dd["val_p"] = cx.dram("val_p", [16, Jp], F32, kind="ExternalInput")
    Dd["ohd_s"] = cx.dram("ohd_s", [32, 259], F32, kind="ExternalInput")
    Dd["val_s"] = cx.dram("val_s", [16, 259], F32, kind="ExternalInput")
    Dd["pcol"] = cx.dram("pcol", [128, 1], F32, kind="ExternalInput")
    NPOOLR = NPOOL * 128
    Dd["ck"] = cx.dram("cache_k", [NPOOLR, 512], F32, kind="ExternalInput")
    Dd["cv"] = cx.dram("cache_v", [NPOOLR, 512], F32, kind="ExternalInput")
    Dd["cki"] = cx.dram("cache_ki", [NPOOLR, 128], F32, kind="ExternalInput")
    Dd["pt"] = cx.dram("ptab", [1, (NB // NCORES_L) * (PAST // 128)], I32, kind="ExternalInput")
    Dd["fv_p"] = cx.dram("fv_p", [16, Jp], F32)
    Dd["fv_s"] = cx.dram("fv_s", [16, 259], F32)
    Dd["attnT"] = cx.dram("attnT", [2048, NOWN_], BF16, kind="Internal" if FUSED else "ExternalOutput")
    hT_own = cx.dram("hT_own", [D, NOWN_], BF16)
    Dd["qT"] = cx.dram("qT_s", [2048, NOWN_], BF16)
    Dd["qiT"] = cx.dram("qiT_s", [4096, NOWN_], BF16)
    Dd["wiT"] = cx.dram("wiT_s", [32, NOWN_], F32)
    KTo = cx.dram("KTo_s", [512, NOWN_], BF16)
    VTo = cx.dram("VTo_s", [512, NOWN_], BF16)
    KITo = cx.dram("KITo_s", [128, NOWN_], BF16)
    Dd["Vsmp"] = cx.dram("Vsmp_s", [128, 512], BF16)
    Dd["Vtok"] = cx.dram("Vtok_s", [NALL, 512], BF16)
    cx.ones_b = cx.sb([128, 128], BF16, "ones_b")
    fw.op("dve", lambda e: e.memset(cx.ones_b.ap, 1.0), writes=[cx.ones_b.t])
    nw_mix = cx.sb([128, KC], F32, "nw_mix")
    fw.dma("sp", nw_mix.ap, norm_mix.ap, writes=[nw_mix.t])

    nsup = NALL // SUP
    hparts = [Trk() for _ in range((NALL + 127) // 128)]
    cx.begin()
    if not SKIP_H:
        rmsnorm_T(cx, xT.ap, [], hT_all.ap, lambda i: [hparts[i]], nw_mix, NALL, pfx="rnA")
    cx.end()
    if STOP == "H":
        fw.finish()
        return cx
    cx.begin()

    lin = Lin(cx)
    act = cx.sb([128, KC, SUP], BF16, "actA")
    ef = [cx.sb([128, SUP], F32, "ef") for _ in range(2)]
    eb = [cx.sb([128, SUP], BF16, "eb") for _ in range(2)]
    ei = [0]
    for s in range(nsup):
        t0 = s * SUP
        ptr = hparts[(t0 // 128):(t0 + SUP + 127) // 128]
        qq = cx.q()
        for c4 in range(0, KC, 8):
            fw.dma(qq, act.ap[:, c4:c4 + 8, :], hT_all.ap[c4 * 128:(c4 + 8) * 128, t0:t0 + SUP].rearrange("(c p) t -> p c t", p=128),
                   reads=ptr, writes=[act.t] if c4 == 0 else [], merge=[] if c4 == 0 else [act.t])
        for ft in range(NFT):
            def epi(ps, ft=ft, t0=t0):
                f = ef[ei[0] % 2]
                b = eb[ei[0] % 2]
                ei[0] += 1
                for s0 in range(0, SUP, 512):
                    n = min(512, SUP - s0)
                    fw.op("act", lambda e: e.copy(f.ap[:, s0:s0 + n], ps.ap[:, s0:s0 + n]), reads=[ps.t], writes=[f.t, ps.t] if DBG == 7 else [f.t])
                    if ft < 9 and DBG == 4:
                        fw.op("dve", lambda e: e.tensor_copy(b.ap[:, s0:s0 + n], f.ap[:, s0:s0 + n]), reads=[f.t], writes=[b.t])
                    elif ft < 9 and DBG == 6:
                        fw.op("act", lambda e: e.copy(b.ap[:, s0:s0 + n], ps.ap[:, s0:s0 + n]), reads=[ps.t], writes=[b.t])
                    elif ft < 9 and DBG == 7:
                        fw.op("dve", lambda e: e.tensor_copy(b.ap[:, s0:s0 + n], ps.ap[:, s0:s0 + n]), reads=[ps.t], writes=[b.t, ps.t])
                    elif ft < 9 and DBG != 1:
                        fw.op("dve", lambda e: e.tensor_copy(b.ap[:, s0:s0 + n], ps.ap[:, s0:s0 + n]), reads=[ps.t], writes=[b.t])
                if ft < 9:
                    fw.dma(cx.q(), kvkT.ap[ft * 128:(ft + 1) * 128, t0:t0 + SUP], f.ap, reads=[f.t])
                    dst = KT.ap[ft * 128:(ft + 1) * 128] if ft < 4 else (VT.ap[(ft - 4) * 128:(ft - 3) * 128] if ft < 8 else KIT.ap)
                    if DBG == 3:
                        fw.dma("sp", dst[:, t0:t0 + SUP], b.ap, reads=[b.t])
                    elif DBG not in (1, 2):
                        fw.dma(cx.q(), dst[:, t0:t0 + SUP], b.ap, reads=[b.t])
                elif ft < 15:
                    fw.dma(cx.q(), qkvr.ap[(ft - 9) * 128:(ft - 8) * 128, t0:t0 + SUP], f.ap, reads=[f.t])
                else:
                    fw.dma(cx.q(), GAB.ap[:, t0:t0 + SUP], f.ap, reads=[f.t])
            lin.run(act.ap, [act.t], KC, SUP, w_seq.ap, ft * 128, 128, epi)
    cx.end()
    if STOP == "A":
        fw.finish()
        return cx
    cx.begin()
    stage_gdn(cx, qkvr, GAB, cst_d, convw_d, sconv_d, galog_d, gnorm_d, sstate_d, ssm_p_out, ssm_s_out, oT_d, NP, NB)
    cx.end()
    if STOP == "G":
        fw.finish()
        return cx
    ident_f = cx.sb([128, 128], F32, "ident_fp")
    fw.dma("sp", ident_f.ap, cst_d.ap[:, 0, :], writes=[ident_f.t])
    ident_bf = cx.sb([128, 128], BF16, "ident_bf")
    fw.op("dve", lambda e: e.tensor_copy(ident_bf.ap, ident_f.ap), reads=[ident_f.t], writes=[ident_bf.t])
    rb_sb = cx.sb([32, 16], F32, "rb_sb")
    fw.dma("sp", rb_sb.ap, relb_d.ap, writes=[rb_sb.t])
    cx.begin()
    stage_vtok(cx, VT, Dd["Vtok"], ident_bf, NP)
    cx.end()
    cx.begin()
    rmsnorm_T(cx, xT_own.ap, [], hT_own.ap, lambda i: [hT_own.t], nw_mix, NOWN_, pfx="rnB")
    cx.end()
    cx.begin()
    lin = Lin(cx)
    act = cx.sb([128, KC, NOWN_], BF16, "actB")
    for c4 in range(0, KC, 8):
        fw.dma("sp", act.ap[:, c4:c4 + 8, :], hT_own.ap[c4 * 128:(c4 + 8) * 128, :].rearrange("(c p) t -> p c t", p=128),
               reads=[hT_own.t], writes=[act.t] if c4 == 0 else [], merge=[] if c4 == 0 else [act.t])
    eb = [cx.sb([128, NOWN_], BF16, "ebB") for _ in range(2)]
    ef = cx.sb([32, NOWN_], F32, "efB")
    ei = [0]
    jobs = [(O_Q, 2048, Dd["qT"]), (O_QI, 4096, Dd["qiT"]), (O_K, 512, KTo), (O_V, 512, VTo), (O_KI, 128, KITo)]
    for (o0, ncols, dstd) in jobs:
        for ft in range(ncols // 128):
            def epi(ps, ft=ft, dstd=dstd):
                b = eb[ei[0] % 2]
                ei[0] += 1
                for s0 in range(0, NOWN_, 512):
                    n = min(512, NOWN_ - s0)
                    fw.op("act", lambda e: e.copy(b.ap[:, s0:s0 + n], ps.ap[:, s0:s0 + n]), reads=[ps.t], writes=[b.t])
                fw.dma("sp", dstd.ap[ft * 128:(ft + 1) * 128, :], b.ap, reads=[b.t], writes=[dstd.t])
            lin.run(act.ap, [act.t], KC, NOWN_, w_in_d.ap, o0 + ft * 128, 128, epi)

    def epi_wi(ps):
        for s0 in range(0, NOWN_, 512):
            n = min(512, NOWN_ - s0)
            fw.op("act", lambda e: e.copy(ef.ap[:, s0:s0 + n], ps.ap[:32, s0:s0 + n]), reads=[ps.t], writes=[ef.t])
        fw.dma("sp", Dd["wiT"].ap, ef.ap, reads=[ef.t], writes=[Dd["wiT"].t])
    lin.run(act.ap, [act.t], KC, NOWN_, w_in_d.ap, O_WI, 32, epi_wi)
    cx.end()
    cx.begin()
    nst = (NB // NCORES_L) * 4
    vin = cx.sb([128, 4, nst], BF16, "vs_in")
    vout = cx.sb([128, 512], BF16, "vs_out")
    vps = cx.ps([128, 512], F32, "vs_ps")
    fw.dma("sp", vin.ap, VTo.ap[:, NBLK * 128:NBLK * 128 + nst].rearrange("(h d) t -> d h t", d=128), reads=[VTo.t], writes=[vin.t])
    for h in range(4):
        fw.op("pe", lambda e: e.matmul(vps.ap[:nst, h * 128:(h + 1) * 128], vin.ap[:, h, :], ident_bf.ap, start=True, stop=True), reads=[vin.t, ident_bf.t], writes=[vps.t])
    fw.op("act", lambda e: e.copy(vout.ap[:nst], vps.ap[:nst]), reads=[vps.t], writes=[vout.t])
    fw.dma("sp", Dd["Vsmp"].ap[:nst], vout.ap[:nst], reads=[vout.t], writes=[Dd["Vsmp"].t])
    cx.end()
    Dd["KT"], Dd["KIT"] = KT, KIT
    Dd["KTsmp"] = B(KTo.ap[:, NBLK * 128:NBLK * 128 + nst]); Dd["KTsmp"].t = KTo.t
    Dd["KITsmp"] = B(KITo.ap[:, NBLK * 128:NBLK * 128 + nst]); Dd["KITsmp"].t = KITo.t
    if STOP != "noP":
        cx.begin()
        stage_attn_prompt(cx, Dd, ident_bf, cx.ones_b, ident_f, rb_sb)
        cx.end()
    if STOP != "noS":
        cx.begin()
        stage_attn_sample(cx, Dd, ident_bf, cx.ones_b, ident_f, rb_sb)
        cx.end()
    if not FUSED:
        fw.finish()
        return cx
    NCL = NCORES_L
    gath = cx.dram("o_gath", [NCL * NCL * 256, NOWN_], BF16)
    oT_all = cx.dram("oT_all_s", [NCL * 256, NOWN_], BF16)
    sel_d = cx.dram("sel", [128, NCL], F32, kind="ExternalInput")
    fw.barrier()
    csem = nc.alloc_semaphore("csem")
    ins = nc.gpsimd.collective_compute("AllGather", mybir.AluOpType.bypass, replica_groups=[list(range(NCL))],
                                       ins=[oT_d.ap.rearrange("a r t -> (a r) t")], outs=[gath.ap])
    ins.then_inc(csem, 1)
    for en in ("sp", "pe", "act", "dve", "pool"):
        fw.engs[en].e.wait_ge(csem, 1)
    cx.begin()
    sel = cx.sb([128, NCL], F32, "sel")
    fw.dma("sp", sel.ap, sel_d.ap, writes=[sel.t])
    gt = [cx.sb([128, NOWN_], BF16, "gt") for _ in range(3)]
    acc = [cx.sb([128, NOWN_], F32, "gacc") for _ in range(2)]
    ob = [cx.sb([128, NOWN_], BF16, "gob") for _ in range(2)]
    n_ = 0
    for i in range(NCL):
        for hf in range(2):
            a = acc[(i * 2 + hf) % 2]
            o = ob[(i * 2 + hf) % 2]
            for k in range(NCL):
                g = gt[n_ % 3]
                n_ += 1
                r0 = (i * NCL + k) * 256 + hf * 128
                fw.dma("sp", g.ap, gath.ap[r0:r0 + 128, :], writes=[g.t])
                if k == 0:
                    fw.op("dve", lambda e: e.tensor_scalar(a.ap, g.ap, sel.ap[:, k:k + 1], None, ALU.mult), reads=[g.t, sel.t], writes=[a.t])
                else:
                    fw.op("dve", lambda e: e.scalar_tensor_tensor(a.ap, g.ap, sel.ap[:, k:k + 1], a.ap, ALU.mult, ALU.add), reads=[g.t, sel.t, a.t], writes=[a.t])
            fw.op("pool", lambda e: e.tensor_copy(o.ap, a.ap), reads=[a.t], writes=[o.t])
            fw.dma("sp", oT_all.ap[i * 256 + hf * 128:i * 256 + (hf + 1) * 128, :], o.ap, reads=[o.t], writes=[oT_all.t])
    cx.end()
    build2(cx, {"xT_own": xT_own, "attnT": Dd["attnT"], "oT_all": oT_all})
    return cx


def _t5_bucket(d):
    import math
    d = np.asarray(d)
    n = np.maximum(d, 0)
    large = 16 + (np.log(np.maximum(n, 1).astype(np.float32) / np.float32(16)) / np.float32(math.log(128 / 16)) * np.float32(16)).astype(np.int32)
    large = np.minimum(large, 31)
    return np.where(n < 16, n, large)


def host_consts(c):
    f32 = np.float32
    out = {}
    TW = STRIDE * 128
    t = np.arange(128)[:, None]
    u = np.arange(TW)[None, :]
    cm = (u <= c * 128 + t).astype(f32)
    out["cmp"] = np.ascontiguousarray(np.stack([cm, (cm - 1) * f32(1e30)], 1).astype(f32))
    s_ = np.arange(128)[None, :]
    cms = ((s_ <= t) & (s_ < 4)).astype(f32)
    out["cms"] = np.ascontiguousarray(np.stack([cms, (cms - 1) * f32(1e30)], 1).astype(f32))
    Jp = 255 + STRIDE * 128
    d = c * 128 + 255 - np.arange(Jp)
    oh = np.zeros((32, Jp), f32)
    bk = _t5_bucket(d)
    oh[bk, np.arange(Jp)] = 1.0
    oh[31, :] -= 1.0
    oh[:, d < 0] = 0.0
    out["ohd_p"] = oh
    out["val_p"] = np.ascontiguousarray(np.broadcast_to((d >= 0).astype(f32)[None, :], (16, Jp)))
    Js = 259
    d = 131 - np.arange(Js)
    oh = np.zeros((32, Js), f32)
    bk = _t5_bucket(d)
    oh[bk, np.arange(Js)] = 1.0
    oh[31, :] -= 1.0
    oh[:, d < 0] = 0.0
    out["ohd_s"] = oh
    out["val_s"] = np.ascontiguousarray(np.broadcast_to((d >= 0).astype(f32)[None, :], (16, Js)))
    out["pcol"] = np.arange(128, dtype=f32)[:, None]
    sel = np.zeros((128, NCORES_L), f32)
    sel[:, c] = 1.0
    out["sel"] = sel
    cst = np.zeros((128, 3, 128), f32)
    cst[:, 0, :] = np.eye(128, dtype=f32)
    pp = np.arange(128)[:, None]
    jj = np.arange(128)[None, :]
    cst[:, 1, :] = (jj < pp)
    cst[:, 2, :] = (jj >= pp)
    out["cst"] = cst
    return out


def build2(cx=None, pre=None):
    fused = cx is not None
    if not fused:
        cx = Ctx()
    nc, fw = cx.nc, cx.fw
    NO = NBLK * 128 + (NB // NCORES_L) * 4
    if fused:
        xT_own, attnT, oT = pre["xT_own"], pre["attnT"], pre["oT_all"]
    else:
        xT_own = cx.dram("xT_own", [D, NO], F32, kind="ExternalInput")
        attnT = cx.dram("attnT_in", [2048, NO], BF16, kind="ExternalInput")
        oT = cx.dram("oT_all", [2048, NO], BF16, kind="ExternalInput")
    pT = cx.dram("pT", [256, NO], F32, kind="ExternalInput")
    w_in = cx.dram("w_in2", [D, 10240], F32, kind="ExternalInput")
    w_au = cx.dram("w_attn_up", [2048, D], F32, kind="ExternalInput")
    w_gu = cx.dram("w_gdn_up", [2048, D], F32, kind="ExternalInput")
    w_out = cx.dram("w_out", [D, D], F32, kind="ExternalInput")
    w_gup = cx.dram("w_gate_up", [D, 2 * DFF], F32, kind="ExternalInput")
    w_dn = cx.dram("w_down", [DFF, D], F32, kind="ExternalInput")
    w_pg = cx.dram("w_ple_gate", [D, D], F32, kind="ExternalInput")
    w_pl = cx.dram("w_ple", [256, D], F32, kind="ExternalInput")
    norms = cx.dram("norms", [128, 4, KC], F32, kind="ExternalInput")
    yT = cx.dram("yT", [D, NO], F32, kind="ExternalOutput")
    hT = cx.dram("t_hT", [D, NO], BF16)
    OG = cx.dram("t_OG", [2048, NO], BF16)
    GA = cx.dram("t_GA", [D, NO], BF16)
    GB = cx.dram("t_GB", [D, NO], BF16)
    M1 = cx.dram("t_M1", [D, NO], F32)
    MG = cx.dram("t_MG", [D, NO], BF16)
    X1 = cx.dram("t_X1", [D, NO], F32)
    H1 = cx.dram("t_H1", [D, NO], BF16)
    MID = cx.dram("t_MID", [DFF, NO], BF16)
    X2 = cx.dram("t_X2", [D, NO], F32)
    H2 = cx.dram("t_H2", [D, NO], BF16)
    PG = cx.dram("t_PG", [D, NO], BF16)
    X3 = cx.dram("t_X3", [D, NO], F32)
    if not fused:
        cx.ones_b = cx.sb([128, 128], BF16, "ones_b")
        fw.op("dve", lambda e: e.memset(cx.ones_b.ap, 1.0), writes=[cx.ones_b.t])
    nws = cx.sb([128, 4, KC], F32, "nws")
    fw.dma("sp", nws.ap, norms.ap, writes=[nws.t])

    def nw(i):
        b = B(nws.ap[:, i, :])
        b.t = nws.t
        return b

    def norm_stage(src, dst, i, out_dt=BF16):
        cx.begin()
        rmsnorm_T(cx, src.ap, [src.t], dst.ap, lambda k: [dst.t], nw(i), NO, out_dt=out_dt, pfx="rn2")
        cx.end()

    def load_act(act, src, nkc, tok0=0, ntok=None):
        ntok = ntok or NO
        for c4 in range(0, nkc, 8):
            ce = min(nkc, c4 + 8)
            fw.dma("sp", act.ap[:, c4:ce, :ntok], src.ap[c4 * 128:ce * 128, tok0:tok0 + ntok].rearrange("(c p) t -> p c t", p=128),
                   reads=[src.t], writes=[act.t] if c4 == 0 else [], merge=[] if c4 == 0 else [act.t])

    def segs(n):
        return [(s0, min(512, n - s0)) for s0 in range(0, n, 512)]

    norm_stage(xT_own, hT, 0)
    cx.begin()
    lin = Lin(cx)
    act = cx.sb([128, KC, NO], BF16, "act")
    load_act(act, hT, KC)
    tf = [cx.sb([128, NO], F32, "tf") for _ in range(2)]
    tb = [cx.sb([128, NO], BF16, "tb") for _ in range(2)]
    tl = [cx.sb([128, NO], BF16, "tl") for _ in range(2)]
    k_ = [0]
    for ft in range(16):
        def epi(ps, ft=ft):
            i = k_[0] % 2
            k_[0] += 1
            fw.dma("sp", tl[i].ap, oT.ap[ft * 128:(ft + 1) * 128, :], reads=[oT.t], writes=[tl[i].t])
            for s0, n in segs(NO):
                fw.op("act", lambda e: e.activation(tf[i].ap[:, s0:s0 + n], ps.ap[:, s0:s0 + n], AF.Silu), reads=[ps.t], writes=[tf[i].t])
            fw.op("dve", lambda e: e.tensor_tensor(tb[i].ap, tf[i].ap, tl[i].ap, ALU.mult), reads=[tf[i].t, tl[i].t], writes=[tb[i].t])
            fw.dma("sp", OG.ap[ft * 128:(ft + 1) * 128, :], tb[i].ap, reads=[tb[i].t], writes=[OG.t])
        lin.run(act.ap, [act.t], KC, NO, w_in.ap, O_GZ - O_GZ + ft * 128, 128, epi)
    for (o0, dstd) in ((O_GTA, GA), (O_GTB, GB)):
        for ft in range(32):
            def epi(ps, ft=ft, dstd=dstd):
                i = k_[0] % 2
                k_[0] += 1
                for s0, n in segs(NO):
                    fw.op("act", lambda e: e.activation(tb[i].ap[:, s0:s0 + n], ps.ap[:, s0:s0 + n], AF.Sigmoid), reads=[ps.t], writes=[tb[i].t])
                fw.dma("sp", dstd.ap[ft * 128:(ft + 1) * 128, :], tb[i].ap, reads=[tb[i].t], writes=[dstd.t])
            lin.run(act.ap, [act.t], KC, NO, w_in.ap, o0 - O_GZ + ft * 128, 128, epi)
    cx.end()
    cx.begin()
    lin = Lin(cx)
    act = cx.sb([128, 16, NO], BF16, "act")
    load_act(act, attnT, 16)
    tf = [cx.sb([128, NO], F32, "tf") for _ in range(2)]
    tl = [cx.sb([128, NO], BF16, "tl") for _ in range(2)]
    for ft in range(32):
        def epi(ps, ft=ft):
            i = ft % 2
            fw.dma("sp", tl[i].ap, GA.ap[ft * 128:(ft + 1) * 128, :], reads=[GA.t], writes=[tl[i].t])
            for s0, n in segs(NO):
                fw.op("dve", lambda e: e.tensor_tensor(tf[i].ap[:, s0:s0 + n], ps.ap[:, s0:s0 + n], tl[i].ap[:, s0:s0 + n], ALU.mult), reads=[ps.t, tl[i].t], writes=[tf[i].t])
            fw.dma("sp", M1.ap[ft * 128:(ft + 1) * 128, :], tf[i].ap, reads=[tf[i].t], writes=[M1.t])
        lin.run(act.ap, [act.t], 16, NO, w_au.ap, ft * 128, 128, epi)
    cx.end()
    cx.begin()
    lin = Lin(cx)
    act = cx.sb([128, 16, NO], BF16, "act")
    load_act(act, OG, 16)
    tf = [cx.sb([128, NO], F32, "tf") for _ in range(2)]
    tm = [cx.sb([128, NO], F32, "tm") for _ in range(2)]
    tl = [cx.sb([128, NO], BF16, "tl") for _ in range(2)]
    tb = [cx.sb([128, NO], BF16, "tb") for _ in range(2)]
    for ft in range(32):
        def epi(ps, ft=ft):
            i = ft % 2
            fw.dma("sp", tl[i].ap, GB.ap[ft * 128:(ft + 1) * 128, :], reads=[GB.t], writes=[tl[i].t])
            fw.dma("sp", tm[i].ap, M1.ap[ft * 128:(ft + 1) * 128, :], reads=[M1.t], writes=[tm[i].t])
            for s0, n in segs(NO):
                fw.op("dve", lambda e: e.tensor_tensor(tf[i].ap[:, s0:s0 + n], ps.ap[:, s0:s0 + n], tl[i].ap[:, s0:s0 + n], ALU.mult), reads=[ps.t, tl[i].t], writes=[tf[i].t])
            fw.op("pool", lambda e: e.tensor_tensor(tb[i].ap, tf[i].ap, tm[i].ap, ALU.add), reads=[tf[i].t, tm[i].t], writes=[tb[i].t])
            fw.dma("sp", MG.ap[ft * 128:(ft + 1) * 128, :], tb[i].ap, reads=[tb[i].t], writes=[MG.t])
        lin.run(act.ap, [act.t], 16, NO, w_gu.ap, ft * 128, 128, epi)
    cx.end()

    def resid_lin(src_act, nkc, W, res_src, dst, ntok_split=1, gate_src=None):
        cx.begin()
        lin = Lin(cx)
        nt = NO // ntok_split
        act = cx.sb([128, nkc, nt], BF16, "act")
        tf = [cx.sb([128, nt], F32, "tf") for _ in range(2)]
        tm = [cx.sb([128, nt], F32, "tm") for _ in range(2)]
        tl = [cx.sb([128, nt], BF16, "tl") for _ in range(2)]
        for half in range(ntok_split):
            tok0 = half * nt
            load_act(act, src_act, nkc, tok0, nt)
            for ft in range(32):
                def epi(ps, ft=ft, tok0=tok0):
                    i = ft % 2
                    fw.dma("sp", tm[i].ap, res_src.ap[ft * 128:(ft + 1) * 128, tok0:tok0 + nt], reads=[res_src.t], writes=[tm[i].t])
                    if gate_src is not None:
                        fw.dma("sp", tl[i].ap, gate_src.ap[ft * 128:(ft + 1) * 128, tok0:tok0 + nt], reads=[gate_src.t], writes=[tl[i].t])
                    for s0, n in segs(nt):
                        if gate_src is not None:
                            fw.op("dve", lambda e: e.tensor_tensor(tf[i].ap[:, s0:s0 + n], ps.ap[:, s0:s0 + n], tl[i].ap[:, s0:s0 + n], ALU.mult), reads=[ps.t, tl[i].t], writes=[tf[i].t])
                            fw.op("pool", lambda e: e.tensor_tensor(tf[i].ap[:, s0:s0 + n], tf[i].ap[:, s0:s0 + n], tm[i].ap[:, s0:s0 + n], ALU.add), reads=[tf[i].t, tm[i].t], writes=[tf[i].t])
                        else:
                            fw.op("dve", lambda e: e.tensor_tensor(tf[i].ap[:, s0:s0 + n], ps.ap[:, s0:s0 + n], tm[i].ap[:, s0:s0 + n], ALU.add), reads=[ps.t, tm[i].t], writes=[tf[i].t])
                    fw.dma("sp", dst.ap[ft * 128:(ft + 1) * 128, tok0:tok0 + nt], tf[i].ap, reads=[tf[i].t], writes=[dst.t])
                lin.run(act.ap, [act.t], nkc, nt, W.ap, ft * 128, 128, epi)
        cx.end()

    resid_lin(MG, KC, w_out, xT_own, X1)
    norm_stage(X1, H1, 1)
    cx.begin()
    lin = Lin(cx)
    act = cx.sb([128, KC, NO], BF16, "act")
    load_act(act, H1, KC)
    tf = [cx.sb([128, NO], F32, "tf") for _ in range(2)]
    tb = [cx.sb([128, NO], BF16, "tb") for _ in range(2)]
    for ft in range(DFF // 128):
        i = ft % 2

        def epi_g(ps, i=i):
            for s0, n in segs(NO):
                fw.op("act", lambda e: e.activation(tf[i].ap[:, s0:s0 + n], ps.ap[:, s0:s0 + n], AF.Silu), reads=[ps.t], writes=[tf[i].t])

        def epi_u(ps, i=i, ft=ft):
            for s0, n in segs(NO):
                fw.op("dve", lambda e: e.tensor_tensor(tb[i].ap[:, s0:s0 + n], ps.ap[:, s0:s0 + n], tf[i].ap[:, s0:s0 + n], ALU.mult), reads=[ps.t, tf[i].t], writes=[tb[i].t])
            fw.dma("sp", MID.ap[ft * 128:(ft + 1) * 128, :], tb[i].ap, reads=[tb[i].t], writes=[MID.t])
        lin.run(act.ap, [act.t], KC, NO, w_gup.ap, ft * 128, 128, epi_g)
        lin.run(act.ap, [act.t], KC, NO, w_gup.ap, DFF + ft * 128, 128, epi_u)
    cx.end()
    resid_lin(MID, DFF // 128, w_dn, X1, X2, ntok_split=2)
    norm_stage(X2, H2, 2)
    cx.begin()
    lin = Lin(cx)
    act = cx.sb([128, KC, NO], BF16, "act")
    load_act(act, H2, KC)
    tb = [cx.sb([128, NO], BF16, "tb") for _ in range(2)]
    for ft in range(32):
        def epi(ps, ft=ft):
            i = ft % 2
            for s0, n in segs(NO):
                fw.op("act", lambda e: e.activation(tb[i].ap[:, s0:s0 + n], ps.ap[:, s0:s0 + n], AF.Sigmoid), reads=[ps.t], writes=[tb[i].t])
            fw.dma("sp", PG.ap[ft * 128:(ft + 1) * 128, :], tb[i].ap, reads=[tb[i].t], writes=[PG.t])
        lin.run(act.ap, [act.t], KC, NO, w_pg.ap, ft * 128, 128, epi)
    cx.end()
    PB = cx.dram("t_PB", [256, NO], BF16)
    cx.begin()
    pf = cx.sb([128, 2, NO], F32, "pf")
    pb = cx.sb([128, 2, NO], BF16, "pb")
    fw.dma("sp", pf.ap, pT.ap.rearrange("(c p) t -> p c t", p=128), writes=[pf.t])
    fw.op("dve", lambda e: e.tensor_copy(pb.ap, pf.ap), reads=[pf.t], writes=[pb.t])
    fw.dma("sp", PB.ap.rearrange("(c p) t -> p c t", p=128), pb.ap, reads=[pb.t], writes=[PB.t])
    cx.end()
    resid_lin(PB, 2, w_pl, X2, X3, gate_src=PG)
    norm_stage(X3, yT, 3, out_dt=F32)
    fw.finish()
    return cx


_BUILT = None


def _own_idx(c):
    idx = []
    for j in range(NBLK):
        g = c + NCORES_L * j
        idx.extend(range(g * 128, (g + 1) * 128))
    nst = (NB // NCORES_L) * 4
    idx.extend(range(NP + nst * c, NP + nst * (c + 1)))
    return np.array(idx)


def kernel(**inp):
    global _BUILT
    f32 = np.float32
    xp = np.asarray(inp["x_prompt"], f32)[0]
    xs = np.asarray(inp["x_sample"], f32).reshape(NS, D)
    xT = np.ascontiguousarray(np.concatenate([xp, xs], 0).T)
    W = np.asarray(inp["w_in"], f32)[0]
    nm = np.ascontiguousarray(np.asarray(inp["norm_mix"], f32)[0].reshape(KC, 128).T)
    conv_w = np.asarray(inp["conv_w"], f32)[0]
    state_conv = np.asarray(inp["state_conv"], f32)[0]
    state_ssm = np.asarray(inp["state_ssm"], f32)[0]
    a_log = np.asarray(inp["a_log"], f32)[0]
    dt_bias = np.asarray(inp["dt_bias"], f32)[0]
    gnorm = np.ascontiguousarray(np.asarray(inp["gdn_norm"], f32).reshape(1, 128))
    cst = np.zeros((128, 3, 128), f32)
    cst[:, 0, :] = np.eye(128, dtype=f32)
    pp = np.arange(128)[:, None]
    jj = np.arange(128)[None, :]
    cst[:, 1, :] = (jj < pp)
    cst[:, 2, :] = (jj >= pp)
    rel_bias = np.asarray(inp["rel_bias"], f32)
    ck = np.asarray(inp["cache_k"], f32)[0].reshape(-1, 512)
    cv = np.asarray(inp["cache_v"], f32)[0].reshape(-1, 512)
    cki = np.asarray(inp["cache_idx_k"], f32)[0].reshape(-1, 128)
    page_table = np.asarray(inp["page_table"], np.int32)
    pall = np.concatenate([np.asarray(inp["p_prompt"], f32)[0, 0], np.asarray(inp["p_sample"], f32)[0].reshape(NS, 256)], 0)
    norms = np.stack([np.asarray(inp[k], f32).reshape(D) for k in ("norm_mix", "norm_ffn", "norm_ple", "norm_final")], 0)
    norms = np.ascontiguousarray(norms.reshape(4, KC, 128).transpose(2, 0, 1))
    tailw = {k: np.asarray(inp[k], f32)[0] for k in ("w_attn_up", "w_gdn_up", "w_out", "w_gate_up", "w_down", "w_ple_gate", "w_ple")}
    xT_owns = []
    W1 = np.ascontiguousarray(W[:, :7328])
    W2 = np.ascontiguousarray(W[:, O_GZ:])
    in_maps = []
    for c in range(NCORES):
        cols = [W[:, O_K:O_K + 512], W[:, O_V:O_V + 512], W[:, O_KI:O_KI + 128]]
        for part in range(3):
            for h in (2 * c, 2 * c + 1):
                o = O_QKV + part * 2048 + h * 128
                cols.append(W[:, o:o + 128])
        gab = np.zeros((D, 128), f32)
        gab[:, 0] = W[:, O_GA + 2 * c]
        gab[:, 32] = W[:, O_GA + 2 * c + 1]
        gab[:, 64] = W[:, O_GB + 2 * c]
        gab[:, 96] = W[:, O_GB + 2 * c + 1]
        cols.append(gab)
        w_seq = np.ascontiguousarray(np.concatenate(cols, 1))
        chs = np.concatenate([np.arange(part * 2048 + h * 128, part * 2048 + (h + 1) * 128) for part in range(3) for h in (2 * c, 2 * c + 1)])
        convw = np.ascontiguousarray(conv_w[:, chs].T.reshape(6, 128, 4).transpose(1, 0, 2))
        sconv = np.ascontiguousarray(state_conv[:, :, chs].transpose(2, 0, 1).reshape(768, 128 * 3))
        galog = np.zeros((128, 2), f32)
        galog[0] = [a_log[2 * c], dt_bias[2 * c]]
        galog[32] = [a_log[2 * c + 1], dt_bias[2 * c + 1]]
        sstate = np.ascontiguousarray(state_ssm[:, 2 * c:2 * c + 2])
        m = {"xT": xT, "w_seq": w_seq, "norm_mix": nm, "convw": convw, "sconv": sconv, "galog": galog,
             "gnorm": gnorm, "sstate": sstate}
        m.update(host_consts(c))
        own = _own_idx(c)
        xT_own = np.ascontiguousarray(xT[:, own])
        xT_owns.append(xT_own)
        m.update({"xT_own": xT_own, "w_in": W1, "rel_bias": rel_bias, "cache_k": ck, "cache_v": cv, "cache_ki": cki,
                  "ptab": np.ascontiguousarray(page_table[16 * c:16 * (c + 1)].reshape(1, -1))})
        if FUSED:
            m.update({"pT": np.ascontiguousarray(pall[own].T), "w_in2": W2, "norms": norms})
            m.update(tailw)
        else:
            m.pop("sel")
        in_maps.append(m)
    if _BUILT is None:
        _BUILT = (build(), None if FUSED else build2())
    cx, cx2 = _BUILT
    res = run_bass_kernel_spmd(cx.nc, in_maps, core_ids=list(range(NCORES)))
    R = res.results
    del in_maps
    if FUSED:
        R2 = R
    else:
        in2 = []
        for c in range(NCORES):
            o_all = np.ascontiguousarray(np.concatenate([R[i]["oT_sh"][c] for i in range(NCORES)], 0))
            m2 = {"xT_own": xT_owns[c], "attnT_in": R[c]["attnT"], "oT_all": o_all,
                  "pT": np.ascontiguousarray(pall[_own_idx(c)].T), "w_in2": W2, "norms": norms}
            m2.update(tailw)
            in2.append(m2)
        R2 = run_bass_kernel_spmd(cx2.nc, in2, core_ids=list(range(NCORES))).results
    yall = np.zeros((NALL, D), f32)
    for c in range(NCORES):
        yall[_own_idx(c)] = R2[c]["yT"].T
    kvk = R[0]["kvkT"]
    kT, vT, kiT = kvk[:512], kvk[512:1024], kvk[1024:1152]
    k_prompt = np.ascontiguousarray(kT[:, :NP].T).reshape(1, 1, NP, 4, 128)
    v_prompt = np.ascontiguousarray(vT[:, :NP].T).reshape(1, 1, NP, 4, 128)
    ki_prompt = np.ascontiguousarray(kiT[:, :NP].T).reshape(1, 1, NP, 128)
    k_sample = np.ascontiguousarray(kT[:, NP:].T).reshape(1, 128, 4, 4, 128)
    v_sample = np.ascontiguousarray(vT[:, NP:].T).reshape(1, 128, 4, 4, 128)
    ki_sample = np.ascontiguousarray(kiT[:, NP:].T).reshape(1, 128, 4, 128)
    conv_p = np.zeros((1, 1, 3, 6144), f32)
    conv_s = np.zeros((1, 128, 3, 6144), f32)
    for c in range(NCORES):
        q = R[c]["qkvr"]
        for part in range(3):
            for hh in range(2):
                h = 2 * c + hh
                rows = q[(part * 2 + hh) * 128:(part * 2 + hh + 1) * 128]
                ch0 = part * 2048 + h * 128
                conv_p[0, 0, :, ch0:ch0 + 128] = rows[:, NP - 3:NP].T
                conv_s[0, :, :, ch0:ch0 + 128] = rows[:, NP:].T.reshape(128, 4, 128)[:, 1:4, :]
    y_prompt = np.ascontiguousarray(yall[:NP]).reshape(1, NP, D)
    y_sample = np.ascontiguousarray(yall[NP:]).reshape(128, 4, D)
    ssm_p = np.zeros((1, 1, 16, 128, 128), f32)
    ssm_s = np.zeros((1, 128, 16, 128, 128), f32)
    for c in range(NCORES):
        ssm_p[0, 0, 2 * c:2 * c + 2] = R[c]["ssm_p"]
        ssm_s[0, :, 2 * c:2 * c + 2] = R[c]["ssm_s"]
    return (y_prompt, y_sample, k_prompt, v_prompt, ki_prompt, conv_p, ssm_p, k_sample, v_sample, ki_sample, conv_s, ssm_s)
```
